# Optimizing a Trainium2 kernel written in Bass

```python
import math, functools
import jax, jax.numpy as jnp
from jax import lax
import numpy as np

D_MODEL = 2048
BATCH = 8
SEQ = 2048
DEPTH = 2
DEC_BATCH = 1
DEC_SEQ = 8192
PAST_LEN = 128

N_MIXERS = 2
N_A_LAYERS = (DEPTH + 1) // 2
N_B_LAYERS = DEPTH // 2
GDN_DK = 128
GDN_DV = 128
GDN_HK = D_MODEL // 128
GDN_HV = 2 * GDN_HK
GDN_KD = GDN_HK * GDN_DK
GDN_VD = GDN_HV * GDN_DV
GDN_CONV = 5
GDN_CHUNK = 64
GDN_PROJ = 2 * GDN_KD + 2 * GDN_VD + 4 * GDN_HV
RWKV_N = 64
RWKV_H = D_MODEL // RWKV_N
RWKV_DECAY_LORA = max(32, int(round(1.8 * D_MODEL ** 0.5 / 32)) * 32)
RWKV_A_LORA = max(32, int(round(1.8 * D_MODEL ** 0.5 / 32)) * 32)
RWKV_GATE_LORA = max(32, int(round(0.6 * D_MODEL ** 0.8 / 32)) * 32)
RWKV_GN_EPS = 64e-5
D_FF = 4 * D_MODEL
PLE_DIM = 256
DN_ALPHA = (2 * DEPTH) ** 0.25
DN_BETA = (8 * DEPTH) ** -0.25
LN_EPS = 1e-5

kernel_name = "hybrid_gdn_rwkv7_deepnorm_encoder"


def _layernorm(x, g, b):
    xf = x.astype(jnp.float32)
    mu = jnp.mean(xf, -1, keepdims=True)
    var = jnp.mean(jnp.square(xf - mu), -1, keepdims=True)
    return ((xf - mu) * lax.rsqrt(var + LN_EPS) * g.astype(jnp.float32) + b.astype(jnp.float32)).astype(x.dtype)


def _l2norm(t):
    t = t.astype(jnp.float32)
    return t * lax.rsqrt(jnp.sum(t * t, -1, keepdims=True) + 1e-6)


def _centred_dwconv(x, w):
    width, ch = w.shape
    pad = (width - 1) // 2
    return lax.conv_general_dilated(x, w[:, None, :], window_strides=(1,), padding=[(pad, pad)],
                                    dimension_numbers=("NWC", "WIO", "NWC"), feature_group_count=ch)


def _gated_delta_rule_chunked(q, k, v, g, beta):
    B, T, H, DK = q.shape
    DV = v.shape[-1]
    C = GDN_CHUNK
    NC = T // C

    def blocks(t):
        t = t.astype(jnp.float32).reshape(B, NC, C, H, *t.shape[3:])
        return jnp.moveaxis(jnp.moveaxis(t, 3, 2), 1, 0)

    q, k, v, g, beta = (blocks(t) for t in (q, k, v, g, beta))
    gc = jnp.cumsum(g, axis=-1)
    idx = jnp.arange(C)
    incl = idx[:, None] >= idx[None, :]
    strict = idx[:, None] > idx[None, :]
    diff = gc[..., :, None] - gc[..., None, :]
    decay = jnp.where(incl, jnp.exp(jnp.where(incl, diff, 0.0)), 0.0)
    kb = k * beta[..., None]
    a_kk = jnp.where(strict, jnp.einsum("nbhcd,nbhsd->nbhcs", kb, k) * decay, 0.0)
    eye = jnp.eye(C, dtype=jnp.float32)
    t_inv = lax.linalg.triangular_solve(a_kk + eye, jnp.broadcast_to(eye, a_kk.shape),
                                        left_side=True, lower=True, unit_diagonal=True)
    u = jnp.einsum("nbhcs,nbhse->nbhce", t_inv, v * beta[..., None])
    w = jnp.einsum("nbhcs,nbhsd->nbhcd", t_inv, kb * jnp.exp(gc)[..., None])
    a_qk = jnp.einsum("nbhcd,nbhsd->nbhcs", q, k) * decay
    q_dec = q * jnp.exp(gc)[..., None]
    g_last = gc[..., -1]
    k_dec = k * jnp.exp(g_last[..., None] - gc)[..., None]

    def step(S, xs):
        q_n, w_n, u_n, aqk_n, k_n, gl_n = xs
        v_new = u_n - jnp.einsum("bhcd,bhde->bhce", w_n, S)
        o = jnp.einsum("bhcd,bhde->bhce", q_n, S) + jnp.einsum("bhcs,bhse->bhce", aqk_n, v_new)
        S = S * jnp.exp(gl_n)[..., None, None] + jnp.einsum("bhcd,bhce->bhde", k_n, v_new)
        return S, o

    S0 = jnp.zeros((B, H, DK, DV), jnp.float32)
    _, o = lax.scan(step, S0, (q_dec, w, u, a_qk, k_dec, g_last))
    o = jnp.moveaxis(o, 0, 1)
    return jnp.swapaxes(o, 2, 3).reshape(B, T, H, DV)


def _gdn_mixer(x, w_in, conv_w, a_log, dt_bias, norm_w, w_out):
    B, T, _ = x.shape
    proj = x @ w_in
    qkv, z, gates = jnp.split(proj, [2 * GDN_KD + GDN_VD, 2 * GDN_KD + 2 * GDN_VD], axis=-1)
    qkv = jax.nn.silu(_centred_dwconv(qkv, conv_w))
    q, k, v = jnp.split(qkv, [GDN_KD, 2 * GDN_KD], axis=-1)
    rep = GDN_HV // GDN_HK
    q = jnp.repeat(_l2norm(q.reshape(B, T, GDN_HK, GDN_DK)) * GDN_DK ** -0.5, rep, axis=2)
    k = jnp.repeat(_l2norm(k.reshape(B, T, GDN_HK, GDN_DK)), rep, axis=2)
    v = v.reshape(B, T, GDN_HV, GDN_DV)
    a_f, a_b, b_f, b_b = jnp.split(gates.astype(jnp.float32), 4, axis=-1)

    def log_decay(a, d):
        return -jnp.exp(a_log[d].astype(jnp.float32)) * jax.nn.softplus(a + dt_bias[d].astype(jnp.float32))

    rev = lambda t: t[:, ::-1]
    o_f = _gated_delta_rule_chunked(q, k, v, log_decay(a_f, 0), jax.nn.sigmoid(b_f))
    o_b = rev(_gated_delta_rule_chunked(rev(q), rev(k), rev(v), rev(log_decay(a_b, 1)), rev(jax.nn.sigmoid(b_b))))
    o = o_f + o_b
    o = o * lax.rsqrt(jnp.mean(o * o, -1, keepdims=True) + 1e-6) * norm_w.astype(jnp.float32)
    o = o * jax.nn.silu(z.astype(jnp.float32).reshape(B, T, GDN_HV, GDN_DV))
    return o.reshape(B, T, GDN_VD).astype(x.dtype) @ w_out


def _rwkv7_scan(r, log_w, k, v, a, b):
    B, T, H, N = r.shape
    xs = tuple(jnp.moveaxis(t.astype(jnp.float32), 1, 0) for t in (r, jnp.exp(log_w.astype(jnp.float32)), k, v, a, b))

    def step(S, xs_t):
        r_t, w_t, k_t, v_t, a_t, b_t = xs_t
        sa = jnp.einsum("bhij,bhj->bhi", S, a_t)
        S = S * w_t[:, :, None, :] + sa[..., None] * b_t[:, :, None, :] + v_t[..., None] * k_t[:, :, None, :]
        return S, jnp.einsum("bhij,bhj->bhi", S, r_t)

    _, y = lax.scan(step, jnp.zeros((B, H, N, N), jnp.float32), xs)
    return jnp.moveaxis(y, 0, 1)


def _rwkv7_direction(r, k, v, kk, xw, xa, w0, w1, w2, a0, a1, a2, k_a, r_k, reverse):
    B, T, H, N = r.shape
    heads = lambda t: t.reshape(B, T, H, N)
    log_w = heads(-jnp.exp(-jax.nn.softplus(-(w0 + jnp.tanh(xw @ w1) @ w2)) - 0.5))
    a = jax.nn.sigmoid(a0 + (xa @ a1) @ a2)
    k_d = heads(k * (1 + (a - 1) * k_a))
    a = heads(a)
    rev = (lambda t: t[:, ::-1]) if reverse else (lambda t: t)
    y = rev(_rwkv7_scan(rev(r), rev(log_w), rev(k_d), rev(v), rev(-kk), rev(kk * a)))
    bonus = jnp.sum(r * k_d * r_k, axis=-1, keepdims=True) * v
    return y, bonus


def _rwkv7_mixer(x, mix, w_rkv, w0, w1, w2, a0, a1, a2, g1, g2, k_k, k_a, r_k, ln_w, ln_b, w_o):
    B, T, D = x.shape
    heads = lambda t: t.reshape(B, T, RWKV_H, RWKV_N)
    x_prev = jnp.pad(x[:, :-1], ((0, 0), (1, 0), (0, 0)))
    x_next = jnp.pad(x[:, 1:], ((0, 0), (0, 1), (0, 0)))
    xx = 0.5 * (x_prev + x_next) - x
    xr, xw, xk, xv, xa, xg = (x + xx * mix[m] for m in range(6))
    r = heads(xr @ w_rkv[0])
    k = xk @ w_rkv[1]
    v = heads(xv @ w_rkv[2])
    g = jax.nn.sigmoid(xg @ g1) @ g2
    kk = _l2norm(heads(k * k_k))
    y_f, bonus_f = _rwkv7_direction(r, k, v, kk, xw, xa, w0[0], w1[0], w2[0], a0[0], a1[0], a2[0], k_a, r_k, reverse=False)
    y_b, bonus_b = _rwkv7_direction(r, k, v, kk, xw, xa, w0[1], w1[1], w2[1], a0[1], a1[1], a2[1], k_a, r_k, reverse=True)
    wkv = y_f + y_b
    mu = jnp.mean(wkv, -1, keepdims=True)
    var = jnp.mean(jnp.square(wkv - mu), -1, keepdims=True)
    o = ((wkv - mu) * lax.rsqrt(var + RWKV_GN_EPS)).reshape(B, T, D) * ln_w.astype(jnp.float32) + ln_b.astype(jnp.float32)
    o = o + (bonus_f + bonus_b).reshape(B, T, D).astype(jnp.float32)
    return (o.astype(x.dtype) * g) @ w_o


def _sqrelu_mlp(x, w_up, w_down):
    return jnp.square(jax.nn.relu(x @ w_up)) @ w_down


def _trunk(x, p, gdn_w_in, gdn_conv, gdn_a_log, gdn_dt_bias, gdn_norm, gdn_w_out,
           rwkv_mix, rwkv_w_rkv, rwkv_w0, rwkv_w1, rwkv_w2, rwkv_a0, rwkv_a1, rwkv_a2,
           rwkv_g1, rwkv_g2, rwkv_k_k, rwkv_k_a, rwkv_r_k, rwkv_ln_w, rwkv_ln_b, rwkv_w_o,
           ln_g, ln_b, mlp_w_up, mlp_w_down, ple_w_proj, ple_w_gate):
    for i in range(DEPTH):
        j = i // N_MIXERS
        if i % N_MIXERS == 0:
            h = _gdn_mixer(x, gdn_w_in[j], gdn_conv[j], gdn_a_log[j], gdn_dt_bias[j], gdn_norm[j], gdn_w_out[j])
        else:
            h = _rwkv7_mixer(x, rwkv_mix[j], rwkv_w_rkv[j], rwkv_w0[j], rwkv_w1[j], rwkv_w2[j],
                             rwkv_a0[j], rwkv_a1[j], rwkv_a2[j], rwkv_g1[j], rwkv_g2[j],
                             rwkv_k_k[j], rwkv_k_a[j], rwkv_r_k[j], rwkv_ln_w[j], rwkv_ln_b[j], rwkv_w_o[j])
        x = _layernorm(DN_ALPHA * x + h, ln_g[i, 0], ln_b[i, 0])
        x = _layernorm(DN_ALPHA * x + _sqrelu_mlp(x, mlp_w_up[i], mlp_w_down[i]), ln_g[i, 1], ln_b[i, 1])
        x = x + jax.nn.sigmoid(x @ ple_w_gate[i]) * (p[i] @ ple_w_proj[i])
    return x


def setup_inputs(seed: int = 0) -> dict:
    key = jax.random.key(seed)
    ks = iter(jax.random.split(key, 48))
    nrm = lambda shape, scale=1.0: scale * jax.random.normal(next(ks), shape, jnp.float32)
    uni = lambda shape, lo, hi: jax.random.uniform(next(ks), shape, jnp.float32, lo, hi)
    dt = jnp.exp(uni((N_A_LAYERS, 2, GDN_HV), math.log(1e-3), math.log(1e-1)))
    return {
        "x_prompt": nrm((BATCH, SEQ, D_MODEL)),
        "x_sample": nrm((DEC_BATCH, DEC_SEQ, D_MODEL)),
        "p_prompt": nrm((DEPTH, BATCH, SEQ, PLE_DIM)),
        "p_sample": nrm((DEPTH, DEC_BATCH, DEC_SEQ, PLE_DIM)),
        "gdn_w_in": nrm((N_A_LAYERS, D_MODEL, GDN_PROJ), D_MODEL ** -0.5),
        "gdn_conv": nrm((N_A_LAYERS, GDN_CONV, 2 * GDN_KD + GDN_VD), GDN_CONV ** -0.5),
        "gdn_a_log": jnp.log(uni((N_A_LAYERS, 2, GDN_HV), 1.0, 16.0)),
        "gdn_dt_bias": dt + jnp.log(-jnp.expm1(-dt)),
        "gdn_norm": 1.0 + nrm((N_A_LAYERS, GDN_DV), 0.1),
        "gdn_w_out": nrm((N_A_LAYERS, GDN_VD, D_MODEL), DN_BETA * GDN_VD ** -0.5),
        "rwkv_mix": uni((N_B_LAYERS, 6, D_MODEL), 0.0, 1.0),
        "rwkv_w_rkv": nrm((N_B_LAYERS, 3, D_MODEL, D_MODEL), D_MODEL ** -0.5),
        "rwkv_w0": uni((N_B_LAYERS, 2, D_MODEL), -6.0, -1.0),
        "rwkv_w1": nrm((N_B_LAYERS, 2, D_MODEL, RWKV_DECAY_LORA), D_MODEL ** -0.5),
        "rwkv_w2": nrm((N_B_LAYERS, 2, RWKV_DECAY_LORA, D_MODEL), 0.5 * RWKV_DECAY_LORA ** -0.5),
        "rwkv_a0": nrm((N_B_LAYERS, 2, D_MODEL), 0.1),
        "rwkv_a1": nrm((N_B_LAYERS, 2, D_MODEL, RWKV_A_LORA), D_MODEL ** -0.5),
        "rwkv_a2": nrm((N_B_LAYERS, 2, RWKV_A_LORA, D_MODEL), 0.5 * RWKV_A_LORA ** -0.5),
        "rwkv_g1": nrm((N_B_LAYERS, D_MODEL, RWKV_GATE_LORA), D_MODEL ** -0.5),
        "rwkv_g2": nrm((N_B_LAYERS, RWKV_GATE_LORA, D_MODEL), RWKV_GATE_LORA ** -0.5),
        "rwkv_k_k": 0.85 + nrm((N_B_LAYERS, D_MODEL), 0.05),
        "rwkv_k_a": 1.0 + nrm((N_B_LAYERS, D_MODEL), 0.05),
        "rwkv_r_k": nrm((N_B_LAYERS, RWKV_H, RWKV_N), 0.1),
        "rwkv_ln_w": 1.0 + nrm((N_B_LAYERS, D_MODEL), 0.1),
        "rwkv_ln_b": nrm((N_B_LAYERS, D_MODEL), 0.01),
        "rwkv_w_o": nrm((N_B_LAYERS, D_MODEL, D_MODEL), DN_BETA * D_MODEL ** -0.5),
        "ln_g": 1.0 + nrm((DEPTH, 2, D_MODEL), 0.1),
        "ln_b": nrm((DEPTH, 2, D_MODEL), 0.01),
        "mlp_w_up": nrm((DEPTH, D_MODEL, D_FF), D_MODEL ** -0.5),
        "mlp_w_down": nrm((DEPTH, D_FF, D_MODEL), DN_BETA * D_FF ** -0.5),
        "ple_w_proj": nrm((DEPTH, PLE_DIM, D_MODEL), PLE_DIM ** -0.5),
        "ple_w_gate": nrm((DEPTH, D_MODEL, D_MODEL), D_MODEL ** -0.5),
    }


def reference(x_prompt, x_sample, p_prompt, p_sample, gdn_w_in, gdn_conv, gdn_a_log, gdn_dt_bias, gdn_norm,
              gdn_w_out, rwkv_mix, rwkv_w_rkv, rwkv_w0, rwkv_w1, rwkv_w2, rwkv_a0, rwkv_a1, rwkv_a2,
              rwkv_g1, rwkv_g2, rwkv_k_k, rwkv_k_a, rwkv_r_k, rwkv_ln_w, rwkv_ln_b, rwkv_w_o,
              ln_g, ln_b, mlp_w_up, mlp_w_down, ple_w_proj, ple_w_gate):
    trunk = functools.partial(
        _trunk, gdn_w_in=gdn_w_in, gdn_conv=gdn_conv, gdn_a_log=gdn_a_log, gdn_dt_bias=gdn_dt_bias,
        gdn_norm=gdn_norm, gdn_w_out=gdn_w_out, rwkv_mix=rwkv_mix, rwkv_w_rkv=rwkv_w_rkv,
        rwkv_w0=rwkv_w0, rwkv_w1=rwkv_w1, rwkv_w2=rwkv_w2, rwkv_a0=rwkv_a0, rwkv_a1=rwkv_a1,
        rwkv_a2=rwkv_a2, rwkv_g1=rwkv_g1, rwkv_g2=rwkv_g2, rwkv_k_k=rwkv_k_k, rwkv_k_a=rwkv_k_a,
        rwkv_r_k=rwkv_r_k, rwkv_ln_w=rwkv_ln_w, rwkv_ln_b=rwkv_ln_b, rwkv_w_o=rwkv_w_o,
        ln_g=ln_g, ln_b=ln_b, mlp_w_up=mlp_w_up, mlp_w_down=mlp_w_down,
        ple_w_proj=ple_w_proj, ple_w_gate=ple_w_gate)
    y_prompt = trunk(x_prompt, p_prompt)
    y_sample = trunk(x_sample, p_sample)
    return (y_prompt, y_sample)
```

```python
import numpy as np
from contextlib import ExitStack
import concourse.bass as bass
import concourse.mybir as mybir
from concourse.bass_utils import run_bass_kernel_spmd

F32 = mybir.dt.float32
BF16 = mybir.dt.bfloat16
AF = mybir.ActivationFunctionType
ALU = mybir.AluOpType

SEM_EPOCH = 30000
SAME_ENGINE_SYNC = False
STAGE = 99


class _Stop(Exception):
    pass


DEAD = [False]


def stage(k):
    if STAGE == k:
        DEAD[0] = True


class Buf:
    __slots__ = ("t", "lw", "rd", "name")

    def __init__(self, t=None, name=""):
        self.t = t
        self.lw = None
        self.rd = []
        self.name = name

    def __getitem__(self, idx):
        return self.t[idx]


class Sched:
    def __init__(self, nc, es, n_dma_sems=6):
        self.nc = nc
        self.es = es
        self.engs = {"pe": nc.tensor, "act": nc.scalar, "dve": nc.vector, "pool": nc.gpsimd, "sp": nc.sync}
        self.sems = {}
        self.cur = {}
        self.epoch = {}
        self.allsems = {e: [] for e in self.engs}
        for e in self.engs:
            self.epoch[e] = 0
            self._new_sem(e)
        self.waited = {}
        self.dq = {}
        for q in ("sp", "pool", "act"):
            lst = []
            for i in range(n_dma_sems):
                k = f"d_{q}_{i}"
                self.sems[k] = es.enter_context(nc.semaphore(k))
                lst.append([k, 0])
            self.dq[q] = [lst, 0]
        self.n_inst = 0

    def _new_sem(self, e):
        k = f"c_{e}_{self.epoch[e]}"
        self.sems[k] = self.es.enter_context(self.nc.semaphore(k))
        self.cur[e] = [k, 0]
        self.allsems[e].append(self.cur[e])
        self.epoch[e] += 1

    def _wait(self, e, tok):
        if tok is None:
            return
        k, v = tok
        if self.waited.get((e, k), 0) >= v:
            return
        self.engs[e].wait_ge(self.sems[k], v)
        self.waited[(e, k)] = v
        self.n_inst += 1

    def _toks(self, r, w):
        toks = []
        for b in r:
            if b.lw is not None:
                toks.append(b.lw)
        for b in w:
            if b.lw is not None:
                toks.append(b.lw)
            toks.extend(b.rd)
        return toks

    def _commit(self, tok, r, w):
        for b in w:
            b.lw = tok
            b.rd = []
        for b in r:
            if b in w:
                continue
            b.rd.append(tok)
            if len(b.rd) > 10:
                d = {}
                for k, v in b.rd:
                    d[k] = max(d.get(k, 0), v)
                b.rd = list(d.items())

    def op(self, e, fn, r=(), w=()):
        if DEAD[0]:
            return None
        pre = f"c_{e}_"
        for tok in self._toks(r, w):
            if (not SAME_ENGINE_SYNC or e == "pe") and tok[0].startswith(pre):
                continue
            self._wait(e, tok)
        if self.cur[e][1] >= SEM_EPOCH:
            self._new_sem(e)
        ins = fn()
        c = self.cur[e]
        c[1] += 1
        ins.then_inc(self.sems[c[0]], 1)
        tok = (c[0], c[1])
        self._commit(tok, r, w)
        self.n_inst += 1
        return tok

    def dma(self, q, out, in_, r=(), w=(), **kw):
        if DEAD[0]:
            return None
        lst, idx = self.dq[q]
        slot = lst[idx % len(lst)]
        self.dq[q][1] = idx + 1
        if slot[1] > 0:
            self._wait(q, (slot[0], slot[1]))
        for tok in self._toks(r, w):
            self._wait(q, tok)
        ins = self.engs[q].dma_start(out=out, in_=in_, **kw)
        slot[1] += 16
        ins.then_inc(self.sems[slot[0]], 16)
        tok = (slot[0], slot[1])
        self._commit(tok, r, w)
        self.n_inst += 1
        return tok

    def barrier(self):
        toks = []
        for e in self.engs:
            c = self.cur[e]
            if c[1] > 0:
                toks.append((c[0], c[1]))
        for q in ("sp", "pool", "act"):
            for k, v in self.dq[q][0]:
                if v > 0:
                    toks.append((k, v))
        for e in self.engs:
            for tok in toks:
                if tok[0].startswith(f"c_{e}_"):
                    continue
                self._wait(e, tok)

    def finish(self):
        self.barrier()


D = 2048
KC = 16
ALPHA = 4 ** 0.25
LN_EPS = 1e-5


def make_consts():
    i = np.arange(128)
    s = i[:, None]
    t = i[None, :]
    c = {}
    c["ident"] = np.eye(128, dtype=np.float32)
    c["triF"] = (s <= t).astype(np.float32)
    c["triB"] = (s >= t).astype(np.float32)
    c["triFs"] = (s < t).astype(np.float32)
    c["triBs"] = (s > t).astype(np.float32)
    c["ones"] = np.ones((128, 128), np.float32)
    blk = (s // 64 == t // 64).astype(np.float32)
    c["blk64"] = blk
    names = ["ident", "triF", "triB", "triFs", "triBs", "ones", "blk64"]
    arr = np.stack([c[n] for n in names], 0)
    return names, np.ascontiguousarray(arr.transpose(1, 0, 2))


CONST_NAMES, CONST_ARR = make_consts()

W_SPECS = [
    ("gdn_w_in", [D, 12416]), ("gdn_conv", [5, 8192]), ("gdn_a_log", [1, 64]), ("gdn_dt_bias", [1, 64]),
    ("gdn_norm", [128, 1]), ("gdn_w_out", [4096, D]),
    ("rwkv_mix", [6, D]), ("rwkv_w_rkv", [3, D, D]), ("rwkv_w0", [2, D]), ("rwkv_w1", [2, D, 96]),
    ("rwkv_w2", [2, 96, D]), ("rwkv_a0", [2, D]), ("rwkv_a1", [2, D, 96]), ("rwkv_a2", [2, 96, D]),
    ("rwkv_g1", [D, 256]), ("rwkv_g2", [256, D]), ("rwkv_k_k", [1, D]), ("rwkv_k_a", [1, D]),
    ("rwkv_r_k", [1, D]), ("rwkv_ln_w", [1, D]), ("rwkv_ln_b", [1, D]), ("rwkv_w_o", [D, D]),
    ("ln_g", [4, D]), ("ln_b", [4, D]), ("mlp_w_up", [2, D, 8192]), ("mlp_w_down", [2, 8192, D]),
    ("ple_w_proj", [2, 256, D]), ("ple_w_gate", [2, D, D]),
]


def build(NSEG, L, n_layers=2):
    DEAD[0] = False
    NTOK = NSEG * L
    NT = NTOK // 512
    NBS = L // 128
    TPS = L // 512
    nc = bass.Bass("TRN2", target_bir_lowering=False)
    I = {}

    def inp(name, shape):
        I[name] = nc.dram_tensor(name, shape, F32, kind="ExternalInput").ap()
        return I[name]

    x_in = inp("x", [NTOK, D])
    p_in = inp("p", [2, NTOK, 256])
    flag_in = inp("flag", [128, 1])
    consts_in = inp("consts", [128, len(CONST_NAMES), 128])
    for n, sh in W_SPECS:
        inp(n, sh)
    y_out = nc.dram_tensor("y", [NTOK, D], F32, kind="ExternalOutput").ap()

    def scr(name, shape, dt):
        return nc.dram_tensor(name, shape, dt).ap()

    xT = [scr("xT0", [D, NTOK], F32), scr("xT1", [D, NTOK], F32)]
    raw = scr("raw", [8192, NTOK], F32)
    qT_d = scr("qT", [2048, NTOK], BF16)
    kT_d = scr("kT", [2048, NTOK], BF16)
    ktok_d = scr("ktok", [NTOK, 2048], BF16)
    vtok_d = scr("vtok", [NTOK, 4096], BF16)
    zs_d = scr("zs", [4096, NTOK], BF16)
    GC_d = scr("GC", [NTOK, 64], F32)
    GX_d = scr("GX", [NTOK, 64], F32)
    BT_d = scr("BT", [NTOK, 64], F32)
    gcT_d = scr("gcT", [64, NTOK], F32)
    oT_d = scr("oT", [4096, NTOK], F32)
    ogT_d = scr("ogT", [4096, NTOK], BF16)

    dbs = {}

    def db(name, i):
        k = (name, i)
        if k not in dbs:
            dbs[k] = Buf(None, f"{name}{i}")
        return dbs[k]

    def dbt(name, t0, t1):
        t0 = max(t0, 0)
        t1 = min(t1, NTOK)
        return [db(name, i) for i in range(t0 // 512, (t1 - 1) // 512 + 1)]

    with ExitStack() as es:
        S = Sched(nc, es)

        uid = [0]

        def sbuf(st, name, shape, dt):
            uid[0] += 1
            name = f"{name}_{uid[0]}"
            return Buf(st.enter_context(nc.sbuf_tensor(name, shape, dt)), name)

        psb = [Buf(es.enter_context(nc.psum_tensor(f"ps{i}", [128, 512], F32)), f"ps{i}") for i in range(6)]
        psh = [Buf(es.enter_context(nc.psum_tensor(f"ph{i}", [128, 512], BF16)), f"ph{i}") for i in range(2)]
        psi = [0, 0]

        def nps():
            psi[0] += 1
            return psb[psi[0] % 6]

        def nph():
            psi[1] += 1
            return psh[psi[1] % 2]

        cst = sbuf(es, "cst", [128, len(CONST_NAMES), 128], F32)
        S.dma("sp", cst[:], consts_in[:, :, :], w=[cst])
        cstb = sbuf(es, "cstb", [128, len(CONST_NAMES), 128], BF16)
        S.op("dve", lambda: nc.vector.tensor_copy(cstb[:], cst[:]), r=[cst], w=[cstb])
        CI = {n: i for i, n in enumerate(CONST_NAMES)}

        def C(n):
            return cst[:, CI[n], :]

        def Cb(n):
            return cstb[:, CI[n], :]

        flag = sbuf(es, "flag_sb", [128, 1], F32)
        S.dma("sp", flag[:], flag_in[:, :], w=[flag])
        def colvec(name, ap_row, nch):
            b = sbuf(es, name, [128, nch], F32)
            with nc.allow_non_contiguous_dma("small param vector"):
                S.dma("sp", b[:], ap_row.rearrange("(c p) -> p c", p=128), w=[b])
            return b

        lng = [colvec(f"lng{i}", I["ln_g"][i, :], 16) for i in range(4)]
        lnb = [colvec(f"lnb{i}", I["ln_b"][i, :], 16) for i in range(4)]

        wbufs = [sbuf(es, f"wb{i}", [128, 16, 256], BF16) for i in range(3)]
        wbi = [0]

        def linear(w_ap, K, N, rhs, rbufs, evac, ncols=512):
            nkc = (K + 127) // 128
            nkg = (nkc + 15) // 16
            for ng in range((N + 255) // 256):
                n0 = ng * 256
                nw = min(256, N - n0)
                pss = [nps() for _ in range((nw + 127) // 128)]
                for kg in range(nkg):
                    wb = wbufs[wbi[0] % 3]
                    wbi[0] += 1
                    kcs = min(16, nkc - kg * 16)
                    k0 = kg * 16 * 128
                    krows = min(K - k0, kcs * 128)
                    if krows % 128 == 0:
                        S.dma("pool", wb[:, 0:kcs, 0:nw],
                              w_ap[k0:k0 + krows, n0:n0 + nw].rearrange("(c p) n -> p c n", p=128), w=[wb])
                    else:
                        S.dma("pool", wb[0:krows, 0, 0:nw], w_ap[k0:k0 + krows, n0:n0 + nw], w=[wb])
                    for j, ps in enumerate(pss):
                        mw = min(128, nw - j * 128)
                        for kc in range(kcs):
                            kk = kg * 16 + kc
                            kp = min(128, K - kk * 128)
                            S.op("pe", lambda ps=ps, wb=wb, kc=kc, j=j, kk=kk, kp=kp, mw=mw: nc.tensor.matmul(
                                ps[0:mw, 0:ncols], wb[0:kp, kc, j * 128:j * 128 + mw], rhs(kk),
                                start=(kk == 0), stop=(kk == nkc - 1)), r=[wb] + list(rbufs), w=[ps])
                for j, ps in enumerate(pss):
                    evac(ng * 2 + j, ps)

        def layernorm(st_tiles, y, li, xb):
            sq, mean, rstd = st_tiles
            p1 = nps()
            p2 = nps()
            for kc in range(16):
                S.op("pe", lambda kc=kc: nc.tensor.matmul(p1[:, :], C("ones"), y[:, kc, :], start=(kc == 0), stop=(kc == 15)),
                     r=[cst, y], w=[p1])
            for kc in range(16):
                S.op("act", lambda kc=kc: nc.scalar.activation(out=sq[:, kc % 2, :], in_=y[:, kc, :], func=AF.Square), r=[y], w=[sq])
                S.op("pe", lambda kc=kc: nc.tensor.matmul(p2[:, :], C("ones"), sq[:, kc % 2, :], start=(kc == 0), stop=(kc == 15)),
                     r=[cst, sq], w=[p2])
            S.op("act", lambda: nc.scalar.mul(mean[:], p1[:, :], 1.0 / D), r=[p1], w=[mean])
            S.op("dve", lambda: nc.vector.tensor_tensor(out=rstd[:], in0=mean[:], in1=mean[:], op=ALU.mult), r=[mean], w=[rstd])
            S.op("dve", lambda: nc.vector.scalar_tensor_tensor(out=rstd[:], in0=p2[:, :], scalar=1.0 / D, in1=rstd[:],
                                                               op0=ALU.mult, op1=ALU.subtract), r=[p2, rstd], w=[rstd])
            S.op("act", lambda: nc.scalar.activation(out=rstd[:], in_=rstd[:], func=AF.Ln, bias=LN_EPS), r=[rstd], w=[rstd])
            S.op("act", lambda: nc.scalar.activation(out=rstd[:], in_=rstd[:], func=AF.Exp, scale=-0.5), r=[rstd], w=[rstd])
            for kc in range(16):
                S.op("dve", lambda kc=kc: nc.vector.tensor_tensor(out=y[:, kc, :], in0=y[:, kc, :], in1=mean[:], op=ALU.subtract), r=[y, mean], w=[y])
                S.op("pool", lambda kc=kc: nc.gpsimd.tensor_tensor(out=y[:, kc, :], in0=y[:, kc, :], in1=rstd[:], op=ALU.mult), r=[y, rstd], w=[y])
                S.op("act", lambda kc=kc: nc.scalar.activation(out=y[:, kc, :], in_=y[:, kc, :], func=AF.Identity,
                                                               scale=lng[li][:, kc:kc + 1], bias=lnb[li][:, kc:kc + 1]), r=[y, lng[li], lnb[li]], w=[y])
                if xb is not None:
                    S.op("dve", lambda kc=kc: nc.vector.tensor_copy(xb[:, kc, :], y[:, kc, :]), r=[y], w=[xb])

        with ExitStack() as st:
            xin = [sbuf(st, f"xin{i}", [128, D], F32) for i in range(2)]
            xo = [sbuf(st, f"xo{i}", [128, 16, 128], F32) for i in range(2)]
            for b in range(NTOK // 128):
                xi = xin[b % 2]
                o = xo[b % 2]
                S.dma("sp", xi[:], x_in[b * 128:(b + 1) * 128, :], w=[xi])
                for g in range(4):
                    ps = nps()
                    for j in range(4):
                        kc = g * 4 + j
                        S.op("pe", lambda ps=ps, j=j, kc=kc, xi=xi: nc.tensor.transpose(ps[:, j * 128:(j + 1) * 128], xi[:, kc * 128:(kc + 1) * 128], C("ident")),
                             r=[xi, cst], w=[ps])
                    eng = "act" if g % 2 == 0 else "dve"
                    if eng == "act":
                        S.op("act", lambda ps=ps, g=g, o=o: nc.scalar.copy(o[:, g * 4:(g + 1) * 4, :], ps[:, :].rearrange("p (a b) -> p a b", a=4)), r=[ps], w=[o])
                    else:
                        S.op("dve", lambda ps=ps, g=g, o=o: nc.vector.tensor_copy(o[:, g * 4:(g + 1) * 4, :], ps[:, :].rearrange("p (a b) -> p a b", a=4)), r=[ps], w=[o])
                S.dma("sp", xT[0][:, b * 128:(b + 1) * 128].rearrange("(c p) t -> p c t", p=128), o[:], r=[o], w=dbt("xT0", b * 128, b * 128 + 128))
        S.barrier()

        def layer_post(li, mix_src, mix_K, w_mix, src_name, xin_name, xin_ap, last):
            with ExitStack() as st:
                nkc_mix = mix_K // 128
                hbuf = sbuf(st, "hbuf", [128, 64, 512], BF16)
                xr = sbuf(st, "xr", [128, 16, 512], F32)
                xb = sbuf(st, "xb", [128, 16, 512], BF16)
                sq = sbuf(st, "lsq", [128, 2, 512], F32)
                mean = sbuf(st, "lmean", [128, 512], F32)
                rstd = sbuf(st, "lrstd", [128, 512], F32)
                tmp = sbuf(st, "ltmp", [128, 2, 512], F32)
                ptok = sbuf(st, "ptok", [128, 4, 256], F32)
                pT = sbuf(st, "pT", [128, 2, 512], BF16)
                yo = [sbuf(st, f"yo{i}", [128, 4, 512], F32) for i in range(2)] if last else None
                for tt in range(NT):
                    t0 = tt * 512
                    S.dma("sp", hbuf[:, 0:nkc_mix, :], mix_src[:, t0:t0 + 512].rearrange("(c p) t -> p c t", p=128),
                          r=dbt(src_name, t0, t0 + 512), w=[hbuf])
                    S.dma("sp", xr[:], xin_ap[:, t0:t0 + 512].rearrange("(c p) t -> p c t", p=128), r=dbt(xin_name, t0, t0 + 512), w=[xr])

                    def ev1(n, ps):
                        S.op("dve", lambda: nc.vector.scalar_tensor_tensor(out=xr[:, n, :], in0=xr[:, n, :], scalar=ALPHA, in1=ps[:, :],
                                                                           op0=ALU.mult, op1=ALU.add), r=[xr, ps], w=[xr])
                    linear(w_mix, mix_K, D, lambda kc: hbuf[:, kc, :], [hbuf], ev1)
                    layernorm((sq, mean, rstd), xr, 2 * li, xb)

                    def ev2(n, ps):
                        S.op("act", lambda: nc.scalar.activation(out=tmp[:, n % 2, :], in_=ps[:, :], func=AF.Relu), r=[ps], w=[tmp])
                        S.op("pool", lambda: nc.gpsimd.tensor_tensor(out=hbuf[:, n, :], in0=tmp[:, n % 2, :], in1=tmp[:, n % 2, :], op=ALU.mult), r=[tmp], w=[hbuf])
                    linear(I["mlp_w_up"][li], D, 8192, lambda kc: xb[:, kc, :], [xb], ev2)
                    linear(I["mlp_w_down"][li], 8192, D, lambda kc: hbuf[:, kc, :], [hbuf], ev1)
                    layernorm((sq, mean, rstd), xr, 2 * li + 1, xb)
                    S.dma("sp", ptok[:], p_in[li, t0:t0 + 512, :].rearrange("(b p) j -> p b j", p=128), w=[ptok])
                    for jc in range(2):
                        ps = nps()
                        for b in range(4):
                            S.op("pe", lambda ps=ps, b=b, jc=jc: nc.tensor.transpose(ps[:, b * 128:(b + 1) * 128], ptok[:, b, jc * 128:(jc + 1) * 128], C("ident")),
                                 r=[ptok, cst], w=[ps])
                        S.op("act", lambda ps=ps, jc=jc: nc.scalar.copy(pT[:, jc, :], ps[:, :]), r=[ps], w=[pT])
                    sg = hbuf

                    def ev3(n, ps):
                        S.op("act", lambda: nc.scalar.activation(out=sg[:, n, :], in_=ps[:, :], func=AF.Sigmoid), r=[ps], w=[sg])
                    linear(I["ple_w_gate"][li], D, D, lambda kc: xb[:, kc, :], [xb], ev3)

                    def ev4(n, ps):
                        S.op("dve", lambda: nc.vector.tensor_tensor(out=tmp[:, n % 2, :], in0=ps[:, :], in1=sg[:, n, :], op=ALU.mult), r=[ps, sg], w=[tmp])
                        S.op("pool", lambda: nc.gpsimd.tensor_tensor(out=xr[:, n, :], in0=xr[:, n, :], in1=tmp[:, n % 2, :], op=ALU.add), r=[xr, tmp], w=[xr])
                    linear(I["ple_w_proj"][li], 256, D, lambda kc: pT[:, kc, :], [pT], ev4)
                    if not last:
                        S.dma("sp", xT[1][:, t0:t0 + 512].rearrange("(c p) t -> p c t", p=128), xr[:], r=[xr], w=dbt("xT1", t0, t0 + 512))
                    else:
                        for b in range(4):
                            for g in range(4):
                                o = yo[(b * 4 + g) % 2]
                                ps = nps()
                                for j in range(4):
                                    kc = g * 4 + j
                                    S.op("pe", lambda ps=ps, j=j, kc=kc, b=b: nc.tensor.transpose(ps[:, j * 128:(j + 1) * 128], xr[:, kc, b * 128:(b + 1) * 128], C("ident")),
                                         r=[xr, cst], w=[ps])
                                if g % 2 == 0:
                                    S.op("act", lambda ps=ps, o=o, g=g: nc.scalar.copy(o[:, g, :], ps[:, :]), r=[ps], w=[o])
                                else:
                                    S.op("dve", lambda ps=ps, o=o, g=g: nc.vector.tensor_copy(o[:, g, :], ps[:, :]), r=[ps], w=[o])
                                S.dma("sp", y_out[t0 + b * 128:t0 + (b + 1) * 128, g * 512:(g + 1) * 512], o[:, g, :], r=[o])
            S.barrier()

        def gdn_layer(li, xin_name, xin_ap):
            Win = I["gdn_w_in"]
            stage(0)
            with ExitStack() as st:
                xb = sbuf(st, "gxb", [128, 16, 512], BF16)
                wg = sbuf(st, "gwg", [128, 16, 128], BF16)
                S.dma("pool", wg[:], Win[:, 12288:12416].rearrange("(c p) n -> p c n", p=128), w=[wg])
                ev_f = [sbuf(st, f"gevf{i}", [128, 512], F32) for i in range(3)]
                ev_b = [sbuf(st, f"gevb{i}", [128, 512], BF16) for i in range(3)]
                dtb = sbuf(st, "dtb", [128, 4, 64], F32)
                nega = sbuf(st, "nega", [128, 4, 64], F32)
                for b in range(4):
                    S.dma("sp", dtb[:, b, :], I["gdn_dt_bias"][0:1, :].partition_broadcast(128), w=[dtb])
                    S.dma("sp", nega[:, b, :], I["gdn_a_log"][0:1, :].partition_broadcast(128), w=[nega])
                S.op("act", lambda: nc.scalar.activation(out=nega[:], in_=nega[:], func=AF.Exp), r=[nega], w=[nega])
                S.op("dve", lambda: nc.vector.tensor_scalar(out=nega[:], in0=nega[:], scalar1=-1.0, scalar2=None, op0=ALU.mult), r=[nega], w=[nega])
                gg = sbuf(st, "gg", [128, 4, 64], F32)
                ggp = sbuf(st, "ggp", [128, 4, 128], F32)
                S.op("pool", lambda: nc.gpsimd.memset(ggp[:], 0.0), w=[ggp])
                gbt = sbuf(st, "gbt", [128, 4, 64], F32)
                gco = sbuf(st, "gco", [128, 4, 64], F32)
                gxo = sbuf(st, "gxo", [128, 4, 64], F32)
                gto = sbuf(st, "gto", [64, 512], F32)
                cnt = [0]
                stage(10)
                for tt in range(NT):
                    t0 = tt * 512
                    S.dma("pool", xb[:], xin_ap[:, t0:t0 + 512].rearrange("(c p) t -> p c t", p=128), r=dbt(xin_name, t0, t0 + 512), w=[xb])

                    def evq(n, ps):
                        i = cnt[0] % 3
                        cnt[0] += 1
                        if n < 64:
                            e = ev_f[i]
                            S.op("act", lambda: nc.scalar.copy(e[:], ps[:, :]), r=[ps], w=[e])
                            S.dma("sp", raw[n * 128:(n + 1) * 128, t0:t0 + 512], e[:], r=[e], w=dbt("raw", t0, t0 + 512))
                        else:
                            e = ev_b[i]
                            S.op("act", lambda: nc.scalar.activation(out=e[:], in_=ps[:, :], func=AF.Silu), r=[ps], w=[e])
                            S.dma("sp", zs_d[(n - 64) * 128:(n - 63) * 128, t0:t0 + 512], e[:], r=[e], w=dbt("zs", t0, t0 + 512))
                    linear(Win, D, 12288, lambda kc: xb[:, kc, :], [xb], evq)
                    stage(11)
                    ps = nps()
                    for b in range(4):
                        for kc in range(16):
                            S.op("pe", lambda b=b, kc=kc, ps=ps: nc.tensor.matmul(ps[:, b * 128:(b + 1) * 128], xb[:, kc, b * 128:(b + 1) * 128], wg[:, kc, :],
                                                                                start=(kc == 0), stop=(kc == 15)), r=[xb, wg], w=[ps])
                    psv = ps[:, :].rearrange("p (b n) -> p b n", b=4)
                    S.op("dve", lambda: nc.vector.tensor_tensor(out=gg[:], in0=psv[:, :, 0:64], in1=dtb[:], op=ALU.add), r=[ps, dtb], w=[gg])
                    S.op("act", lambda: nc.scalar.activation(out=gbt[:], in_=psv[:, :, 64:128], func=AF.Sigmoid), r=[ps], w=[gbt])
                    S.op("act", lambda: nc.scalar.activation(out=gg[:], in_=gg[:], func=AF.Exp), r=[gg], w=[gg])
                    S.op("act", lambda: nc.scalar.activation(out=gg[:], in_=gg[:], func=AF.Ln, bias=1.0), r=[gg], w=[gg])
                    S.op("dve", lambda: nc.vector.tensor_tensor(out=gg[:], in0=gg[:], in1=nega[:], op=ALU.mult), r=[gg, nega], w=[gg])
                    stage(13)
                    S.op("pool", lambda: nc.gpsimd.tensor_copy(ggp[:, :, 0:64], gg[:]), r=[gg], w=[ggp])
                    for ti, tri in enumerate(("triF", "triB", "triBs", "triFs")):
                        pc = nps()
                        for b in range(4):
                            S.op("pe", lambda b=b, tri=tri, pc=pc: nc.tensor.matmul(pc[:, b * 128:(b + 1) * 128], C(tri), ggp[:, b, :], start=True, stop=True), r=[cst, ggp], w=[pc])
                        pcv = pc[:, :].rearrange("p (b n) -> p b n", b=4)
                        dst = gco if ti < 2 else gxo
                        cs_ = slice(0, 32) if ti in (0, 2) else slice(32, 64)
                        if ti % 2 == 0:
                            S.op("act", lambda pcv=pcv, dst=dst, cs_=cs_: nc.scalar.copy(dst[:, :, cs_], pcv[:, :, cs_]), r=[pc], w=[dst])
                        else:
                            S.op("dve", lambda pcv=pcv, dst=dst, cs_=cs_: nc.vector.tensor_copy(dst[:, :, cs_], pcv[:, :, cs_]), r=[pc], w=[dst])
                    stage(14)
                    for d, tri in enumerate(("triF", "triB")):
                        pq = nps()
                        for b in range(4):
                            S.op("pe", lambda b=b, tri=tri, pq=pq: nc.tensor.matmul(pq[:, b * 128:(b + 1) * 128], ggp[:, b, :], C(tri), start=True, stop=True),
                                 r=[cst, ggp], w=[pq])
                        if d == 0:
                            S.op("act", lambda pq=pq: nc.scalar.copy(gto[0:32, :], pq[0:32, :]), r=[pq], w=[gto])
                        else:
                            S.op("dve", lambda pq=pq: nc.vector.tensor_copy(gto[32:64, :], pq[32:64, :]), r=[pq], w=[gto])
                    stage(12)
                    with nc.allow_non_contiguous_dma("gate tables"):
                        S.dma("sp", GC_d[t0:t0 + 512, :].rearrange("(b p) n -> p b n", p=128), gco[:], r=[gco], w=dbt("GC", t0, t0 + 512))
                        S.dma("sp", GX_d[t0:t0 + 512, :].rearrange("(b p) n -> p b n", p=128), gxo[:], r=[gxo], w=dbt("GX", t0, t0 + 512))
                        S.dma("sp", BT_d[t0:t0 + 512, :].rearrange("(b p) n -> p b n", p=128), gbt[:], r=[gbt], w=dbt("BT", t0, t0 + 512))
                    S.dma("sp", gcT_d[:, t0:t0 + 512], gto[:], r=[gto], w=dbt("gcT", t0, t0 + 512))
            S.barrier()
            stage(1)
            with ExitStack() as st:
                cw = sbuf(st, "cw", [128, 64, 5], F32)
                with nc.allow_non_contiguous_dma("conv weights"):
                    for j in range(5):
                        S.dma("sp", cw[:, :, j], I["gdn_conv"][j, :].rearrange("(c p) -> p c", p=128), w=[cw])
                win = [sbuf(st, f"win{i}", [128, 516], F32) for i in range(2)]
                acc = [sbuf(st, f"cacc{i}", [128, 512], F32) for i in range(2)]
                sq = [sbuf(st, f"csq{i}", [128, 512], F32) for i in range(2)]
                rn = [sbuf(st, f"crn{i}", [128, 512], F32) for i in range(2)]
                ob = [sbuf(st, f"cob{i}", [128, 512], BF16) for i in range(2)]
                tk = [sbuf(st, f"ctk{i}", [128, 4, 128], BF16) for i in range(2)]
                it = 0
                for tt in range(NT):
                    t0 = tt * 512
                    for fc in range(64):
                        w_ = win[it % 2]
                        a_ = acc[it % 2]
                        s_ = sq[it % 2]
                        r_ = rn[it % 2]
                        o_ = ob[it % 2]
                        k_ = tk[it % 2]
                        it += 1
                        lo = max(t0 - 2, 0)
                        hi = min(t0 + 514, NTOK)
                        if lo > t0 - 2:
                            S.op("pool", lambda w_=w_: nc.gpsimd.memset(w_[:, 0:2], 0.0), w=[w_])
                        if hi < t0 + 514:
                            S.op("pool", lambda w_=w_: nc.gpsimd.memset(w_[:, 514:516], 0.0), w=[w_])
                        S.dma("sp", w_[:, lo - (t0 - 2):hi - (t0 - 2)], raw[fc * 128:(fc + 1) * 128, lo:hi], r=dbt("raw", lo, hi), w=[w_])
                        if t0 % L == 0 and t0 > 0:
                            S.op("dve", lambda w_=w_: nc.vector.tensor_scalar(out=w_[:, 0:2], in0=w_[:, 0:2], scalar1=flag[:, 0:1], scalar2=None, op0=ALU.mult), r=[w_, flag], w=[w_])
                        if (t0 + 512) % L == 0 and t0 + 512 < NTOK:
                            S.op("dve", lambda w_=w_: nc.vector.tensor_scalar(out=w_[:, 514:516], in0=w_[:, 514:516], scalar1=flag[:, 0:1], scalar2=None, op0=ALU.mult), r=[w_, flag], w=[w_])
                        S.op("dve", lambda w_=w_, a_=a_, fc=fc: nc.vector.tensor_scalar(out=a_[:], in0=w_[:, 0:512], scalar1=cw[:, fc, 0:1], scalar2=None, op0=ALU.mult), r=[w_, cw], w=[a_])
                        for j in range(1, 5):
                            S.op("dve", lambda w_=w_, a_=a_, fc=fc, j=j: nc.vector.scalar_tensor_tensor(out=a_[:], in0=w_[:, j:j + 512], scalar=cw[:, fc, j:j + 1], in1=a_[:],
                                                                                                   op0=ALU.mult, op1=ALU.add), r=[w_, cw, a_], w=[a_])
                        S.op("act", lambda a_=a_: nc.scalar.activation(out=a_[:], in_=a_[:], func=AF.Silu), r=[a_], w=[a_])
                        if fc < 32:
                            S.op("pool", lambda a_=a_, s_=s_: nc.gpsimd.tensor_tensor(out=s_[:], in0=a_[:], in1=a_[:], op=ALU.mult), r=[a_], w=[s_])
                            ps = nps()
                            S.op("pe", lambda ps=ps, s_=s_: nc.tensor.matmul(ps[:, :], C("ones"), s_[:], start=True, stop=True), r=[cst, s_], w=[ps])
                            S.op("act", lambda ps=ps, r_=r_: nc.scalar.activation(out=r_[:], in_=ps[:, :], func=AF.Ln, bias=1e-6), r=[ps], w=[r_])
                            S.op("act", lambda r_=r_: nc.scalar.activation(out=r_[:], in_=r_[:], func=AF.Exp, scale=-0.5), r=[r_], w=[r_])
                            sc = (128.0 ** -0.5) if fc < 16 else 1.0
                            S.op("dve", lambda a_=a_, r_=r_, o_=o_, sc=sc: nc.vector.scalar_tensor_tensor(out=o_[:], in0=a_[:], scalar=sc, in1=r_[:], op0=ALU.mult, op1=ALU.mult),
                                 r=[a_, r_], w=[o_])
                            dst = qT_d if fc < 16 else kT_d
                            nm = "qT" if fc < 16 else "kT"
                            S.dma("sp", dst[(fc % 16) * 128:(fc % 16 + 1) * 128, t0:t0 + 512], o_[:], r=[o_], w=dbt(nm, t0, t0 + 512))
                        else:
                            S.op("dve", lambda a_=a_, o_=o_: nc.vector.tensor_copy(o_[:], a_[:]), r=[a_], w=[o_])
                        if fc >= 16:
                            ph = nph()
                            for b in range(4):
                                S.op("pe", lambda ph=ph, b=b, o_=o_: nc.tensor.transpose(ph[:, b * 128:(b + 1) * 128], o_[:, b * 128:(b + 1) * 128], Cb("ident")),
                                     r=[o_, cstb], w=[ph])
                            S.op("act", lambda ph=ph, k_=k_: nc.scalar.copy(k_[:], ph[:, :].rearrange("p (b n) -> p b n", b=4)), r=[ph], w=[k_])
                            if fc < 32:
                                S.dma("sp", ktok_d[t0:t0 + 512, (fc - 16) * 128:(fc - 15) * 128].rearrange("(b p) n -> p b n", p=128), k_[:], r=[k_], w=dbt("ktok", t0, t0 + 512))
                            else:
                                S.dma("sp", vtok_d[t0:t0 + 512, (fc - 32) * 128:(fc - 31) * 128].rearrange("(b p) n -> p b n", p=128), k_[:], r=[k_], w=dbt("vtok", t0, t0 + 512))
            S.barrier()
            stage(2)
            gdn_mixer()
            stage(4)
            layer_post(li, ogT_d, 4096, I["gdn_w_out"], "ogT", xin_name, xin_ap, last=(li == n_layers - 1))

        def gdn_mixer():
            LM = min(L, 1024)
            NBM = LM // 128
            TPM = LM // 512
            NSUB = NTOK // LM
            for d in range(2):
                with ExitStack() as st:
                    qTs = sbuf(st, "mq", [128, 2, LM], BF16)
                    kTs = sbuf(st, "mk", [128, 2, LM], BF16)
                    kts = sbuf(st, "mkt", [128, NBM, 2, 128], BF16)
                    vts = sbuf(st, "mvt", [128, NBM, 4, 128], BF16)
                    grow = sbuf(st, "mgrow", [128, 4, LM], F32)
                    egrow = sbuf(st, "megrow", [128, 4, LM], F32)
                    qd = sbuf(st, "mqd", [128, 4, LM], BF16)
                    gcc = sbuf(st, "mgcc", [128, NBM, 4], F32)
                    gxc = sbuf(st, "mgxc", [128, NBM, 4], F32)
                    btc = sbuf(st, "mbtc", [128, NBM, 4], F32)
                    nbt = sbuf(st, "mnbt", [128, NBM, 4], F32)
                    egc = sbuf(st, "megc", [128, NBM, 4], F32)
                    ekd = sbuf(st, "mekd", [128, NBM, 4], F32)
                    oacc = sbuf(st, "moacc", [128, 4, LM], F32)
                    Sst = sbuf(st, "mS", [128, 32, 128], F32)
                    Sb = sbuf(st, "mSb", [128, 4, 128], BF16)
                    I4 = sbuf(st, "mI4", [128, 4, 128], BF16)
                    for u in range(4):
                        S.op("dve", lambda u=u: nc.vector.tensor_copy(I4[:, u, :], C("ident")), r=[cst], w=[I4])
                    S.op("pool", lambda: nc.gpsimd.memset(Sst[:], 0.0), w=[Sst])
                    mk_s = "triFs" if d == 0 else "triBs"
                    mk_i = "triF" if d == 0 else "triB"
                    kk0 = sbuf(st, "mkk0", [128, 4, 128], F32)
                    qkm = sbuf(st, "mqkm", [128, 4, 128], F32)
                    ex4 = sbuf(st, "mex4", [128, 4, 128], F32)
                    DT4 = sbuf(st, "mDT4", [128, 4, 128], F32)
                    AT4 = sbuf(st, "mAT4", [128, 4, 128], BF16)
                    A4 = sbuf(st, "mA4", [128, 4, 128], BF16)
                    Aqk = [sbuf(st, f"mAqk{i}", [128, 4, 128], BF16) for i in range(2)]
                    XT = [sbuf(st, f"mXT{i}", [128, 4, 128], BF16) for i in range(2)]
                    P4 = [sbuf(st, f"mP{i}", [128, 4, 128], BF16) for i in range(2)]
                    PT4 = [sbuf(st, f"mPT{i}", [128, 4, 128], BF16) for i in range(2)]
                    kg = sbuf(st, "mkg", [128, 4, 128], BF16)
                    kdec = [sbuf(st, f"mkdec{i}", [128, 4, 128], BF16) for i in range(2)]
                    wT = [sbuf(st, f"mwT{i}", [128, 4, 128], BF16) for i in range(2)]
                    ub = [sbuf(st, f"mub{i}", [128, 4, 128], F32) for i in range(2)]
                    vn = sbuf(st, "mvn", [128, 4, 128], BF16)
                    sqn = sbuf(st, "msqn", [128, 512], F32)
                    rnn = sbuf(st, "mrnn", [128, 512], F32)
                    zt = sbuf(st, "mzt", [128, 4, 512], BF16)
                    ogo = sbuf(st, "mogo", [128, 4, 512], BF16)
                    nw = sbuf(st, "mnw", [128, 1], F32)
                    S.dma("sp", nw[:], I["gdn_norm"][:, :], w=[nw])
                    segs = list(range(NSUB)) if d == 0 else list(range(NSUB - 1, -1, -1))
                    for si, seg in enumerate(segs):
                        s0 = seg * LM
                        if si > 0 and ((s0 % L == 0) if d == 0 else ((s0 + LM) % L == 0)):
                            S.op("dve", lambda: nc.vector.tensor_scalar(out=Sst[:], in0=Sst[:], scalar1=flag[:, 0:1], scalar2=None, op0=ALU.mult), r=[Sst, flag], w=[Sst])
                        for hg in range(8):
                            rd = lambda nm: dbt(nm, s0, s0 + LM)
                            S.dma("sp", qTs[:], qT_d[hg * 256:(hg + 1) * 256, s0:s0 + LM].rearrange("(h p) t -> p h t", p=128), r=rd("qT"), w=[qTs])
                            S.dma("sp", kTs[:], kT_d[hg * 256:(hg + 1) * 256, s0:s0 + LM].rearrange("(h p) t -> p h t", p=128), r=rd("kT"), w=[kTs])
                            S.dma("sp", kts[:], ktok_d[s0:s0 + LM, hg * 256:(hg + 1) * 256].rearrange("(b p) (h n) -> p b h n", p=128, h=2), r=rd("ktok"), w=[kts])
                            S.dma("sp", vts[:], vtok_d[s0:s0 + LM, hg * 512:(hg + 1) * 512].rearrange("(b p) (h n) -> p b h n", p=128, h=4), r=rd("vtok"), w=[vts])
                            c0 = d * 32 + hg * 4
                            with nc.allow_non_contiguous_dma("gate cols"):
                                S.dma("sp", gcc[:], GC_d[s0:s0 + LM, c0:c0 + 4].rearrange("(b p) n -> p b n", p=128), r=rd("GC"), w=[gcc])
                                S.dma("sp", gxc[:], GX_d[s0:s0 + LM, c0:c0 + 4].rearrange("(b p) n -> p b n", p=128), r=rd("GX"), w=[gxc])
                                S.dma("sp", btc[:], BT_d[s0:s0 + LM, c0:c0 + 4].rearrange("(b p) n -> p b n", p=128), r=rd("BT"), w=[btc])
                            for u in range(4):
                                S.dma("sp", grow[:, u, :], gcT_d[c0 + u:c0 + u + 1, s0:s0 + LM].partition_broadcast(128), r=rd("gcT"), w=[grow])
                            if d == 1:
                                S.dma("sp", oacc[:], oT_d[hg * 512:(hg + 1) * 512, s0:s0 + LM].rearrange("(h p) t -> p h t", p=128), r=rd("oT"), w=[oacc])
                            S.op("act", lambda: nc.scalar.activation(out=egrow[:], in_=grow[:], func=AF.Exp), r=[grow], w=[egrow])
                            S.op("act", lambda: nc.scalar.activation(out=egc[:], in_=gcc[:], func=AF.Exp), r=[gcc], w=[egc])
                            S.op("act", lambda: nc.scalar.activation(out=ekd[:], in_=gxc[:], func=AF.Exp), r=[gxc], w=[ekd])
                            S.op("dve", lambda: nc.vector.tensor_scalar(out=nbt[:], in0=btc[:], scalar1=-1.0, scalar2=None, op0=ALU.mult), r=[btc], w=[nbt])
                            for u in range(4):
                                S.op("pool" if u % 2 else "dve",
                                     (lambda u=u: nc.gpsimd.tensor_tensor(out=qd[:, u, :], in0=qTs[:, u // 2, :], in1=egrow[:, u, :], op=ALU.mult)) if u % 2 else
                                     (lambda u=u: nc.vector.tensor_tensor(out=qd[:, u, :], in0=qTs[:, u // 2, :], in1=egrow[:, u, :], op=ALU.mult)),
                                     r=[qTs, egrow], w=[qd])
                            S.op("act", lambda hg=hg: nc.scalar.copy(Sb[:], Sst[:, hg * 4:(hg + 1) * 4, :]), r=[Sst], w=[Sb])
                            chunks = list(range(NBM)) if d == 0 else list(range(NBM - 1, -1, -1))
                            for ci, c in enumerate(chunks):
                                cs = slice(c * 128, (c + 1) * 128)
                                par = ci % 2
                                pk = nps()
                                pq = nps()
                                for u in range(4):
                                    h = u // 2
                                    S.op("pe", lambda u=u, h=h, pk=pk: nc.tensor.matmul(pk[:, u * 128:(u + 1) * 128], kTs[:, h, cs], kTs[:, h, cs], start=True, stop=True), r=[kTs], w=[pk])
                                    S.op("pe", lambda u=u, h=h, pq=pq: nc.tensor.matmul(pq[:, u * 128:(u + 1) * 128], kTs[:, h, cs], qTs[:, h, cs], start=True, stop=True), r=[kTs, qTs], w=[pq])
                                pkv = pk[:, :].rearrange("p (u n) -> p u n", u=4)
                                pqv = pq[:, :].rearrange("p (u n) -> p u n", u=4)
                                for u in range(4):
                                    S.op("dve", lambda u=u, pkv=pkv: nc.vector.tensor_tensor(out=kk0[:, u, :], in0=pkv[:, u, :], in1=C(mk_s), op=ALU.mult), r=[pk, cst], w=[kk0])
                                    S.op("dve", lambda u=u, pqv=pqv: nc.vector.tensor_tensor(out=qkm[:, u, :], in0=pqv[:, u, :], in1=C(mk_i), op=ALU.mult), r=[pq, cst], w=[qkm])
                                    S.op("dve", lambda u=u: nc.vector.tensor_scalar(out=ex4[:, u, :], in0=grow[:, u, cs], scalar1=gcc[:, c, u:u + 1], scalar2=0.0,
                                                                                    op0=ALU.subtract, op1=ALU.min), r=[grow, gcc], w=[ex4])
                                S.op("act", lambda: nc.scalar.activation(out=DT4[:], in_=ex4[:], func=AF.Exp), r=[ex4], w=[DT4])
                                aq = Aqk[par]
                                for u in range(4):
                                    S.op("dve", lambda u=u: nc.vector.scalar_tensor_tensor(out=AT4[:, u, :], in0=kk0[:, u, :], scalar=btc[:, c, u:u + 1], in1=DT4[:, u, :],
                                                                                           op0=ALU.mult, op1=ALU.mult), r=[kk0, btc, DT4], w=[AT4])
                                S.op("pool", lambda aq=aq: nc.gpsimd.tensor_tensor(out=aq[:], in0=qkm[:], in1=DT4[:], op=ALU.mult), r=[qkm, DT4], w=[aq])
                                ph = nph()
                                for u in range(4):
                                    S.op("pe", lambda u=u, ph=ph: nc.tensor.transpose(ph[:, u * 128:(u + 1) * 128], AT4[:, u, :], Cb("ident")), r=[AT4, cstb], w=[ph])
                                S.op("act", lambda ph=ph: nc.scalar.copy(A4[:], ph[:, :].rearrange("p (u n) -> p u n", u=4)), r=[ph], w=[A4])
                                xt = XT[0]
                                S.op("dve", lambda xt=xt: nc.vector.tensor_tensor(out=xt[:], in0=I4[:], in1=AT4[:], op=ALU.subtract), r=[I4, AT4], w=[xt])
                                Pc, PTc = A4, AT4
                                xi = 0
                                for lev in range(6):
                                    Pn = P4[lev % 2]
                                    PTn = PT4[lev % 2]
                                    pp = nps()
                                    for u in range(4):
                                        S.op("pe", lambda u=u, pp=pp, Pc=Pc, PTc=PTc: nc.tensor.matmul(pp[:, u * 128:(u + 1) * 128], PTc[:, u, :], Pc[:, u, :], start=True, stop=True),
                                             r=[Pc, PTc], w=[pp])
                                    S.op("act", lambda pp=pp, Pn=Pn: nc.scalar.copy(Pn[:], pp[:, :].rearrange("p (u n) -> p u n", u=4)), r=[pp], w=[Pn])
                                    if lev < 5:
                                        pp2 = nps()
                                        for u in range(4):
                                            S.op("pe", lambda u=u, pp2=pp2, Pc=Pc, PTc=PTc: nc.tensor.matmul(pp2[:, u * 128:(u + 1) * 128], Pc[:, u, :], PTc[:, u, :], start=True, stop=True),
                                                 r=[Pc, PTc], w=[pp2])
                                        S.op("act", lambda pp2=pp2, PTn=PTn: nc.scalar.copy(PTn[:], pp2[:, :].rearrange("p (u n) -> p u n", u=4)), r=[pp2], w=[PTn])
                                    px = nps()
                                    xo_ = XT[xi % 2]
                                    xn_ = XT[(xi + 1) % 2]
                                    for u in range(4):
                                        S.op("pe", lambda u=u, px=px, Pn=Pn, xo_=xo_: nc.tensor.matmul(px[:, u * 128:(u + 1) * 128], Pn[:, u, :], xo_[:, u, :], start=True, stop=True),
                                             r=[Pn, xo_], w=[px])
                                    S.op("dve", lambda px=px, xo_=xo_, xn_=xn_: nc.vector.tensor_tensor(out=xn_[:], in0=px[:, :].rearrange("p (u n) -> p u n", u=4), in1=xo_[:], op=ALU.add),
                                         r=[px, xo_], w=[xn_])
                                    xi += 1
                                    Pc, PTc = Pn, PTn
                                xt = XT[xi % 2]
                                kd = kdec[par]
                                for u in range(4):
                                    S.op("pool", lambda u=u: nc.gpsimd.tensor_scalar(out=kg[:, u, :], in0=kts[:, c, u // 2, :], scalar1=egc[:, c, u:u + 1], scalar2=None, op0=ALU.mult),
                                         r=[kts, egc], w=[kg])
                                    S.op("pool", lambda u=u, kd=kd: nc.gpsimd.tensor_scalar(out=kd[:, u, :], in0=kts[:, c, u // 2, :], scalar1=ekd[:, c, u:u + 1], scalar2=None, op0=ALU.mult),
                                         r=[kts, ekd], w=[kd])
                                pw = nps()
                                pu = nps()
                                for u in range(4):
                                    S.op("pe", lambda u=u, pw=pw, xt=xt: nc.tensor.matmul(pw[:, u * 128:(u + 1) * 128], kg[:, u, :], xt[:, u, :], start=True, stop=True), r=[kg, xt], w=[pw])
                                    S.op("pe", lambda u=u, pu=pu, xt=xt: nc.tensor.matmul(pu[:, u * 128:(u + 1) * 128], xt[:, u, :], vts[:, c, u, :], start=True, stop=True), r=[xt, vts], w=[pu])
                                w_ = wT[par]
                                u_ = ub[par]
                                S.op("act", lambda pw=pw, w_=w_: nc.scalar.copy(w_[:], pw[:, :].rearrange("p (u n) -> p u n", u=4)), r=[pw], w=[w_])
                                for u in range(4):
                                    S.op("act", lambda u=u, pu=pu, u_=u_: nc.scalar.activation(out=u_[:, u, :], in_=pu[:, u * 128:(u + 1) * 128], func=AF.Copy, scale=btc[:, c, u:u + 1]),
                                         r=[pu, btc], w=[u_])
                                pv = nps()
                                for u in range(4):
                                    S.op("pe", lambda u=u, pv=pv, w_=w_: nc.tensor.matmul(pv[:, u * 128:(u + 1) * 128], w_[:, u, :], Sb[:, u, :], start=True, stop=True), r=[w_, Sb], w=[pv])
                                for u in range(4):
                                    S.op("dve", lambda u=u, pv=pv, u_=u_: nc.vector.scalar_tensor_tensor(out=vn[:, u, :], in0=pv[:, u * 128:(u + 1) * 128], scalar=nbt[:, c, u:u + 1], in1=u_[:, u, :],
                                                                                                  op0=ALU.mult, op1=ALU.add), r=[pv, nbt, u_], w=[vn])
                                po = nps()
                                pS = nps()
                                for u in range(4):
                                    S.op("pe", lambda u=u, po=po: nc.tensor.matmul(po[:, u * 128:(u + 1) * 128], Sb[:, u, :], qd[:, u, cs], start=True, stop=False), r=[Sb, qd], w=[po])
                                    S.op("pe", lambda u=u, po=po, aq=aq: nc.tensor.matmul(po[:, u * 128:(u + 1) * 128], vn[:, u, :], aq[:, u, :], start=False, stop=True), r=[vn, aq], w=[po])
                                    S.op("pe", lambda u=u, pS=pS, kd=kd: nc.tensor.matmul(pS[:, u * 128:(u + 1) * 128], kd[:, u, :], vn[:, u, :], start=True, stop=True), r=[kd, vn], w=[pS])
                                pov = po[:, :].rearrange("p (u n) -> p u n", u=4)
                                if d == 0:
                                    S.op("act", lambda pov=pov: nc.scalar.copy(oacc[:, :, cs], pov), r=[po], w=[oacc])
                                else:
                                    S.op("dve", lambda pov=pov: nc.vector.tensor_tensor(out=oacc[:, :, cs], in0=oacc[:, :, cs], in1=pov, op=ALU.add), r=[po, oacc], w=[oacc])
                                gl = c * 128 + (127 if d == 0 else 0)
                                for u in range(4):
                                    S.op("dve", lambda u=u, pS=pS, hg=hg, gl=gl: nc.vector.scalar_tensor_tensor(out=Sst[:, hg * 4 + u, :], in0=Sst[:, hg * 4 + u, :], scalar=egrow[:, u, gl:gl + 1],
                                                                                                         in1=pS[:, u * 128:(u + 1) * 128], op0=ALU.mult, op1=ALU.add), r=[Sst, egrow, pS], w=[Sst])
                                S.op("act", lambda hg=hg: nc.scalar.copy(Sb[:], Sst[:, hg * 4:(hg + 1) * 4, :]), r=[Sst], w=[Sb])
                            if d == 0:
                                S.dma("sp", oT_d[hg * 512:(hg + 1) * 512, s0:s0 + LM].rearrange("(h p) t -> p h t", p=128), oacc[:], r=[oacc], w=dbt("oT", s0, s0 + LM))
                            else:
                                for tq in range(TPM):
                                    ts_ = slice(tq * 512, (tq + 1) * 512)
                                    S.dma("sp", zt[:], zs_d[hg * 512:(hg + 1) * 512, s0 + tq * 512:s0 + (tq + 1) * 512].rearrange("(h p) t -> p h t", p=128),
                                          r=dbt("zs", s0 + tq * 512, s0 + tq * 512 + 512), w=[zt])
                                    for u in range(4):
                                        S.op("act", lambda u=u: nc.scalar.activation(out=sqn[:], in_=oacc[:, u, ts_], func=AF.Square), r=[oacc], w=[sqn])
                                        ps = nps()
                                        S.op("pe", lambda ps=ps: nc.tensor.matmul(ps[:, :], C("ones"), sqn[:], start=True, stop=True), r=[cst, sqn], w=[ps])
                                        S.op("act", lambda ps=ps: nc.scalar.activation(out=rnn[:], in_=ps[:, :], func=AF.Ln, scale=1.0 / 128, bias=1e-6), r=[ps], w=[rnn])
                                        S.op("act", lambda: nc.scalar.activation(out=rnn[:], in_=rnn[:], func=AF.Exp, scale=-0.5), r=[rnn], w=[rnn])
                                        S.op("dve", lambda u=u: nc.vector.scalar_tensor_tensor(out=sqn[:], in0=oacc[:, u, ts_], scalar=nw[:, 0:1], in1=rnn[:], op0=ALU.mult, op1=ALU.mult),
                                             r=[oacc, nw, rnn], w=[sqn])
                                        S.op("pool", lambda u=u: nc.gpsimd.tensor_tensor(out=ogo[:, u, :], in0=sqn[:], in1=zt[:, u, :], op=ALU.mult), r=[sqn, zt], w=[ogo])
                                    S.dma("sp", ogT_d[hg * 512:(hg + 1) * 512, s0 + tq * 512:s0 + (tq + 1) * 512].rearrange("(h p) t -> p h t", p=128), ogo[:], r=[ogo],
                                          w=dbt("ogT", s0 + tq * 512, s0 + tq * 512 + 512))
                S.barrier()
                stage(3)

        def rwkv_layer(li, xin_name, xin_ap):
            NB = NTOK // 128
            rT_d = scr("r_r", [D, NTOK], F32)
            kT2_d = scr("r_k", [D, NTOK], F32)
            vT_d = scr("r_v", [D, NTOK], F32)
            lw_d = [scr(f"r_lw{d}", [D, NTOK], F32) for d in range(2)]
            a_d = [scr(f"r_a{d}", [D, NTOK], F32) for d in range(2)]
            g_d = scr("r_g", [D, NTOK], BF16)
            rt_d = [scr(f"r_rt{d}", [D, NTOK], BF16) for d in range(2)]
            bt_d = [scr(f"r_bt{d}", [D, NTOK], BF16) for d in range(2)]
            kt_d = [scr(f"r_kt{d}", [D, NTOK], BF16) for d in range(2)]
            at_d = [scr(f"r_at{d}", [D, NTOK], BF16) for d in range(2)]
            bhk_d = [scr(f"r_bhk{d}", [NTOK, D], BF16) for d in range(2)]
            khk_d = [scr(f"r_khk{d}", [NTOK, D], BF16) for d in range(2)]
            vtk_d = scr("r_vtk", [NTOK, D], BF16)
            WC_d = [scr(f"r_wc{d}", [D, NB], F32) for d in range(2)]
            bonus_d = scr("r_bonus", [D, NTOK], F32)
            ytok_d = scr("r_ytok", [NTOK, D], F32)
            ym_d = scr("r_ym", [D, NTOK], BF16)
            with ExitStack() as stl:
                w0c = [colvec(f"w0c{d}", I["rwkv_w0"][d, :], 16) for d in range(2)]
                a0c = [colvec(f"a0c{d}", I["rwkv_a0"][d, :], 16) for d in range(2)]
                kkc = colvec("kkc", I["rwkv_k_k"][0, :], 16)
                kac = colvec("kac", I["rwkv_k_a"][0, :], 16)
                rkc = colvec("rkc", I["rwkv_r_k"][0, :], 16)
                lwc = colvec("lwc", I["rwkv_ln_w"][0, :], 16)
                lbc = colvec("lbc", I["rwkv_ln_b"][0, :], 16)
                mixc = [colvec(f"mixc{m}", I["rwkv_mix"][m, :], 16) for m in range(6)]
                omka = sbuf(es, "omka", [128, 16], F32)
                S.op("dve", lambda: nc.vector.tensor_scalar(out=omka[:], in0=kac[:], scalar1=-1.0, scalar2=1.0, op0=ALU.mult, op1=ALU.add), r=[kac], w=[omka])
                with ExitStack() as st:
                    xh = sbuf(st, "rxh", [128, 16, 514], F32)
                    xx = sbuf(st, "rxx", [128, 16, 512], F32)
                    xm = [sbuf(st, f"rxm{i}", [128, 16, 512], BF16) for i in range(2)]
                    hl = [sbuf(st, f"rhl{i}", [128, 2, 512], BF16) for i in range(2)]
                    evf = [sbuf(st, f"revf{i}", [128, 512], F32) for i in range(3)]
                    evb = [sbuf(st, f"revb{i}", [128, 512], BF16) for i in range(3)]
                    cnt = [0]
                    for tt in range(NT):
                        t0 = tt * 512
                        lo = max(t0 - 1, 0)
                        hi = min(t0 + 513, NTOK)
                        if lo > t0 - 1:
                            S.op("pool", lambda: nc.gpsimd.memset(xh[:, :, 0:1], 0.0), w=[xh])
                        if hi < t0 + 513:
                            S.op("pool", lambda: nc.gpsimd.memset(xh[:, :, 513:514], 0.0), w=[xh])
                        S.dma("sp", xh[:, :, lo - (t0 - 1):hi - (t0 - 1)], xin_ap[:, lo:hi].rearrange("(c p) t -> p c t", p=128), r=dbt(xin_name, lo, hi), w=[xh])
                        if t0 % L == 0 and t0 > 0:
                            S.op("dve", lambda: nc.vector.tensor_scalar(out=xh[:, :, 0:1], in0=xh[:, :, 0:1], scalar1=flag[:, 0:1], scalar2=None, op0=ALU.mult), r=[xh, flag], w=[xh])
                        if (t0 + 512) % L == 0 and t0 + 512 < NTOK:
                            S.op("dve", lambda: nc.vector.tensor_scalar(out=xh[:, :, 513:514], in0=xh[:, :, 513:514], scalar1=flag[:, 0:1], scalar2=None, op0=ALU.mult), r=[xh, flag], w=[xh])
                        S.op("pool", lambda: nc.gpsimd.tensor_tensor(out=xx[:], in0=xh[:, :, 0:512], in1=xh[:, :, 2:514], op=ALU.add), r=[xh], w=[xx])
                        S.op("dve", lambda: nc.vector.scalar_tensor_tensor(out=xx[:], in0=xx[:], scalar=0.5, in1=xh[:, :, 1:513], op0=ALU.mult, op1=ALU.subtract), r=[xx, xh], w=[xx])

                        def mixed(m):
                            b = xm[m % 2]
                            for kc in range(16):
                                S.op("dve", lambda kc=kc, b=b: nc.vector.scalar_tensor_tensor(out=b[:, kc, :], in0=xx[:, kc, :], scalar=mixc[m][:, kc:kc + 1], in1=xh[:, kc, 1:513],
                                                                                          op0=ALU.mult, op1=ALU.add), r=[xx, xh, mixc[m]], w=[b])
                            return b

                        def ev_store(dst, nm, func=None, dtype_f32=True, bias=None, scale=1.0, post=None):
                            def ev(n, ps):
                                i = cnt[0] % 3
                                cnt[0] += 1
                                e = evf[i] if dtype_f32 else evb[i]
                                if func is None:
                                    S.op("act", lambda: nc.scalar.copy(e[:], ps[:, :]), r=[ps], w=[e])
                                elif bias is None:
                                    S.op("act", lambda: nc.scalar.activation(out=e[:], in_=ps[:, :], func=func), r=[ps], w=[e])
                                else:
                                    S.op("act", lambda: nc.scalar.activation(out=e[:], in_=ps[:, :], func=func, bias=bias[:, n:n + 1]), r=[ps, bias], w=[e])
                                if post is not None:
                                    S.op("dve", lambda: nc.vector.tensor_scalar(out=e[:], in0=e[:], scalar1=post, scalar2=None, op0=ALU.mult), r=[e], w=[e])
                                S.dma("sp", dst[n * 128:(n + 1) * 128, t0:t0 + 512], e[:], r=[e], w=dbt(nm, t0, t0 + 512))
                            return ev

                        b = mixed(0)
                        linear(I["rwkv_w_rkv"][0], D, D, lambda kc, b=b: b[:, kc, :], [b], ev_store(rT_d, "r_r"))
                        b = mixed(1)
                        for d in range(2):
                            h = hl[d]

                            def evh(n, ps, h=h):
                                S.op("act", lambda: nc.scalar.activation(out=h[0:96, 0, :], in_=ps[0:96, :], func=AF.Tanh), r=[ps], w=[h])
                            linear(I["rwkv_w1"][d], D, 96, lambda kc, b=b: b[:, kc, :], [b], evh)
                            linear(I["rwkv_w2"][d], 96, D, lambda kc, h=h: h[0:96, 0, :], [h], ev_store(lw_d[d], f"r_lw{d}", func=AF.Sigmoid, bias=w0c[d], post=-0.6065306597126334))
                        b = mixed(2)
                        linear(I["rwkv_w_rkv"][1], D, D, lambda kc, b=b: b[:, kc, :], [b], ev_store(kT2_d, "r_k"))
                        b = mixed(3)
                        linear(I["rwkv_w_rkv"][2], D, D, lambda kc, b=b: b[:, kc, :], [b], ev_store(vT_d, "r_v"))
                        b = mixed(4)
                        for d in range(2):
                            h = hl[d]

                            def evh2(n, ps, h=h):
                                S.op("act", lambda: nc.scalar.copy(h[0:96, 0, :], ps[0:96, :]), r=[ps], w=[h])
                            linear(I["rwkv_a1"][d], D, 96, lambda kc, b=b: b[:, kc, :], [b], evh2)
                            linear(I["rwkv_a2"][d], 96, D, lambda kc, h=h: h[0:96, 0, :], [h], ev_store(a_d[d], f"r_a{d}", func=AF.Sigmoid, bias=a0c[d]))
                        b = mixed(5)
                        h = hl[0]

                        def evg(n, ps, h=h):
                            S.op("act", lambda: nc.scalar.activation(out=h[:, n, :], in_=ps[:, :], func=AF.Sigmoid), r=[ps], w=[h])
                        linear(I["rwkv_g1"], D, 256, lambda kc, b=b: b[:, kc, :], [b], evg)
                        linear(I["rwkv_g2"], 256, D, lambda kc, h=h: h[:, kc, :], [h], ev_store(g_d, "r_g", dtype_f32=False))
                S.barrier()
                stage(23)
                with ExitStack() as st:
                    def T(name, dt=F32, shape=(128, 512)):
                        return sbuf(st, name, list(shape), dt)
                    k_ = T("qk"); v_ = T("qv"); r_ = T("qr")
                    a_ = [T("qa0"), T("qa1")]
                    lw_ = [T("ql0"), T("ql1")]
                    t1 = T("qt1"); sq = T("qsq"); rn = T("qrn"); kk = T("qkk"); tmp = T("qtmp"); kd = T("qkd"); bp = T("qbp"); kds = T("qkds")
                    lwt = sbuf(st, "qlwt", [128, 4, 128], F32)
                    gc = T("qgc"); gcp = T("qgcp"); E1 = T("qE1"); E2 = T("qE2"); E3 = T("qE3"); E4 = T("qE4")
                    ob = [T(f"qob{i}", BF16) for i in range(6)]
                    vb = T("qvb", BF16)
                    tkb = [sbuf(st, f"qtk{i}", [128, 4, 128], BF16) for i in range(3)]
                    wcs = sbuf(st, "qwc", [128, 4], F32)
                    bon = T("qbon")
                    for tt in range(NT):
                        t0 = tt * 512
                        for fc in range(16):
                            rows = slice(fc * 128, (fc + 1) * 128)
                            S.dma("sp", k_[:], kT2_d[rows, t0:t0 + 512], r=dbt("r_k", t0, t0 + 512), w=[k_])
                            S.dma("sp", v_[:], vT_d[rows, t0:t0 + 512], r=dbt("r_v", t0, t0 + 512), w=[v_])
                            S.dma("sp", r_[:], rT_d[rows, t0:t0 + 512], r=dbt("r_r", t0, t0 + 512), w=[r_])
                            for d in range(2):
                                S.dma("sp", a_[d][:], a_d[d][rows, t0:t0 + 512], r=dbt(f"r_a{d}", t0, t0 + 512), w=[a_[d]])
                                S.dma("sp", lw_[d][:], lw_d[d][rows, t0:t0 + 512], r=dbt(f"r_lw{d}", t0, t0 + 512), w=[lw_[d]])
                            S.op("dve", lambda: nc.vector.tensor_scalar(out=t1[:], in0=k_[:], scalar1=kkc[:, fc:fc + 1], scalar2=None, op0=ALU.mult), r=[k_, kkc], w=[t1])
                            S.op("pool", lambda: nc.gpsimd.tensor_tensor(out=sq[:], in0=t1[:], in1=t1[:], op=ALU.mult), r=[t1], w=[sq])
                            ps = nps()
                            S.op("pe", lambda ps=ps: nc.tensor.matmul(ps[:, :], C("blk64"), sq[:], start=True, stop=True), r=[cst, sq], w=[ps])
                            S.op("act", lambda ps=ps: nc.scalar.activation(out=rn[:], in_=ps[:, :], func=AF.Ln, bias=1e-6), r=[ps], w=[rn])
                            S.op("act", lambda: nc.scalar.activation(out=rn[:], in_=rn[:], func=AF.Exp, scale=-0.5), r=[rn], w=[rn])
                            S.op("dve", lambda: nc.vector.tensor_tensor(out=kk[:], in0=t1[:], in1=rn[:], op=ALU.mult), r=[t1, rn], w=[kk])
                            S.op("act", lambda: nc.scalar.copy(vb[:], v_[:]), r=[v_], w=[vb])
                            ph = nph()
                            for b in range(4):
                                S.op("pe", lambda ph=ph, b=b: nc.tensor.transpose(ph[:, b * 128:(b + 1) * 128], vb[:, b * 128:(b + 1) * 128], Cb("ident")), r=[vb, cstb], w=[ph])
                            tk = tkb[0]
                            S.op("act", lambda ph=ph, tk=tk: nc.scalar.copy(tk[:], ph[:, :].rearrange("p (b n) -> p b n", b=4)), r=[ph], w=[tk])
                            S.dma("sp", vtk_d[t0:t0 + 512, rows].rearrange("(b p) n -> p b n", p=128), tk[:], r=[tk], w=dbt("r_vtk", t0, t0 + 512))
                            for d in range(2):
                                tri = "triF" if d == 0 else "triB"
                                S.op("dve", lambda d=d: nc.vector.tensor_scalar(out=tmp[:], in0=a_[d][:], scalar1=kac[:, fc:fc + 1], scalar2=omka[:, fc:fc + 1], op0=ALU.mult, op1=ALU.add),
                                     r=[a_[d], kac, omka], w=[tmp])
                                S.op("pool", lambda: nc.gpsimd.tensor_tensor(out=kd[:], in0=k_[:], in1=tmp[:], op=ALU.mult), r=[k_, tmp], w=[kd])
                                S.op("dve", lambda d=d: nc.vector.tensor_tensor(out=bp[:], in0=kk[:], in1=a_[d][:], op=ALU.mult), r=[kk, a_[d]], w=[bp])
                                if d == 0:
                                    S.op("pool", lambda: nc.gpsimd.tensor_copy(kds[:], kd[:]), r=[kd], w=[kds])
                                else:
                                    S.op("pool", lambda: nc.gpsimd.tensor_tensor(out=kds[:], in0=kds[:], in1=kd[:], op=ALU.add), r=[kds, kd], w=[kds])
                                pt = nps()
                                for b in range(4):
                                    S.op("pe", lambda pt=pt, b=b, d=d: nc.tensor.transpose(pt[:, b * 128:(b + 1) * 128], lw_[d][:, b * 128:(b + 1) * 128], C("ident")), r=[lw_[d], cst], w=[pt])
                                S.op("act", lambda pt=pt: nc.scalar.copy(lwt[:], pt[:, :].rearrange("p (b n) -> p b n", b=4)), r=[pt], w=[lwt])
                                pg = nps()
                                for b in range(4):
                                    S.op("pe", lambda pg=pg, b=b, tri=tri: nc.tensor.matmul(pg[:, b * 128:(b + 1) * 128], lwt[:, b, :], C(tri), start=True, stop=True), r=[lwt, cst], w=[pg])
                                S.op("dve", lambda pg=pg: nc.vector.tensor_copy(gc[:], pg[:, :]), r=[pg], w=[gc])
                                S.op("pool", lambda d=d: nc.gpsimd.tensor_tensor(out=gcp[:], in0=gc[:], in1=lw_[d][:], op=ALU.subtract), r=[gc, lw_[d]], w=[gcp])
                                S.op("act", lambda: nc.scalar.activation(out=E1[:], in_=gc[:], func=AF.Exp), r=[gc], w=[E1])
                                S.op("act", lambda: nc.scalar.activation(out=E2[:], in_=gc[:], func=AF.Exp, scale=-1.0), r=[gc], w=[E2])
                                S.op("act", lambda: nc.scalar.activation(out=E3[:], in_=gcp[:], func=AF.Exp), r=[gcp], w=[E3])
                                for b in range(4):
                                    gl = b * 128 + (127 if d == 0 else 0)
                                    S.op("act", lambda b=b, gl=gl: nc.scalar.activation(out=E4[:, b * 128:(b + 1) * 128], in_=gc[:, b * 128:(b + 1) * 128], func=AF.Exp, scale=-1.0, bias=gc[:, gl:gl + 1]),
                                         r=[gc], w=[E4])
                                    S.op("act", lambda b=b, gl=gl: nc.scalar.copy(wcs[:, b:b + 1], E1[:, gl:gl + 1]), r=[E1], w=[wcs])
                                with nc.allow_non_contiguous_dma("wc"):
                                    S.dma("sp", WC_d[d][rows, tt * 4:(tt + 1) * 4], wcs[:], r=[wcs], w=[db(f"r_wc{d}", tt)])
                                S.op("dve", lambda: nc.vector.tensor_tensor(out=ob[0][:], in0=r_[:], in1=E1[:], op=ALU.mult), r=[r_, E1], w=[ob[0]])
                                S.op("pool", lambda: nc.gpsimd.tensor_tensor(out=ob[1][:], in0=bp[:], in1=E2[:], op=ALU.mult), r=[bp, E2], w=[ob[1]])
                                S.op("dve", lambda: nc.vector.tensor_tensor(out=ob[2][:], in0=kd[:], in1=E2[:], op=ALU.mult), r=[kd, E2], w=[ob[2]])
                                S.op("dve", lambda: nc.vector.scalar_tensor_tensor(out=ob[3][:], in0=kk[:], scalar=-1.0, in1=E3[:], op0=ALU.mult, op1=ALU.mult), r=[kk, E3], w=[ob[3]])
                                S.op("pool", lambda: nc.gpsimd.tensor_tensor(out=ob[4][:], in0=bp[:], in1=E4[:], op=ALU.mult), r=[bp, E4], w=[ob[4]])
                                S.op("dve", lambda: nc.vector.tensor_tensor(out=ob[5][:], in0=kd[:], in1=E4[:], op=ALU.mult), r=[kd, E4], w=[ob[5]])
                                for i, (dst, nm) in enumerate(((rt_d[d], f"r_rt{d}"), (bt_d[d], f"r_bt{d}"), (kt_d[d], f"r_kt{d}"), (at_d[d], f"r_at{d}"))):
                                    S.dma("sp", dst[rows, t0:t0 + 512], ob[i][:], r=[ob[i]], w=dbt(nm, t0, t0 + 512))
                                for i, (dst, nm) in ((4, (bhk_d[d], f"r_bhk{d}")), (5, (khk_d[d], f"r_khk{d}"))):
                                    ph = nph()
                                    for b in range(4):
                                        S.op("pe", lambda ph=ph, b=b, i=i: nc.tensor.transpose(ph[:, b * 128:(b + 1) * 128], ob[i][:, b * 128:(b + 1) * 128], Cb("ident")), r=[ob[i], cstb], w=[ph])
                                    tk = tkb[i - 3]
                                    S.op("act", lambda ph=ph, tk=tk: nc.scalar.copy(tk[:], ph[:, :].rearrange("p (b n) -> p b n", b=4)), r=[ph], w=[tk])
                                    S.dma("sp", dst[t0:t0 + 512, rows].rearrange("(b p) n -> p b n", p=128), tk[:], r=[tk], w=dbt(nm, t0, t0 + 512))
                            S.op("dve", lambda: nc.vector.scalar_tensor_tensor(out=tmp[:], in0=r_[:], scalar=rkc[:, fc:fc + 1], in1=kds[:], op0=ALU.mult, op1=ALU.mult), r=[r_, rkc, kds], w=[tmp])
                            ps = nps()
                            S.op("pe", lambda ps=ps: nc.tensor.matmul(ps[:, :], C("blk64"), tmp[:], start=True, stop=True), r=[cst, tmp], w=[ps])
                            S.op("dve", lambda ps=ps: nc.vector.tensor_tensor(out=bon[:], in0=ps[:, :], in1=v_[:], op=ALU.mult), r=[ps, v_], w=[bon])
                            S.dma("sp", bonus_d[rows, t0:t0 + 512], bon[:], r=[bon], w=dbt("r_bonus", t0, t0 + 512))
                S.barrier()
                stage(26)
                LM = min(L, 1024)
                NBM = LM // 128
                NSUB = NTOK // LM
                for d in range(2):
                    with ExitStack() as st:
                        rt = sbuf(st, "srt", [128, 2, LM], BF16)
                        bt = sbuf(st, "sbt", [128, 2, LM], BF16)
                        kt = sbuf(st, "skt", [128, 2, LM], BF16)
                        at = sbuf(st, "sat", [128, 2, LM], BF16)
                        ath = [sbuf(st, f"sath{i}", [128, 2, LM], BF16) for i in range(2)]
                        rth = [sbuf(st, f"srth{i}", [128, 2, LM], BF16) for i in range(2)]
                        bhk = sbuf(st, "sbhk", [128, NBM, 256], BF16)
                        khk = sbuf(st, "skhk", [128, NBM, 256], BF16)
                        vtk = sbuf(st, "svtk", [128, NBM, 256], BF16)
                        wc = sbuf(st, "swc", [128, 2, NBM], F32)
                        oacc = sbuf(st, "soacc", [128, NBM, 256], F32)
                        Hst = sbuf(st, "sH", [128, 16, 128], F32)
                        Hb = sbuf(st, "sHb", [128, 2, 128], BF16)
                        S.op("pool", lambda: nc.gpsimd.memset(Hst[:], 0.0), w=[Hst])
                        I4 = sbuf(st, "sI4", [128, 4, 128], BF16)
                        mS = sbuf(st, "smS", [128, 4, 128], F32)
                        mSt = sbuf(st, "smSt", [128, 4, 128], F32)
                        mI = sbuf(st, "smI", [128, 4, 128], F32)
                        for u in range(4):
                            S.op("dve", lambda u=u: nc.vector.tensor_copy(I4[:, u, :], C("ident")), r=[cst], w=[I4])
                            S.op("dve", lambda u=u: nc.vector.tensor_copy(mS[:, u, :], C("triFs" if d == 0 else "triBs")), r=[cst], w=[mS])
                            S.op("dve", lambda u=u: nc.vector.tensor_copy(mSt[:, u, :], C("triBs" if d == 0 else "triFs")), r=[cst], w=[mSt])
                            S.op("dve", lambda u=u: nc.vector.tensor_copy(mI[:, u, :], C("triF" if d == 0 else "triB")), r=[cst], w=[mI])
                        NT4 = sbuf(st, "sNT4", [128, 4, 128], BF16)
                        N4 = sbuf(st, "sN4", [128, 4, 128], BF16)
                        XT = [sbuf(st, f"sXT{i}", [128, 4, 128], BF16) for i in range(2)]
                        P4 = [sbuf(st, f"sP{i}", [128, 4, 128], BF16) for i in range(2)]
                        PT4 = [sbuf(st, f"sPT{i}", [128, 4, 128], BF16) for i in range(2)]
                        evt = [sbuf(st, f"sevt{i}", [128, 4, 128], F32) for i in range(3)]
                        Aak = sbuf(st, "sAak", [128, 4, 128], BF16)
                        Arb = [sbuf(st, f"sArb{i}", [128, 4, 128], BF16) for i in range(2)]
                        Ark = [sbuf(st, f"sArk{i}", [128, 4, 128], BF16) for i in range(2)]
                        akv = [sbuf(st, f"sakv{i}", [128, 256], F32) for i in range(2)]
                        rhsu = sbuf(st, "srhsu", [128, 256], BF16)
                        ut = sbuf(st, "sut", [128, 256], BF16)

                        def v4(ps):
                            return ps[:, :].rearrange("p (u n) -> p u n", u=4)

                        subs = list(range(NSUB)) if d == 0 else list(range(NSUB - 1, -1, -1))
                        for si, sub in enumerate(subs):
                            s0 = sub * LM
                            if si > 0 and ((s0 % L == 0) if d == 0 else ((s0 + LM) % L == 0)):
                                S.op("dve", lambda: nc.vector.tensor_scalar(out=Hst[:], in0=Hst[:], scalar1=flag[:, 0:1], scalar2=None, op0=ALU.mult), r=[Hst, flag], w=[Hst])
                            for fp in range(8):
                                frows = slice(fp * 256, (fp + 1) * 256)
                                for tile_, src, nm in ((rt, rt_d[d], f"r_rt{d}"), (bt, bt_d[d], f"r_bt{d}"), (kt, kt_d[d], f"r_kt{d}"), (at, at_d[d], f"r_at{d}")):
                                    S.dma("sp", tile_[:], src[frows, s0:s0 + LM].rearrange("(f p) t -> p f t", p=128), r=dbt(nm, s0, s0 + LM), w=[tile_])
                                for tile_, src, nm in ((bhk, bhk_d[d], f"r_bhk{d}"), (khk, khk_d[d], f"r_khk{d}"), (vtk, vtk_d, "r_vtk")):
                                    S.dma("sp", tile_[:], src[s0:s0 + LM, frows].rearrange("(b p) n -> p b n", p=128), r=dbt(nm, s0, s0 + LM), w=[tile_])
                                with nc.allow_non_contiguous_dma("wc"):
                                    S.dma("sp", wc[:], WC_d[d][frows, s0 // 128:s0 // 128 + NBM].rearrange("(f p) b -> p f b", p=128),
                                          r=[db(f"r_wc{d}", i) for i in range(s0 // 512, (s0 + LM) // 512)], w=[wc])
                                if d == 1:
                                    S.dma("sp", oacc[:], ytok_d[s0:s0 + LM, frows].rearrange("(b p) n -> p b n", p=128), r=dbt("r_ytok", s0, s0 + LM), w=[oacc])
                                for h_ in range(2):
                                    oth = slice((1 - h_) * 64, (2 - h_) * 64)
                                    S.op("pool", lambda h_=h_: nc.gpsimd.tensor_copy(ath[h_][:], at[:]), r=[at], w=[ath[h_]])
                                    S.op("pool", lambda h_=h_, oth=oth: nc.gpsimd.memset(ath[h_][oth, :, :], 0.0), w=[ath[h_]])
                                    S.op("dve", lambda h_=h_: nc.vector.tensor_copy(rth[h_][:], rt[:]), r=[rt], w=[rth[h_]])
                                    S.op("dve", lambda h_=h_, oth=oth: nc.vector.memset(rth[h_][oth, :, :], 0.0), w=[rth[h_]])
                                S.op("act", lambda fp=fp: nc.scalar.copy(Hb[:], Hst[:, 2 * fp:2 * fp + 2, :]), r=[Hst], w=[Hb])
                                chunks = list(range(NBM)) if d == 0 else list(range(NBM - 1, -1, -1))
                                for ci, c in enumerate(chunks):
                                    cs = slice(c * 128, (c + 1) * 128)
                                    par = ci % 2
                                    UN = [(u // 2, u % 2, slice((u % 2) * 64, (u % 2) * 64 + 64)) for u in range(4)]
                                    stage(29)
                                    pNT = nps()
                                    pN = nps()
                                    for u, (f, h, P) in enumerate(UN):
                                        S.op("pe", lambda u=u, f=f, h=h, pNT=pNT: nc.tensor.matmul(pNT[:, u * 128:(u + 1) * 128], bt[:, f, cs], ath[h][:, f, cs], start=True, stop=True), r=[bt, ath[h]], w=[pNT])
                                        S.op("pe", lambda u=u, f=f, h=h, pN=pN: nc.tensor.matmul(pN[:, u * 128:(u + 1) * 128], ath[h][:, f, cs], bt[:, f, cs], start=True, stop=True), r=[bt, ath[h]], w=[pN])
                                    S.op("dve", lambda pNT=pNT: nc.vector.tensor_tensor(out=NT4[:], in0=v4(pNT), in1=mS[:], op=ALU.mult), r=[pNT, mS], w=[NT4])
                                    S.op("dve", lambda pN=pN: nc.vector.tensor_tensor(out=N4[:], in0=v4(pN), in1=mSt[:], op=ALU.mult), r=[pN, mSt], w=[N4])
                                    xt = XT[0]
                                    S.op("pool", lambda xt=xt: nc.gpsimd.tensor_tensor(out=xt[:], in0=I4[:], in1=NT4[:], op=ALU.add), r=[I4, NT4], w=[xt])
                                    stage(30)
                                    outs = []
                                    for j, (la, ra, msk, dst) in enumerate(((kt, ath, mS, Aak), (bt, rth, mI, Arb[par]), (kt, rth, mI, Ark[par]))):
                                        pp = nps()
                                        for u, (f, h, P) in enumerate(UN):
                                            S.op("pe", lambda u=u, f=f, h=h, pp=pp, la=la, ra=ra: nc.tensor.matmul(pp[:, u * 128:(u + 1) * 128], la[:, f, cs], ra[h][:, f, cs], start=True, stop=True),
                                                 r=[la, ra[h]], w=[pp])
                                        e = evt[j]
                                        S.op("act", lambda pp=pp, e=e: nc.scalar.copy(e[:], v4(pp)), r=[pp], w=[e])
                                        S.op("pool", lambda e=e, msk=msk, dst=dst: nc.gpsimd.tensor_tensor(out=dst[:], in0=e[:], in1=msk[:], op=ALU.mult), r=[e, msk], w=[dst])
                                    arb = Arb[par]
                                    ark = Ark[par]
                                    stage(31)
                                    Pc, PTc = N4, NT4
                                    xi = 0
                                    for lev in range(6):
                                        Pn = P4[lev % 2]
                                        PTn = PT4[lev % 2]
                                        pp = nps()
                                        for u in range(4):
                                            S.op("pe", lambda u=u, pp=pp, Pc=Pc, PTc=PTc: nc.tensor.matmul(pp[:, u * 128:(u + 1) * 128], PTc[:, u, :], Pc[:, u, :], start=True, stop=True), r=[Pc, PTc], w=[pp])
                                        S.op("act", lambda pp=pp, Pn=Pn: nc.scalar.copy(Pn[:], v4(pp)), r=[pp], w=[Pn])
                                        if lev < 5:
                                            pp2 = nps()
                                            for u in range(4):
                                                S.op("pe", lambda u=u, pp2=pp2, Pc=Pc, PTc=PTc: nc.tensor.matmul(pp2[:, u * 128:(u + 1) * 128], Pc[:, u, :], PTc[:, u, :], start=True, stop=True), r=[Pc, PTc], w=[pp2])
                                            S.op("act", lambda pp2=pp2, PTn=PTn: nc.scalar.copy(PTn[:], v4(pp2)), r=[pp2], w=[PTn])
                                        px = nps()
                                        xo_ = XT[xi % 2]
                                        xn_ = XT[(xi + 1) % 2]
                                        for u in range(4):
                                            S.op("pe", lambda u=u, px=px, Pn=Pn, xo_=xo_: nc.tensor.matmul(px[:, u * 128:(u + 1) * 128], Pn[:, u, :], xo_[:, u, :], start=True, stop=True), r=[Pn, xo_], w=[px])
                                        S.op("dve", lambda px=px, xo_=xo_, xn_=xn_: nc.vector.tensor_tensor(out=xn_[:], in0=v4(px), in1=xo_[:], op=ALU.add), r=[px, xo_], w=[xn_])
                                        xi += 1
                                        Pc, PTc = Pn, PTn
                                    xt = XT[xi % 2]
                                    stage(32)
                                    pk = nps()
                                    for u in range(4):
                                        S.op("pe", lambda u=u, pk=pk: nc.tensor.matmul(pk[:, u * 128:u * 128 + 64], Aak[:, u, :], vtk[:, c, u * 64:(u + 1) * 64], start=True, stop=True), r=[Aak, vtk], w=[pk])
                                    av = akv[par]
                                    S.op("act", lambda pk=pk, av=av: nc.scalar.copy(av[:, :].rearrange("p (u n) -> p u n", u=4), v4(pk)[:, :, 0:64]), r=[pk], w=[av])
                                    stage(33)
                                    p1 = nps()
                                    for u, (f, h, P) in enumerate(UN):
                                        S.op("pe", lambda u=u, f=f, h=h, P=P, p1=p1: nc.tensor.matmul(p1[:, u * 128:u * 128 + 64], ath[h][:, f, cs], Hb[:, f, h * 64:(h + 1) * 64], start=True, stop=True), r=[ath[h], Hb], w=[p1])
                                    S.op("dve", lambda p1=p1, av=av: nc.vector.tensor_tensor(out=rhsu[:, :].rearrange("p (u n) -> p u n", u=4), in0=v4(p1)[:, :, 0:64],
                                                                                         in1=av[:, :].rearrange("p (u n) -> p u n", u=4), op=ALU.add), r=[p1, av], w=[rhsu])
                                    stage(34)
                                    p2 = nps()
                                    for u in range(4):
                                        S.op("pe", lambda u=u, p2=p2, xt=xt: nc.tensor.matmul(p2[:, u * 128:u * 128 + 64], xt[:, u, :], rhsu[:, u * 64:(u + 1) * 64], start=True, stop=True), r=[xt, rhsu], w=[p2])
                                    S.op("act", lambda p2=p2: nc.scalar.copy(ut[:, :].rearrange("p (u n) -> p u n", u=4), v4(p2)[:, :, 0:64]), r=[p2], w=[ut])
                                    stage(35)
                                    po = nps()
                                    for u, (f, h, P) in enumerate(UN):
                                        S.op("pe", lambda u=u, f=f, h=h, P=P, po=po: nc.tensor.matmul(po[:, u * 128:u * 128 + 64], rth[h][:, f, cs], Hb[:, f, h * 64:(h + 1) * 64], start=True, stop=False), r=[rth[h], Hb], w=[po])
                                        S.op("pe", lambda u=u, po=po, arb=arb: nc.tensor.matmul(po[:, u * 128:u * 128 + 64], arb[:, u, :], ut[:, u * 64:(u + 1) * 64], start=False, stop=False), r=[arb, ut], w=[po])
                                        S.op("pe", lambda u=u, po=po, ark=ark: nc.tensor.matmul(po[:, u * 128:u * 128 + 64], ark[:, u, :], vtk[:, c, u * 64:(u + 1) * 64], start=False, stop=True), r=[ark, vtk], w=[po])
                                    ov = oacc[:, c, :].rearrange("p (u n) -> p u n", u=4)
                                    if d == 0:
                                        S.op("act", lambda po=po, ov=ov: nc.scalar.copy(ov, v4(po)[:, :, 0:64]), r=[po], w=[oacc])
                                    else:
                                        S.op("dve", lambda po=po, ov=ov: nc.vector.tensor_tensor(out=ov, in0=ov, in1=v4(po)[:, :, 0:64], op=ALU.add), r=[po, oacc], w=[oacc])
                                    stage(36)
                                    pH = nps()
                                    for f in range(2):
                                        S.op("pe", lambda f=f, pH=pH: nc.tensor.matmul(pH[:, f * 128:(f + 1) * 128], bhk[:, c, f * 128:(f + 1) * 128], ut[:, f * 128:(f + 1) * 128], start=True, stop=False), r=[bhk, ut], w=[pH])
                                        S.op("pe", lambda f=f, pH=pH: nc.tensor.matmul(pH[:, f * 128:(f + 1) * 128], khk[:, c, f * 128:(f + 1) * 128], vtk[:, c, f * 128:(f + 1) * 128], start=False, stop=True), r=[khk, vtk], w=[pH])
                                    for f in range(2):
                                        S.op("dve", lambda f=f, pH=pH, fp=fp: nc.vector.scalar_tensor_tensor(out=Hst[:, 2 * fp + f, :], in0=Hst[:, 2 * fp + f, :], scalar=wc[:, f, c:c + 1],
                                                                                                         in1=pH[:, f * 128:(f + 1) * 128], op0=ALU.mult, op1=ALU.add), r=[Hst, wc, pH], w=[Hst])
                                    S.op("act", lambda fp=fp: nc.scalar.copy(Hb[:], Hst[:, 2 * fp:2 * fp + 2, :]), r=[Hst], w=[Hb])
                                S.dma("sp", ytok_d[s0:s0 + LM, frows].rearrange("(b p) n -> p b n", p=128), oacc[:], r=[oacc], w=dbt("r_ytok", s0, s0 + LM))
                    S.barrier()
                stage(27)
                with ExitStack() as st:
                    yt = sbuf(st, "pyt", [128, 4, D], F32)
                    yf = sbuf(st, "pyf", [128, 512], F32)
                    sq2 = sbuf(st, "psq", [128, 512], F32)
                    mean = sbuf(st, "pmean", [128, 512], F32)
                    rstd = sbuf(st, "prstd", [128, 512], F32)
                    bon = sbuf(st, "pbon", [128, 512], F32)
                    gt = sbuf(st, "pgt", [128, 512], BF16)
                    yo = [sbuf(st, f"pyo{i}", [128, 512], BF16) for i in range(2)]
                    for tt in range(NT):
                        t0 = tt * 512
                        S.dma("sp", yt[:], ytok_d[t0:t0 + 512, :].rearrange("(b p) n -> p b n", p=128), r=dbt("r_ytok", t0, t0 + 512), w=[yt])
                        for fc in range(16):
                            rows = slice(fc * 128, (fc + 1) * 128)
                            S.dma("sp", bon[:], bonus_d[rows, t0:t0 + 512], r=dbt("r_bonus", t0, t0 + 512), w=[bon])
                            S.dma("sp", gt[:], g_d[rows, t0:t0 + 512], r=dbt("r_g", t0, t0 + 512), w=[gt])
                            pt = nps()
                            for b in range(4):
                                S.op("pe", lambda pt=pt, b=b: nc.tensor.transpose(pt[:, b * 128:(b + 1) * 128], yt[:, b, fc * 128:(fc + 1) * 128], C("ident")), r=[yt, cst], w=[pt])
                            S.op("act", lambda pt=pt: nc.scalar.copy(yf[:], pt[:, :]), r=[pt], w=[yf])
                            S.op("pool", lambda: nc.gpsimd.tensor_tensor(out=sq2[:], in0=yf[:], in1=yf[:], op=ALU.mult), r=[yf], w=[sq2])
                            p1 = nps()
                            p2 = nps()
                            S.op("pe", lambda p1=p1: nc.tensor.matmul(p1[:, :], C("blk64"), yf[:], start=True, stop=True), r=[cst, yf], w=[p1])
                            S.op("pe", lambda p2=p2: nc.tensor.matmul(p2[:, :], C("blk64"), sq2[:], start=True, stop=True), r=[cst, sq2], w=[p2])
                            S.op("act", lambda p1=p1: nc.scalar.mul(mean[:], p1[:, :], 1.0 / 64), r=[p1], w=[mean])
                            S.op("dve", lambda: nc.vector.tensor_tensor(out=rstd[:], in0=mean[:], in1=mean[:], op=ALU.mult), r=[mean], w=[rstd])
                            S.op("dve", lambda p2=p2: nc.vector.scalar_tensor_tensor(out=rstd[:], in0=p2[:, :], scalar=1.0 / 64, in1=rstd[:], op0=ALU.mult, op1=ALU.subtract), r=[p2, rstd], w=[rstd])
                            S.op("act", lambda: nc.scalar.activation(out=rstd[:], in_=rstd[:], func=AF.Ln, bias=64e-5), r=[rstd], w=[rstd])
                            S.op("act", lambda: nc.scalar.activation(out=rstd[:], in_=rstd[:], func=AF.Exp, scale=-0.5), r=[rstd], w=[rstd])
                            S.op("dve", lambda: nc.vector.tensor_tensor(out=yf[:], in0=yf[:], in1=mean[:], op=ALU.subtract), r=[yf, mean], w=[yf])
                            S.op("pool", lambda: nc.gpsimd.tensor_tensor(out=yf[:], in0=yf[:], in1=rstd[:], op=ALU.mult), r=[yf, rstd], w=[yf])
                            S.op("act", lambda: nc.scalar.activation(out=yf[:], in_=yf[:], func=AF.Identity, scale=lwc[:, fc:fc + 1], bias=lbc[:, fc:fc + 1]), r=[yf, lwc, lbc], w=[yf])
                            S.op("dve", lambda: nc.vector.tensor_tensor(out=yf[:], in0=yf[:], in1=bon[:], op=ALU.add), r=[yf, bon], w=[yf])
                            o = yo[fc % 2]
                            S.op("pool", lambda o=o: nc.gpsimd.tensor_tensor(out=o[:], in0=yf[:], in1=gt[:], op=ALU.mult), r=[yf, gt], w=[o])
                            S.dma("sp", ym_d[rows, t0:t0 + 512], o[:], r=[o], w=dbt("r_ym", t0, t0 + 512))
                S.barrier()
                stage(20)
                layer_post(li, ym_d, D, I["rwkv_w_o"], "r_ym", xin_name, xin_ap, last=(li == n_layers - 1))

        try:
            gdn_layer(0, "xT0", xT[0])
            if n_layers > 1:
                rwkv_layer(1, "xT1", xT[1])
        except _Stop:
            pass
        S.finish()
        n_inst = S.n_inst
    return nc, n_inst


_CACHE = {}


def run_stream(NSEG, L, xs, ps, flags, W, n_layers=2):
    key = (NSEG, L, n_layers)
    if key not in _CACHE:
        _CACHE[key] = build(NSEG, L, n_layers)
    nc, n_inst = _CACHE[key]
    in_maps = []
    for x, p, f in zip(xs, ps, flags):
        m = {"x": np.ascontiguousarray(x, dtype=np.float32), "p": np.ascontiguousarray(p, dtype=np.float32),
             "flag": np.full((128, 1), f, np.float32), "consts": CONST_ARR}
        m.update(W)
        in_maps.append(m)
    res = run_bass_kernel_spmd(nc, in_maps, core_ids=list(range(len(in_maps))))
    return [np.asarray(r["y"]) for r in res.results]


def prep_weights(kw):
    W = {}
    for n, sh in W_SPECS:
        a = np.asarray(kw[n], dtype=np.float32)
        if n in ("ln_g", "ln_b"):
            a = a.reshape(4, D)
        elif n in ("gdn_a_log", "gdn_dt_bias"):
            a = a.reshape(1, 64)
        elif n == "gdn_norm":
            a = a.reshape(128, 1)
        elif n == "rwkv_r_k":
            a = a.reshape(1, D)
        else:
            a = a.reshape(sh) if a.ndim != len(sh) or list(a.shape) != sh else a
            a = a.reshape(sh)
        W[n] = np.ascontiguousarray(a)
    return W


def kernel(x_prompt, x_sample, p_prompt, p_sample, **kw):
    W = prep_weights(kw)
    NSEG, L = 4, 2048
    xp = np.asarray(x_prompt, np.float32)
    xs_ = np.asarray(x_sample, np.float32)
    pp = np.asarray(p_prompt, np.float32)
    psm = np.asarray(p_sample, np.float32)
    zx = np.zeros((NSEG * L, D), np.float32)
    zp = np.zeros((2, NSEG * L, 256), np.float32)
    xs = [zx] * 8
    ps = [zp] * 8
    flags = [0.0] * 8
    xs[0], ps[0], flags[0] = xp[0:4].reshape(NSEG * L, D), pp[:, 0:4].reshape(2, NSEG * L, 256), 0.0
    xs[1], ps[1], flags[1] = xp[4:8].reshape(NSEG * L, D), pp[:, 4:8].reshape(2, NSEG * L, 256), 0.0
    xs[4], ps[4], flags[4] = xs_[0], psm[:, 0], 1.0
    ys = run_stream(NSEG, L, xs, ps, flags, W)
    y_prompt = np.concatenate([ys[0].reshape(4, L, D), ys[1].reshape(4, L, D)], 0)
    y_sample = ys[4].reshape(1, NSEG * L, D)
    return (y_prompt.astype(np.float32), y_sample.astype(np.float32))
```

```python
import numpy as np
from contextlib import ExitStack
import concourse.bass as bass
import concourse.mybir as mybir
from concourse.bass_utils import run_bass_kernel_spmd

F32 = mybir.dt.float32
BF16 = mybir.dt.bfloat16
AF = mybir.ActivationFunctionType
ALU = mybir.AluOpType

SEM_EPOCH = 30000
SAME_ENGINE_SYNC = False
STAGE = 99


class _Stop(Exception):
    pass


DEAD = [False]


def stage(k):
    if STAGE == k:
        DEAD[0] = True


class Buf:
    __slots__ = ("t", "lw", "rd", "name")

    def __init__(self, t=None, name=""):
        self.t = t
        self.lw = None
        self.rd = []
        self.name = name

    def __getitem__(self, idx):
        return self.t[idx]


class Sched:
    def __init__(self, nc, es, n_dma_sems=6):
        self.nc = nc
        self.es = es
        self.engs = {"pe": nc.tensor, "act": nc.scalar, "dve": nc.vector, "pool": nc.gpsimd, "sp": nc.sync}
        self.sems = {}
        self.cur = {}
        self.epoch = {}
        self.allsems = {e: [] for e in self.engs}
        for e in self.engs:
            self.epoch[e] = 0
            self._new_sem(e)
        self.waited = {}
        self.dq = {}
        for q in ("sp", "pool", "act"):
            lst = []
            for i in range(n_dma_sems):
                k = f"d_{q}_{i}"
                self.sems[k] = es.enter_context(nc.semaphore(k))
                lst.append([k, 0])
            self.dq[q] = [lst, 0]
        self.n_inst = 0

    def _new_sem(self, e):
        k = f"c_{e}_{self.epoch[e]}"
        self.sems[k] = self.es.enter_context(self.nc.semaphore(k))
        self.cur[e] = [k, 0]
        self.allsems[e].append(self.cur[e])
        self.epoch[e] += 1

    def _wait(self, e, tok):
        if tok is None:
            return
        k, v = tok
        if self.waited.get((e, k), 0) >= v:
            return
        self.engs[e].wait_ge(self.sems[k], v)
        self.waited[(e, k)] = v
        self.n_inst += 1

    def _toks(self, r, w):
        toks = []
        for b in r:
            if b.lw is not None:
                toks.append(b.lw)
        for b in w:
            if b.lw is not None:
                toks.append(b.lw)
            toks.extend(b.rd)
        return toks

    def _commit(self, tok, r, w):
        for b in w:
            b.lw = tok
            b.rd = []
        for b in r:
            if b in w:
                continue
            b.rd.append(tok)
            if len(b.rd) > 10:
                d = {}
                for k, v in b.rd:
                    d[k] = max(d.get(k, 0), v)
                b.rd = list(d.items())

    def op(self, e, fn, r=(), w=()):
        if DEAD[0]:
            return None
        pre = f"c_{e}_"
        for tok in self._toks(r, w):
            if (not SAME_ENGINE_SYNC or e == "pe") and tok[0].startswith(pre):
                continue
            self._wait(e, tok)
        if self.cur[e][1] >= SEM_EPOCH:
            self._new_sem(e)
        ins = fn()
        c = self.cur[e]
        c[1] += 1
        ins.then_inc(self.sems[c[0]], 1)
        tok = (c[0], c[1])
        self._commit(tok, r, w)
        self.n_inst += 1
        return tok

    def dma(self, q, out, in_, r=(), w=(), **kw):
        if DEAD[0]:
            return None
        lst, idx = self.dq[q]
        slot = lst[idx % len(lst)]
        self.dq[q][1] = idx + 1
        if slot[1] > 0:
            self._wait(q, (slot[0], slot[1]))
        for tok in self._toks(r, w):
            self._wait(q, tok)
        ins = self.engs[q].dma_start(out=out, in_=in_, **kw)
        slot[1] += 16
        ins.then_inc(self.sems[slot[0]], 16)
        tok = (slot[0], slot[1])
        self._commit(tok, r, w)
        self.n_inst += 1
        return tok

    def barrier(self):
        toks = []
        for e in self.engs:
            c = self.cur[e]
            if c[1] > 0:
                toks.append((c[0], c[1]))
        for q in ("sp", "pool", "act"):
            for k, v in self.dq[q][0]:
                if v > 0:
                    toks.append((k, v))
        for e in self.engs:
            for tok in toks:
                if tok[0].startswith(f"c_{e}_"):
                    continue
                self._wait(e, tok)

    def finish(self):
        self.barrier()


D = 2048
KC = 16
ALPHA = 4 ** 0.25
LN_EPS = 1e-5


def make_consts():
    i = np.arange(128)
    s = i[:, None]
    t = i[None, :]
    c = {}
    c["ident"] = np.eye(128, dtype=np.float32)
    c["triF"] = (s <= t).astype(np.float32)
    c["triB"] = (s >= t).astype(np.float32)
    c["triFs"] = (s < t).astype(np.float32)
    c["triBs"] = (s > t).astype(np.float32)
    c["ones"] = np.ones((128, 128), np.float32)
    blk = (s // 64 == t // 64).astype(np.float32)
    c["blk64"] = blk
    names = ["ident", "triF", "triB", "triFs", "triBs", "ones", "blk64"]
    arr = np.stack([c[n] for n in names], 0)
    return names, np.ascontiguousarray(arr.transpose(1, 0, 2))


CONST_NAMES, CONST_ARR = make_consts()

W_SPECS = [
    ("gdn_w_in", [D, 12416]), ("gdn_conv", [5, 8192]), ("gdn_a_log", [1, 64]), ("gdn_dt_bias", [1, 64]),
    ("gdn_norm", [128, 1]), ("gdn_w_out", [4096, D]),
    ("rwkv_mix", [6, D]), ("rwkv_w_rkv", [3, D, D]), ("rwkv_w0", [2, D]), ("rwkv_w1", [2, D, 96]),
    ("rwkv_w2", [2, 96, D]), ("rwkv_a0", [2, D]), ("rwkv_a1", [2, D, 96]), ("rwkv_a2", [2, 96, D]),
    ("rwkv_g1", [D, 256]), ("rwkv_g2", [256, D]), ("rwkv_k_k", [1, D]), ("rwkv_k_a", [1, D]),
    ("rwkv_r_k", [1, D]), ("rwkv_ln_w", [1, D]), ("rwkv_ln_b", [1, D]), ("rwkv_w_o", [D, D]),
    ("ln_g", [4, D]), ("ln_b", [4, D]), ("mlp_w_up", [2, D, 8192]), ("mlp_w_down", [2, 8192, D]),
    ("ple_w_proj", [2, 256, D]), ("ple_w_gate", [2, D, D]),
]


def build(NSEG, L, n_layers=2):
    DEAD[0] = False
    NTOK = NSEG * L
    NT = NTOK // 512
    NBS = L // 128
    TPS = L // 512
    nc = bass.Bass("TRN2", target_bir_lowering=False)
    I = {}

    def inp(name, shape):
        I[name] = nc.dram_tensor(name, shape, F32, kind="ExternalInput").ap()
        return I[name]

    x_in = inp("x", [NTOK, D])
    p_in = inp("p", [2, NTOK, 256])
    flag_in = inp("flag", [128, 1])
    consts_in = inp("consts", [128, len(CONST_NAMES), 128])
    for n, sh in W_SPECS:
        inp(n, sh)
    y_out = nc.dram_tensor("y", [NTOK, D], F32, kind="ExternalOutput").ap()

    def scr(name, shape, dt):
        return nc.dram_tensor(name, shape, dt).ap()

    xT = [scr("xT0", [D, NTOK], F32), scr("xT1", [D, NTOK], F32)]
    raw = scr("raw", [8192, NTOK], F32)
    qT_d = scr("qT", [2048, NTOK], BF16)
    kT_d = scr("kT", [2048, NTOK], BF16)
    ktok_d = scr("ktok", [NTOK, 2048], BF16)
    vtok_d = scr("vtok", [NTOK, 4096], BF16)
    zs_d = scr("zs", [4096, NTOK], BF16)
    GC_d = scr("GC", [NTOK, 64], F32)
    GX_d = scr("GX", [NTOK, 64], F32)
    BT_d = scr("BT", [NTOK, 64], F32)
    gcT_d = scr("gcT", [64, NTOK], F32)
    oT_d = scr("oT", [4096, NTOK], F32)
    ogT_d = scr("ogT", [4096, NTOK], BF16)

    dbs = {}

    def db(name, i):
        k = (name, i)
        if k not in dbs:
            dbs[k] = Buf(None, f"{name}{i}")
        return dbs[k]

    def dbt(name, t0, t1):
        t0 = max(t0, 0)
        t1 = min(t1, NTOK)
        return [db(name, i) for i in range(t0 // 512, (t1 - 1) // 512 + 1)]

    with ExitStack() as es:
        S = Sched(nc, es)

        uid = [0]

        def sbuf(st, name, shape, dt):
            uid[0] += 1
            name = f"{name}_{uid[0]}"
            return Buf(st.enter_context(nc.sbuf_tensor(name, shape, dt)), name)

        psb = [Buf(es.enter_context(nc.psum_tensor(f"ps{i}", [128, 512], F32)), f"ps{i}") for i in range(6)]
        psh = [Buf(es.enter_context(nc.psum_tensor(f"ph{i}", [128, 512], BF16)), f"ph{i}") for i in range(2)]
        psi = [0, 0]

        def nps():
            psi[0] += 1
            return psb[psi[0] % 6]

        def nph():
            psi[1] += 1
            return psh[psi[1] % 2]

        cst = sbuf(es, "cst", [128, len(CONST_NAMES), 128], F32)
        S.dma("sp", cst[:], consts_in[:, :, :], w=[cst])
        cstb = sbuf(es, "cstb", [128, len(CONST_NAMES), 128], BF16)
        S.op("dve", lambda: nc.vector.tensor_copy(cstb[:], cst[:]), r=[cst], w=[cstb])
        CI = {n: i for i, n in enumerate(CONST_NAMES)}

        def C(n):
            return cst[:, CI[n], :]

        def Cb(n):
            return cstb[:, CI[n], :]

        flag = sbuf(es, "flag_sb", [128, 1], F32)
        S.dma("sp", flag[:], flag_in[:, :], w=[flag])
        def colvec(name, ap_row, nch):
            b = sbuf(es, name, [128, nch], F32)
            with nc.allow_non_contiguous_dma("small param vector"):
                S.dma("sp", b[:], ap_row.rearrange("(c p) -> p c", p=128), w=[b])
            return b

        lng = [colvec(f"lng{i}", I["ln_g"][i, :], 16) for i in range(4)]
        lnb = [colvec(f"lnb{i}", I["ln_b"][i, :], 16) for i in range(4)]

        wbufs = [sbuf(es, f"wb{i}", [128, 16, 256], BF16) for i in range(3)]
        wbi = [0]

        def linear(w_ap, K, N, rhs, rbufs, evac, ncols=512):
            nkc = (K + 127) // 128
            nkg = (nkc + 15) // 16
            for ng in range((N + 255) // 256):
                n0 = ng * 256
                nw = min(256, N - n0)
                pss = [nps() for _ in range((nw + 127) // 128)]
                for kg in range(nkg):
                    wb = wbufs[wbi[0] % 3]
                    wbi[0] += 1
                    kcs = min(16, nkc - kg * 16)
                    k0 = kg * 16 * 128
                    krows = min(K - k0, kcs * 128)
                    if krows % 128 == 0:
                        S.dma("pool", wb[:, 0:kcs, 0:nw],
                              w_ap[k0:k0 + krows, n0:n0 + nw].rearrange("(c p) n -> p c n", p=128), w=[wb])
                    else:
                        S.dma("pool", wb[0:krows, 0, 0:nw], w_ap[k0:k0 + krows, n0:n0 + nw], w=[wb])
                    for j, ps in enumerate(pss):
                        mw = min(128, nw - j * 128)
                        for kc in range(kcs):
                            kk = kg * 16 + kc
                            kp = min(128, K - kk * 128)
                            S.op("pe", lambda ps=ps, wb=wb, kc=kc, j=j, kk=kk, kp=kp, mw=mw: nc.tensor.matmul(
                                ps[0:mw, 0:ncols], wb[0:kp, kc, j * 128:j * 128 + mw], rhs(kk),
                                start=(kk == 0), stop=(kk == nkc - 1)), r=[wb] + list(rbufs), w=[ps])
                for j, ps in enumerate(pss):
                    evac(ng * 2 + j, ps)

        def layernorm(st_tiles, y, li, xb):
            sq, mean, rstd = st_tiles
            p1 = nps()
            p2 = nps()
            for kc in range(16):
                S.op("pe", lambda kc=kc: nc.tensor.matmul(p1[:, :], C("ones"), y[:, kc, :], start=(kc == 0), stop=(kc == 15)),
                     r=[cst, y], w=[p1])
            for kc in range(16):
                S.op("act", lambda kc=kc: nc.scalar.activation(out=sq[:, kc % 2, :], in_=y[:, kc, :], func=AF.Square), r=[y], w=[sq])
                S.op("pe", lambda kc=kc: nc.tensor.matmul(p2[:, :], C("ones"), sq[:, kc % 2, :], start=(kc == 0), stop=(kc == 15)),
                     r=[cst, sq], w=[p2])
            S.op("act", lambda: nc.scalar.mul(mean[:], p1[:, :], 1.0 / D), r=[p1], w=[mean])
            S.op("dve", lambda: nc.vector.tensor_tensor(out=rstd[:], in0=mean[:], in1=mean[:], op=ALU.mult), r=[mean], w=[rstd])
            S.op("dve", lambda: nc.vector.scalar_tensor_tensor(out=rstd[:], in0=p2[:, :], scalar=1.0 / D, in1=rstd[:],
                                                               op0=ALU.mult, op1=ALU.subtract), r=[p2, rstd], w=[rstd])
            S.op("act", lambda: nc.scalar.activation(out=rstd[:], in_=rstd[:], func=AF.Ln, bias=LN_EPS), r=[rstd], w=[rstd])
            S.op("act", lambda: nc.scalar.activation(out=rstd[:], in_=rstd[:], func=AF.Exp, scale=-0.5), r=[rstd], w=[rstd])
            for kc in range(16):
                S.op("dve", lambda kc=kc: nc.vector.tensor_tensor(out=y[:, kc, :], in0=y[:, kc, :], in1=mean[:], op=ALU.subtract), r=[y, mean], w=[y])
                S.op("pool", lambda kc=kc: nc.gpsimd.tensor_tensor(out=y[:, kc, :], in0=y[:, kc, :], in1=rstd[:], op=ALU.mult), r=[y, rstd], w=[y])
                S.op("act", lambda kc=kc: nc.scalar.activation(out=y[:, kc, :], in_=y[:, kc, :], func=AF.Identity,
                                                               scale=lng[li][:, kc:kc + 1], bias=lnb[li][:, kc:kc + 1]), r=[y, lng[li], lnb[li]], w=[y])
                if xb is not None:
                    S.op("dve", lambda kc=kc: nc.vector.tensor_copy(xb[:, kc, :], y[:, kc, :]), r=[y], w=[xb])

        with ExitStack() as st:
            xin = [sbuf(st, f"xin{i}", [128, D], F32) for i in range(2)]
            xo = [sbuf(st, f"xo{i}", [128, 16, 128], F32) for i in range(2)]
            for b in range(NTOK // 128):
                xi = xin[b % 2]
                o = xo[b % 2]
                S.dma("pool", xi[:], x_in[b * 128:(b + 1) * 128, :], w=[xi])
                for g in range(4):
                    ps = nps()
                    for j in range(4):
                        kc = g * 4 + j
                        S.op("pe", lambda ps=ps, j=j, kc=kc, xi=xi: nc.tensor.transpose(ps[:, j * 128:(j + 1) * 128], xi[:, kc * 128:(kc + 1) * 128], C("ident")),
                             r=[xi, cst], w=[ps])
                    eng = "act" if g % 2 == 0 else "dve"
                    if eng == "act":
                        S.op("act", lambda ps=ps, g=g, o=o: nc.scalar.copy(o[:, g * 4:(g + 1) * 4, :], ps[:, :].rearrange("p (a b) -> p a b", a=4)), r=[ps], w=[o])
                    else:
                        S.op("dve", lambda ps=ps, g=g, o=o: nc.vector.tensor_copy(o[:, g * 4:(g + 1) * 4, :], ps[:, :].rearrange("p (a b) -> p a b", a=4)), r=[ps], w=[o])
                S.dma("sp", xT[0][:, b * 128:(b + 1) * 128].rearrange("(c p) t -> p c t", p=128), o[:], r=[o], w=dbt("xT0", b * 128, b * 128 + 128))
        S.barrier()

        def layer_post(li, mix_src, mix_K, w_mix, src_name, xin_name, xin_ap, last):
            with ExitStack() as st:
                nkc_mix = mix_K // 128
                hbuf = sbuf(st, "hbuf", [128, 64, 512], BF16)
                xr = sbuf(st, "xr", [128, 16, 512], F32)
                xb = sbuf(st, "xb", [128, 16, 512], BF16)
                sq = sbuf(st, "lsq", [128, 2, 512], F32)
                mean = sbuf(st, "lmean", [128, 512], F32)
                rstd = sbuf(st, "lrstd", [128, 512], F32)
                tmp = sbuf(st, "ltmp", [128, 2, 512], F32)
                ptok = sbuf(st, "ptok", [128, 4, 256], F32)
                pT = sbuf(st, "pT", [128, 2, 512], BF16)
                yo = [sbuf(st, f"yo{i}", [128, 4, 512], F32) for i in range(2)] if last else None
                for tt in range(NT):
                    t0 = tt * 512
                    S.dma("sp", hbuf[:, 0:nkc_mix, :], mix_src[:, t0:t0 + 512].rearrange("(c p) t -> p c t", p=128),
                          r=dbt(src_name, t0, t0 + 512), w=[hbuf])
                    S.dma("sp", xr[:], xin_ap[:, t0:t0 + 512].rearrange("(c p) t -> p c t", p=128), r=dbt(xin_name, t0, t0 + 512), w=[xr])

                    def ev1(n, ps):
                        S.op("dve", lambda: nc.vector.scalar_tensor_tensor(out=xr[:, n, :], in0=xr[:, n, :], scalar=ALPHA, in1=ps[:, :],
                                                                           op0=ALU.mult, op1=ALU.add), r=[xr, ps], w=[xr])
                    linear(w_mix, mix_K, D, lambda kc: hbuf[:, kc, :], [hbuf], ev1)
                    layernorm((sq, mean, rstd), xr, 2 * li, xb)

                    def ev2(n, ps):
                        S.op("act", lambda: nc.scalar.activation(out=tmp[:, n % 2, :], in_=ps[:, :], func=AF.Relu), r=[ps], w=[tmp])
                        S.op("pool", lambda: nc.gpsimd.tensor_tensor(out=hbuf[:, n, :], in0=tmp[:, n % 2, :], in1=tmp[:, n % 2, :], op=ALU.mult), r=[tmp], w=[hbuf])
                    linear(I["mlp_w_up"][li], D, 8192, lambda kc: xb[:, kc, :], [xb], ev2)
                    linear(I["mlp_w_down"][li], 8192, D, lambda kc: hbuf[:, kc, :], [hbuf], ev1)
                    layernorm((sq, mean, rstd), xr, 2 * li + 1, xb)
                    S.dma("sp", ptok[:], p_in[li, t0:t0 + 512, :].rearrange("(b p) j -> p b j", p=128), w=[ptok])
                    for jc in range(2):
                        ps = nps()
                        for b in range(4):
                            S.op("pe", lambda ps=ps, b=b, jc=jc: nc.tensor.transpose(ps[:, b * 128:(b + 1) * 128], ptok[:, b, jc * 128:(jc + 1) * 128], C("ident")),
                                 r=[ptok, cst], w=[ps])
                        S.op("act", lambda ps=ps, jc=jc: nc.scalar.copy(pT[:, jc, :], ps[:, :]), r=[ps], w=[pT])
                    sg = hbuf

                    def ev3(n, ps):
                        S.op("act", lambda: nc.scalar.activation(out=sg[:, n, :], in_=ps[:, :], func=AF.Sigmoid), r=[ps], w=[sg])
                    linear(I["ple_w_gate"][li], D, D, lambda kc: xb[:, kc, :], [xb], ev3)

                    def ev4(n, ps):
                        S.op("dve", lambda: nc.vector.tensor_tensor(out=tmp[:, n % 2, :], in0=ps[:, :], in1=sg[:, n, :], op=ALU.mult), r=[ps, sg], w=[tmp])
                        S.op("pool", lambda: nc.gpsimd.tensor_tensor(out=xr[:, n, :], in0=xr[:, n, :], in1=tmp[:, n % 2, :], op=ALU.add), r=[xr, tmp], w=[xr])
                    linear(I["ple_w_proj"][li], 256, D, lambda kc: pT[:, kc, :], [pT], ev4)
                    if not last:
                        S.dma("sp", xT[1][:, t0:t0 + 512].rearrange("(c p) t -> p c t", p=128), xr[:], r=[xr], w=dbt("xT1", t0, t0 + 512))
                    else:
                        for b in range(4):
                            for g in range(4):
                                o = yo[(b * 4 + g) % 2]
                                ps = nps()
                                for j in range(4):
                                    kc = g * 4 + j
                                    S.op("pe", lambda ps=ps, j=j, kc=kc, b=b: nc.tensor.transpose(ps[:, j * 128:(j + 1) * 128], xr[:, kc, b * 128:(b + 1) * 128], C("ident")),
                                         r=[xr, cst], w=[ps])
                                if g % 2 == 0:
                                    S.op("act", lambda ps=ps, o=o, g=g: nc.scalar.copy(o[:, g, :], ps[:, :]), r=[ps], w=[o])
                                else:
                                    S.op("dve", lambda ps=ps, o=o, g=g: nc.vector.tensor_copy(o[:, g, :], ps[:, :]), r=[ps], w=[o])
                                S.dma("sp", y_out[t0 + b * 128:t0 + (b + 1) * 128, g * 512:(g + 1) * 512], o[:, g, :], r=[o])
            S.barrier()

        def gdn_layer(li, xin_name, xin_ap):
            Win = I["gdn_w_in"]
            stage(0)
            with ExitStack() as st:
                xb = sbuf(st, "gxb", [128, 16, 512], BF16)
                wg = sbuf(st, "gwg", [128, 16, 128], BF16)
                S.dma("pool", wg[:], Win[:, 12288:12416].rearrange("(c p) n -> p c n", p=128), w=[wg])
                ev_f = [sbuf(st, f"gevf{i}", [128, 512], F32) for i in range(3)]
                ev_b = [sbuf(st, f"gevb{i}", [128, 512], BF16) for i in range(3)]
                dtb = sbuf(st, "dtb", [128, 4, 64], F32)
                nega = sbuf(st, "nega", [128, 4, 64], F32)
                for b in range(4):
                    S.dma("sp", dtb[:, b, :], I["gdn_dt_bias"][0:1, :].partition_broadcast(128), w=[dtb])
                    S.dma("sp", nega[:, b, :], I["gdn_a_log"][0:1, :].partition_broadcast(128), w=[nega])
                S.op("act", lambda: nc.scalar.activation(out=nega[:], in_=nega[:], func=AF.Exp), r=[nega], w=[nega])
                S.op("dve", lambda: nc.vector.tensor_scalar(out=nega[:], in0=nega[:], scalar1=-1.0, scalar2=None, op0=ALU.mult), r=[nega], w=[nega])
                gg = sbuf(st, "gg", [128, 4, 64], F32)
                ggp = sbuf(st, "ggp", [128, 4, 128], F32)
                S.op("pool", lambda: nc.gpsimd.memset(ggp[:], 0.0), w=[ggp])
                gbt = sbuf(st, "gbt", [128, 4, 64], F32)
                gco = sbuf(st, "gco", [128, 4, 64], F32)
                gxo = sbuf(st, "gxo", [128, 4, 64], F32)
                gto = sbuf(st, "gto", [64, 512], F32)
                cnt = [0]
                stage(10)
                for tt in range(NT):
                    t0 = tt * 512
                    S.dma("pool", xb[:], xin_ap[:, t0:t0 + 512].rearrange("(c p) t -> p c t", p=128), r=dbt(xin_name, t0, t0 + 512), w=[xb])

                    def evq(n, ps):
                        i = cnt[0] % 3
                        cnt[0] += 1
                        if n < 64:
                            e = ev_f[i]
                            S.op("act", lambda: nc.scalar.copy(e[:], ps[:, :]), r=[ps], w=[e])
                            S.dma("sp", raw[n * 128:(n + 1) * 128, t0:t0 + 512], e[:], r=[e], w=dbt("raw", t0, t0 + 512))
                        else:
                            e = ev_b[i]
                            S.op("act", lambda: nc.scalar.activation(out=e[:], in_=ps[:, :], func=AF.Silu), r=[ps], w=[e])
                            S.dma("sp", zs_d[(n - 64) * 128:(n - 63) * 128, t0:t0 + 512], e[:], r=[e], w=dbt("zs", t0, t0 + 512))
                    linear(Win, D, 12288, lambda kc: xb[:, kc, :], [xb], evq)
                    stage(11)
                    ps = nps()
                    for b in range(4):
                        for kc in range(16):
                            S.op("pe", lambda b=b, kc=kc, ps=ps: nc.tensor.matmul(ps[:, b * 128:(b + 1) * 128], xb[:, kc, b * 128:(b + 1) * 128], wg[:, kc, :],
                                                                                start=(kc == 0), stop=(kc == 15)), r=[xb, wg], w=[ps])
                    psv = ps[:, :].rearrange("p (b n) -> p b n", b=4)
                    S.op("dve", lambda: nc.vector.tensor_tensor(out=gg[:], in0=psv[:, :, 0:64], in1=dtb[:], op=ALU.add), r=[ps, dtb], w=[gg])
                    S.op("act", lambda: nc.scalar.activation(out=gbt[:], in_=psv[:, :, 64:128], func=AF.Sigmoid), r=[ps], w=[gbt])
                    S.op("act", lambda: nc.scalar.activation(out=gg[:], in_=gg[:], func=AF.Exp), r=[gg], w=[gg])
                    S.op("act", lambda: nc.scalar.activation(out=gg[:], in_=gg[:], func=AF.Ln, bias=1.0), r=[gg], w=[gg])
                    S.op("dve", lambda: nc.vector.tensor_tensor(out=gg[:], in0=gg[:], in1=nega[:], op=ALU.mult), r=[gg, nega], w=[gg])
                    stage(13)
                    S.op("pool", lambda: nc.gpsimd.tensor_copy(ggp[:, :, 0:64], gg[:]), r=[gg], w=[ggp])
                    for ti, tri in enumerate(("triF", "triB", "triBs", "triFs")):
                        pc = nps()
                        for b in range(4):
                            S.op("pe", lambda b=b, tri=tri, pc=pc: nc.tensor.matmul(pc[:, b * 128:(b + 1) * 128], C(tri), ggp[:, b, :], start=True, stop=True), r=[cst, ggp], w=[pc])
                        pcv = pc[:, :].rearrange("p (b n) -> p b n", b=4)
                        dst = gco if ti < 2 else gxo
                        cs_ = slice(0, 32) if ti in (0, 2) else slice(32, 64)
                        if ti % 2 == 0:
                            S.op("act", lambda pcv=pcv, dst=dst, cs_=cs_: nc.scalar.copy(dst[:, :, cs_], pcv[:, :, cs_]), r=[pc], w=[dst])
                        else:
                            S.op("dve", lambda pcv=pcv, dst=dst, cs_=cs_: nc.vector.tensor_copy(dst[:, :, cs_], pcv[:, :, cs_]), r=[pc], w=[dst])
                    stage(14)
                    for d, tri in enumerate(("triF", "triB")):
                        pq = nps()
                        for b in range(4):
                            S.op("pe", lambda b=b, tri=tri, pq=pq: nc.tensor.matmul(pq[:, b * 128:(b + 1) * 128], ggp[:, b, :], C(tri), start=True, stop=True),
                                 r=[cst, ggp], w=[pq])
                        if d == 0:
                            S.op("act", lambda pq=pq: nc.scalar.copy(gto[0:32, :], pq[0:32, :]), r=[pq], w=[gto])
                        else:
                            S.op("dve", lambda pq=pq: nc.vector.tensor_copy(gto[32:64, :], pq[32:64, :]), r=[pq], w=[gto])
                    stage(12)
                    with nc.allow_non_contiguous_dma("gate tables"):
                        S.dma("sp", GC_d[t0:t0 + 512, :].rearrange("(b p) n -> p b n", p=128), gco[:], r=[gco], w=dbt("GC", t0, t0 + 512))
                        S.dma("sp", GX_d[t0:t0 + 512, :].rearrange("(b p) n -> p b n", p=128), gxo[:], r=[gxo], w=dbt("GX", t0, t0 + 512))
                        S.dma("sp", BT_d[t0:t0 + 512, :].rearrange("(b p) n -> p b n", p=128), gbt[:], r=[gbt], w=dbt("BT", t0, t0 + 512))
                    S.dma("sp", gcT_d[:, t0:t0 + 512], gto[:], r=[gto], w=dbt("gcT", t0, t0 + 512))
            S.barrier()
            stage(1)
            with ExitStack() as st:
                cw = sbuf(st, "cw", [128, 64, 5], F32)
                with nc.allow_non_contiguous_dma("conv weights"):
                    for j in range(5):
                        S.dma("sp", cw[:, :, j], I["gdn_conv"][j, :].rearrange("(c p) -> p c", p=128), w=[cw])
                win = [sbuf(st, f"win{i}", [128, 516], F32) for i in range(2)]
                acc = [sbuf(st, f"cacc{i}", [128, 512], F32) for i in range(2)]
                sq = [sbuf(st, f"csq{i}", [128, 512], F32) for i in range(2)]
                rn = [sbuf(st, f"crn{i}", [128, 512], F32) for i in range(2)]
                ob = [sbuf(st, f"cob{i}", [128, 512], BF16) for i in range(2)]
                tk = [sbuf(st, f"ctk{i}", [128, 4, 128], BF16) for i in range(2)]
                it = 0
                for tt in range(NT):
                    t0 = tt * 512
                    for fc in range(64):
                        w_ = win[it % 2]
                        a_ = acc[it % 2]
                        s_ = sq[it % 2]
                        r_ = rn[it % 2]
                        o_ = ob[it % 2]
                        k_ = tk[it % 2]
                        it += 1
                        lo = max(t0 - 2, 0)
                        hi = min(t0 + 514, NTOK)
                        if lo > t0 - 2:
                            S.op("pool", lambda w_=w_: nc.gpsimd.memset(w_[:, 0:2], 0.0), w=[w_])
                        if hi < t0 + 514:
                            S.op("pool", lambda w_=w_: nc.gpsimd.memset(w_[:, 514:516], 0.0), w=[w_])
                        S.dma("pool", w_[:, lo - (t0 - 2):hi - (t0 - 2)], raw[fc * 128:(fc + 1) * 128, lo:hi], r=dbt("raw", lo, hi), w=[w_])
                        if t0 % L == 0 and t0 > 0:
                            S.op("dve", lambda w_=w_: nc.vector.tensor_scalar(out=w_[:, 0:2], in0=w_[:, 0:2], scalar1=flag[:, 0:1], scalar2=None, op0=ALU.mult), r=[w_, flag], w=[w_])
                        if (t0 + 512) % L == 0 and t0 + 512 < NTOK:
                            S.op("dve", lambda w_=w_: nc.vector.tensor_scalar(out=w_[:, 514:516], in0=w_[:, 514:516], scalar1=flag[:, 0:1], scalar2=None, op0=ALU.mult), r=[w_, flag], w=[w_])
                        S.op("dve", lambda w_=w_, a_=a_, fc=fc: nc.vector.tensor_scalar(out=a_[:], in0=w_[:, 0:512], scalar1=cw[:, fc, 0:1], scalar2=None, op0=ALU.mult), r=[w_, cw], w=[a_])
                        for j in range(1, 5):
                            S.op("dve", lambda w_=w_, a_=a_, fc=fc, j=j: nc.vector.scalar_tensor_tensor(out=a_[:], in0=w_[:, j:j + 512], scalar=cw[:, fc, j:j + 1], in1=a_[:],
                                                                                                   op0=ALU.mult, op1=ALU.add), r=[w_, cw, a_], w=[a_])
                        S.op("act", lambda a_=a_: nc.scalar.activation(out=a_[:], in_=a_[:], func=AF.Silu), r=[a_], w=[a_])
                        if fc < 32:
                            S.op("act", lambda a_=a_, s_=s_: nc.scalar.activation(out=s_[:], in_=a_[:], func=AF.Square), r=[a_], w=[s_])
                            ps = nps()
                            S.op("pe", lambda ps=ps, s_=s_: nc.tensor.matmul(ps[:, :], C("ones"), s_[:], start=True, stop=True), r=[cst, s_], w=[ps])
                            S.op("act", lambda ps=ps, r_=r_: nc.scalar.activation(out=r_[:], in_=ps[:, :], func=AF.Ln, bias=1e-6), r=[ps], w=[r_])
                            S.op("act", lambda r_=r_: nc.scalar.activation(out=r_[:], in_=r_[:], func=AF.Exp, scale=-0.5), r=[r_], w=[r_])
                            sc = (128.0 ** -0.5) if fc < 16 else 1.0
                            S.op("dve", lambda a_=a_, r_=r_, o_=o_, sc=sc: nc.vector.scalar_tensor_tensor(out=o_[:], in0=a_[:], scalar=sc, in1=r_[:], op0=ALU.mult, op1=ALU.mult),
                                 r=[a_, r_], w=[o_])
                            dst = qT_d if fc < 16 else kT_d
                            nm = "qT" if fc < 16 else "kT"
                            S.dma("sp", dst[(fc % 16) * 128:(fc % 16 + 1) * 128, t0:t0 + 512], o_[:], r=[o_], w=dbt(nm, t0, t0 + 512))
                        else:
                            S.op("dve", lambda a_=a_, o_=o_: nc.vector.tensor_copy(o_[:], a_[:]), r=[a_], w=[o_])
                        if fc >= 16:
                            ph = nph()
                            for b in range(4):
                                S.op("pe", lambda ph=ph, b=b, o_=o_: nc.tensor.transpose(ph[:, b * 128:(b + 1) * 128], o_[:, b * 128:(b + 1) * 128], Cb("ident")),
                                     r=[o_, cstb], w=[ph])
                            S.op("act", lambda ph=ph, k_=k_: nc.scalar.copy(k_[:], ph[:, :].rearrange("p (b n) -> p b n", b=4)), r=[ph], w=[k_])
                            if fc < 32:
                                S.dma("sp", ktok_d[t0:t0 + 512, (fc - 16) * 128:(fc - 15) * 128].rearrange("(b p) n -> p b n", p=128), k_[:], r=[k_], w=dbt("ktok", t0, t0 + 512))
                            else:
                                S.dma("sp", vtok_d[t0:t0 + 512, (fc - 32) * 128:(fc - 31) * 128].rearrange("(b p) n -> p b n", p=128), k_[:], r=[k_], w=dbt("vtok", t0, t0 + 512))
            S.barrier()
            stage(2)
            gdn_mixer()
            stage(4)
            layer_post(li, ogT_d, 4096, I["gdn_w_out"], "ogT", xin_name, xin_ap, last=(li == n_layers - 1))

        def gdn_mixer():
            LM = min(L, 1024)
            NBM = LM // 128
            TPM = LM // 512
            NSUB = NTOK // LM
            for d in range(2):
                with ExitStack() as st:
                    qTs = sbuf(st, "mq", [128, 2, LM], BF16)
                    kTs = sbuf(st, "mk", [128, 2, LM], BF16)
                    kts = sbuf(st, "mkt", [128, NBM, 2, 128], BF16)
                    vts = sbuf(st, "mvt", [128, NBM, 4, 128], BF16)
                    grow = sbuf(st, "mgrow", [128, 4, LM], F32)
                    egrow = sbuf(st, "megrow", [128, 4, LM], F32)
                    qd = sbuf(st, "mqd", [128, 4, LM], BF16)
                    gcc = sbuf(st, "mgcc", [128, NBM, 4], F32)
                    gxc = sbuf(st, "mgxc", [128, NBM, 4], F32)
                    btc = sbuf(st, "mbtc", [128, NBM, 4], F32)
                    nbt = sbuf(st, "mnbt", [128, NBM, 4], F32)
                    egc = sbuf(st, "megc", [128, NBM, 4], F32)
                    ekd = sbuf(st, "mekd", [128, NBM, 4], F32)
                    oacc = sbuf(st, "moacc", [128, 4, LM], F32)
                    Sst = sbuf(st, "mS", [128, 32, 128], F32)
                    Sb = sbuf(st, "mSb", [128, 4, 128], BF16)
                    I4 = sbuf(st, "mI4", [128, 4, 128], BF16)
                    for u in range(4):
                        S.op("dve", lambda u=u: nc.vector.tensor_copy(I4[:, u, :], C("ident")), r=[cst], w=[I4])
                    S.op("pool", lambda: nc.gpsimd.memset(Sst[:], 0.0), w=[Sst])
                    mk_s = "triFs" if d == 0 else "triBs"
                    mk_i = "triF" if d == 0 else "triB"
                    kk0 = sbuf(st, "mkk0", [128, 4, 128], F32)
                    qkm = sbuf(st, "mqkm", [128, 4, 128], F32)
                    ex4 = sbuf(st, "mex4", [128, 4, 128], F32)
                    DT4 = sbuf(st, "mDT4", [128, 4, 128], F32)
                    AT4 = sbuf(st, "mAT4", [128, 4, 128], BF16)
                    A4 = sbuf(st, "mA4", [128, 4, 128], BF16)
                    Aqk = [sbuf(st, f"mAqk{i}", [128, 4, 128], BF16) for i in range(2)]
                    XT = [sbuf(st, f"mXT{i}", [128, 4, 128], BF16) for i in range(2)]
                    P4 = [sbuf(st, f"mP{i}", [128, 4, 128], BF16) for i in range(2)]
                    PT4 = [sbuf(st, f"mPT{i}", [128, 4, 128], BF16) for i in range(2)]
                    kg = sbuf(st, "mkg", [128, 4, 128], BF16)
                    kdec = [sbuf(st, f"mkdec{i}", [128, 4, 128], BF16) for i in range(2)]
                    wT = [sbuf(st, f"mwT{i}", [128, 4, 128], BF16) for i in range(2)]
                    ub = [sbuf(st, f"mub{i}", [128, 4, 128], F32) for i in range(2)]
                    vn = sbuf(st, "mvn", [128, 4, 128], BF16)
                    sqn = sbuf(st, "msqn", [128, 512], F32)
                    rnn = sbuf(st, "mrnn", [128, 512], F32)
                    zt = sbuf(st, "mzt", [128, 4, 512], BF16)
                    ogo = sbuf(st, "mogo", [128, 4, 512], BF16)
                    nw = sbuf(st, "mnw", [128, 1], F32)
                    S.dma("sp", nw[:], I["gdn_norm"][:, :], w=[nw])
                    segs = list(range(NSUB)) if d == 0 else list(range(NSUB - 1, -1, -1))
                    for si, seg in enumerate(segs):
                        s0 = seg * LM
                        if si > 0 and ((s0 % L == 0) if d == 0 else ((s0 + LM) % L == 0)):
                            S.op("dve", lambda: nc.vector.tensor_scalar(out=Sst[:], in0=Sst[:], scalar1=flag[:, 0:1], scalar2=None, op0=ALU.mult), r=[Sst, flag], w=[Sst])
                        for hg in range(8):
                            rd = lambda nm: dbt(nm, s0, s0 + LM)
                            S.dma("sp", qTs[:], qT_d[hg * 256:(hg + 1) * 256, s0:s0 + LM].rearrange("(h p) t -> p h t", p=128), r=rd("qT"), w=[qTs])
                            S.dma("sp", kTs[:], kT_d[hg * 256:(hg + 1) * 256, s0:s0 + LM].rearrange("(h p) t -> p h t", p=128), r=rd("kT"), w=[kTs])
                            S.dma("sp", kts[:], ktok_d[s0:s0 + LM, hg * 256:(hg + 1) * 256].rearrange("(b p) (h n) -> p b h n", p=128, h=2), r=rd("ktok"), w=[kts])
                            S.dma("sp", vts[:], vtok_d[s0:s0 + LM, hg * 512:(hg + 1) * 512].rearrange("(b p) (h n) -> p b h n", p=128, h=4), r=rd("vtok"), w=[vts])
                            c0 = d * 32 + hg * 4
                            with nc.allow_non_contiguous_dma("gate cols"):
                                S.dma("sp", gcc[:], GC_d[s0:s0 + LM, c0:c0 + 4].rearrange("(b p) n -> p b n", p=128), r=rd("GC"), w=[gcc])
                                S.dma("sp", gxc[:], GX_d[s0:s0 + LM, c0:c0 + 4].rearrange("(b p) n -> p b n", p=128), r=rd("GX"), w=[gxc])
                                S.dma("sp", btc[:], BT_d[s0:s0 + LM, c0:c0 + 4].rearrange("(b p) n -> p b n", p=128), r=rd("BT"), w=[btc])
                            for u in range(4):
                                S.dma("sp", grow[:, u, :], gcT_d[c0 + u:c0 + u + 1, s0:s0 + LM].partition_broadcast(128), r=rd("gcT"), w=[grow])
                            if d == 1:
                                S.dma("sp", oacc[:], oT_d[hg * 512:(hg + 1) * 512, s0:s0 + LM].rearrange("(h p) t -> p h t", p=128), r=rd("oT"), w=[oacc])
                            S.op("act", lambda: nc.scalar.activation(out=egrow[:], in_=grow[:], func=AF.Exp), r=[grow], w=[egrow])
                            S.op("act", lambda: nc.scalar.activation(out=egc[:], in_=gcc[:], func=AF.Exp), r=[gcc], w=[egc])
                            S.op("act", lambda: nc.scalar.activation(out=ekd[:], in_=gxc[:], func=AF.Exp), r=[gxc], w=[ekd])
                            S.op("dve", lambda: nc.vector.tensor_scalar(out=nbt[:], in0=btc[:], scalar1=-1.0, scalar2=None, op0=ALU.mult), r=[btc], w=[nbt])
                            for u in range(4):
                                S.op("pool" if u % 2 else "dve",
                                     (lambda u=u: nc.gpsimd.tensor_tensor(out=qd[:, u, :], in0=qTs[:, u // 2, :], in1=egrow[:, u, :], op=ALU.mult)) if u % 2 else
                                     (lambda u=u: nc.vector.tensor_tensor(out=qd[:, u, :], in0=qTs[:, u // 2, :], in1=egrow[:, u, :], op=ALU.mult)),
                                     r=[qTs, egrow], w=[qd])
                            S.op("act", lambda hg=hg: nc.scalar.copy(Sb[:], Sst[:, hg * 4:(hg + 1) * 4, :]), r=[Sst], w=[Sb])
                            chunks = list(range(NBM)) if d == 0 else list(range(NBM - 1, -1, -1))
                            for ci, c in enumerate(chunks):
                                cs = slice(c * 128, (c + 1) * 128)
                                par = ci % 2
                                pk = nps()
                                pq = nps()
                                for u in range(4):
                                    h = u // 2
                                    S.op("pe", lambda u=u, h=h, pk=pk: nc.tensor.matmul(pk[:, u * 128:(u + 1) * 128], kTs[:, h, cs], kTs[:, h, cs], start=True, stop=True), r=[kTs], w=[pk])
                                    S.op("pe", lambda u=u, h=h, pq=pq: nc.tensor.matmul(pq[:, u * 128:(u + 1) * 128], kTs[:, h, cs], qTs[:, h, cs], start=True, stop=True), r=[kTs, qTs], w=[pq])
                                pkv = pk[:, :].rearrange("p (u n) -> p u n", u=4)
                                pqv = pq[:, :].rearrange("p (u n) -> p u n", u=4)
                                for u in range(4):
                                    S.op("dve", lambda u=u, pkv=pkv: nc.vector.tensor_tensor(out=kk0[:, u, :], in0=pkv[:, u, :], in1=C(mk_s), op=ALU.mult), r=[pk, cst], w=[kk0])
                                    S.op("dve", lambda u=u, pqv=pqv: nc.vector.tensor_tensor(out=qkm[:, u, :], in0=pqv[:, u, :], in1=C(mk_i), op=ALU.mult), r=[pq, cst], w=[qkm])
                                    S.op("dve", lambda u=u: nc.vector.tensor_scalar(out=ex4[:, u, :], in0=grow[:, u, cs], scalar1=gcc[:, c, u:u + 1], scalar2=0.0,
                                                                                    op0=ALU.subtract, op1=ALU.min), r=[grow, gcc], w=[ex4])
                                S.op("act", lambda: nc.scalar.activation(out=DT4[:], in_=ex4[:], func=AF.Exp), r=[ex4], w=[DT4])
                                aq = Aqk[par]
                                for u in range(4):
                                    S.op("dve", lambda u=u: nc.vector.scalar_tensor_tensor(out=AT4[:, u, :], in0=kk0[:, u, :], scalar=btc[:, c, u:u + 1], in1=DT4[:, u, :],
                                                                                           op0=ALU.mult, op1=ALU.mult), r=[kk0, btc, DT4], w=[AT4])
                                S.op("pool", lambda aq=aq: nc.gpsimd.tensor_tensor(out=aq[:], in0=qkm[:], in1=DT4[:], op=ALU.mult), r=[qkm, DT4], w=[aq])
                                ph = nph()
                                for u in range(4):
                                    S.op("pe", lambda u=u, ph=ph: nc.tensor.transpose(ph[:, u * 128:(u + 1) * 128], AT4[:, u, :], Cb("ident")), r=[AT4, cstb], w=[ph])
                                S.op("act", lambda ph=ph: nc.scalar.copy(A4[:], ph[:, :].rearrange("p (u n) -> p u n", u=4)), r=[ph], w=[A4])
                                xt = XT[0]
                                S.op("dve", lambda xt=xt: nc.vector.tensor_tensor(out=xt[:], in0=I4[:], in1=AT4[:], op=ALU.subtract), r=[I4, AT4], w=[xt])
                                Pc, PTc = A4, AT4
                                xi = 0
                                for lev in range(6):
                                    Pn = P4[lev % 2]
                                    PTn = PT4[lev % 2]
                                    pp = nps()
                                    for u in range(4):
                                        S.op("pe", lambda u=u, pp=pp, Pc=Pc, PTc=PTc: nc.tensor.matmul(pp[:, u * 128:(u + 1) * 128], PTc[:, u, :], Pc[:, u, :], start=True, stop=True),
                                             r=[Pc, PTc], w=[pp])
                                    S.op("act", lambda pp=pp, Pn=Pn: nc.scalar.copy(Pn[:], pp[:, :].rearrange("p (u n) -> p u n", u=4)), r=[pp], w=[Pn])
                                    if lev < 5:
                                        pp2 = nps()
                                        for u in range(4):
                                            S.op("pe", lambda u=u, pp2=pp2, Pc=Pc, PTc=PTc: nc.tensor.matmul(pp2[:, u * 128:(u + 1) * 128], Pc[:, u, :], PTc[:, u, :], start=True, stop=True),
                                                 r=[Pc, PTc], w=[pp2])
                                        S.op("act", lambda pp2=pp2, PTn=PTn: nc.scalar.copy(PTn[:], pp2[:, :].rearrange("p (u n) -> p u n", u=4)), r=[pp2], w=[PTn])
                                    px = nps()
                                    xo_ = XT[xi % 2]
                                    xn_ = XT[(xi + 1) % 2]
                                    for u in range(4):
                                        S.op("pe", lambda u=u, px=px, Pn=Pn, xo_=xo_: nc.tensor.matmul(px[:, u * 128:(u + 1) * 128], Pn[:, u, :], xo_[:, u, :], start=True, stop=True),
                                             r=[Pn, xo_], w=[px])
                                    S.op("dve", lambda px=px, xo_=xo_, xn_=xn_: nc.vector.tensor_tensor(out=xn_[:], in0=px[:, :].rearrange("p (u n) -> p u n", u=4), in1=xo_[:], op=ALU.add),
                                         r=[px, xo_], w=[xn_])
                                    xi += 1
                                    Pc, PTc = Pn, PTn
                                xt = XT[xi % 2]
                                kd = kdec[par]
                                for u in range(4):
                                    S.op("pool", lambda u=u: nc.gpsimd.tensor_scalar(out=kg[:, u, :], in0=kts[:, c, u // 2, :], scalar1=egc[:, c, u:u + 1], scalar2=None, op0=ALU.mult),
                                         r=[kts, egc], w=[kg])
                                    S.op("pool", lambda u=u, kd=kd: nc.gpsimd.tensor_scalar(out=kd[:, u, :], in0=kts[:, c, u // 2, :], scalar1=ekd[:, c, u:u + 1], scalar2=None, op0=ALU.mult),
                                         r=[kts, ekd], w=[kd])
                                pw = nps()
                                pu = nps()
                                for u in range(4):
                                    S.op("pe", lambda u=u, pw=pw, xt=xt: nc.tensor.matmul(pw[:, u * 128:(u + 1) * 128], kg[:, u, :], xt[:, u, :], start=True, stop=True), r=[kg, xt], w=[pw])
                                    S.op("pe", lambda u=u, pu=pu, xt=xt: nc.tensor.matmul(pu[:, u * 128:(u + 1) * 128], xt[:, u, :], vts[:, c, u, :], start=True, stop=True), r=[xt, vts], w=[pu])
                                w_ = wT[par]
                                u_ = ub[par]
                                S.op("act", lambda pw=pw, w_=w_: nc.scalar.copy(w_[:], pw[:, :].rearrange("p (u n) -> p u n", u=4)), r=[pw], w=[w_])
                                for u in range(4):
                                    S.op("act", lambda u=u, pu=pu, u_=u_: nc.scalar.activation(out=u_[:, u, :], in_=pu[:, u * 128:(u + 1) * 128], func=AF.Copy, scale=btc[:, c, u:u + 1]),
                                         r=[pu, btc], w=[u_])
                                pv = nps()
                                for u in range(4):
                                    S.op("pe", lambda u=u, pv=pv, w_=w_: nc.tensor.matmul(pv[:, u * 128:(u + 1) * 128], w_[:, u, :], Sb[:, u, :], start=True, stop=True), r=[w_, Sb], w=[pv])
                                for u in range(4):
                                    S.op("dve", lambda u=u, pv=pv, u_=u_: nc.vector.scalar_tensor_tensor(out=vn[:, u, :], in0=pv[:, u * 128:(u + 1) * 128], scalar=nbt[:, c, u:u + 1], in1=u_[:, u, :],
                                                                                                  op0=ALU.mult, op1=ALU.add), r=[pv, nbt, u_], w=[vn])
                                po = nps()
                                pS = nps()
                                for u in range(4):
                                    S.op("pe", lambda u=u, po=po: nc.tensor.matmul(po[:, u * 128:(u + 1) * 128], Sb[:, u, :], qd[:, u, cs], start=True, stop=False), r=[Sb, qd], w=[po])
                                    S.op("pe", lambda u=u, po=po, aq=aq: nc.tensor.matmul(po[:, u * 128:(u + 1) * 128], vn[:, u, :], aq[:, u, :], start=False, stop=True), r=[vn, aq], w=[po])
                                    S.op("pe", lambda u=u, pS=pS, kd=kd: nc.tensor.matmul(pS[:, u * 128:(u + 1) * 128], kd[:, u, :], vn[:, u, :], start=True, stop=True), r=[kd, vn], w=[pS])
                                pov = po[:, :].rearrange("p (u n) -> p u n", u=4)
                                if d == 0:
                                    S.op("act", lambda pov=pov: nc.scalar.copy(oacc[:, :, cs], pov), r=[po], w=[oacc])
                                else:
                                    S.op("dve", lambda pov=pov: nc.vector.tensor_tensor(out=oacc[:, :, cs], in0=oacc[:, :, cs], in1=pov, op=ALU.add), r=[po, oacc], w=[oacc])
                                gl = c * 128 + (127 if d == 0 else 0)
                                for u in range(4):
                                    S.op("dve", lambda u=u, pS=pS, hg=hg, gl=gl: nc.vector.scalar_tensor_tensor(out=Sst[:, hg * 4 + u, :], in0=Sst[:, hg * 4 + u, :], scalar=egrow[:, u, gl:gl + 1],
                                                                                                         in1=pS[:, u * 128:(u + 1) * 128], op0=ALU.mult, op1=ALU.add), r=[Sst, egrow, pS], w=[Sst])
                                S.op("act", lambda hg=hg: nc.scalar.copy(Sb[:], Sst[:, hg * 4:(hg + 1) * 4, :]), r=[Sst], w=[Sb])
                            if d == 0:
                                S.dma("sp", oT_d[hg * 512:(hg + 1) * 512, s0:s0 + LM].rearrange("(h p) t -> p h t", p=128), oacc[:], r=[oacc], w=dbt("oT", s0, s0 + LM))
                            else:
                                for tq in range(TPM):
                                    ts_ = slice(tq * 512, (tq + 1) * 512)
                                    S.dma("sp", zt[:], zs_d[hg * 512:(hg + 1) * 512, s0 + tq * 512:s0 + (tq + 1) * 512].rearrange("(h p) t -> p h t", p=128),
                                          r=dbt("zs", s0 + tq * 512, s0 + tq * 512 + 512), w=[zt])
                                    for u in range(4):
                                        S.op("act", lambda u=u: nc.scalar.activation(out=sqn[:], in_=oacc[:, u, ts_], func=AF.Square), r=[oacc], w=[sqn])
                                        ps = nps()
                                        S.op("pe", lambda ps=ps: nc.tensor.matmul(ps[:, :], C("ones"), sqn[:], start=True, stop=True), r=[cst, sqn], w=[ps])
                                        S.op("act", lambda ps=ps: nc.scalar.activation(out=rnn[:], in_=ps[:, :], func=AF.Ln, scale=1.0 / 128, bias=1e-6), r=[ps], w=[rnn])
                                        S.op("act", lambda: nc.scalar.activation(out=rnn[:], in_=rnn[:], func=AF.Exp, scale=-0.5), r=[rnn], w=[rnn])
                                        S.op("dve", lambda u=u: nc.vector.scalar_tensor_tensor(out=sqn[:], in0=oacc[:, u, ts_], scalar=nw[:, 0:1], in1=rnn[:], op0=ALU.mult, op1=ALU.mult),
                                             r=[oacc, nw, rnn], w=[sqn])
                                        S.op("pool", lambda u=u: nc.gpsimd.tensor_tensor(out=ogo[:, u, :], in0=sqn[:], in1=zt[:, u, :], op=ALU.mult), r=[sqn, zt], w=[ogo])
                                    S.dma("sp", ogT_d[hg * 512:(hg + 1) * 512, s0 + tq * 512:s0 + (tq + 1) * 512].rearrange("(h p) t -> p h t", p=128), ogo[:], r=[ogo],
                                          w=dbt("ogT", s0 + tq * 512, s0 + tq * 512 + 512))
                S.barrier()
                stage(3)

        def rwkv_layer(li, xin_name, xin_ap):
            NB = NTOK // 128
            rT_d = scr("r_r", [D, NTOK], F32)
            kT2_d = scr("r_k", [D, NTOK], F32)
            vT_d = scr("r_v", [D, NTOK], F32)
            lw_d = [scr(f"r_lw{d}", [D, NTOK], F32) for d in range(2)]
            a_d = [scr(f"r_a{d}", [D, NTOK], F32) for d in range(2)]
            g_d = scr("r_g", [D, NTOK], BF16)
            rt_d = [scr(f"r_rt{d}", [D, NTOK], BF16) for d in range(2)]
            bt_d = [scr(f"r_bt{d}", [D, NTOK], BF16) for d in range(2)]
            kt_d = [scr(f"r_kt{d}", [D, NTOK], BF16) for d in range(2)]
            at_d = [scr(f"r_at{d}", [D, NTOK], BF16) for d in range(2)]
            bhk_d = [scr(f"r_bhk{d}", [NTOK, D], BF16) for d in range(2)]
            khk_d = [scr(f"r_khk{d}", [NTOK, D], BF16) for d in range(2)]
            vtk_d = scr("r_vtk", [NTOK, D], BF16)
            WC_d = [scr(f"r_wc{d}", [D, NB], F32) for d in range(2)]
            bonus_d = scr("r_bonus", [D, NTOK], F32)
            ytok_d = scr("r_ytok", [NTOK, D], F32)
            ym_d = scr("r_ym", [D, NTOK], BF16)
            with ExitStack() as stl:
                w0c = [colvec(f"w0c{d}", I["rwkv_w0"][d, :], 16) for d in range(2)]
                a0c = [colvec(f"a0c{d}", I["rwkv_a0"][d, :], 16) for d in range(2)]
                kkc = colvec("kkc", I["rwkv_k_k"][0, :], 16)
                kac = colvec("kac", I["rwkv_k_a"][0, :], 16)
                rkc = colvec("rkc", I["rwkv_r_k"][0, :], 16)
                lwc = colvec("lwc", I["rwkv_ln_w"][0, :], 16)
                lbc = colvec("lbc", I["rwkv_ln_b"][0, :], 16)
                mixc = [colvec(f"mixc{m}", I["rwkv_mix"][m, :], 16) for m in range(6)]
                omka = sbuf(es, "omka", [128, 16], F32)
                S.op("dve", lambda: nc.vector.tensor_scalar(out=omka[:], in0=kac[:], scalar1=-1.0, scalar2=1.0, op0=ALU.mult, op1=ALU.add), r=[kac], w=[omka])
                with ExitStack() as st:
                    xh = sbuf(st, "rxh", [128, 16, 514], F32)
                    xx = sbuf(st, "rxx", [128, 16, 512], F32)
                    xm = [sbuf(st, f"rxm{i}", [128, 16, 512], BF16) for i in range(2)]
                    hl = [sbuf(st, f"rhl{i}", [128, 2, 512], BF16) for i in range(2)]
                    evf = [sbuf(st, f"revf{i}", [128, 512], F32) for i in range(3)]
                    evb = [sbuf(st, f"revb{i}", [128, 512], BF16) for i in range(3)]
                    cnt = [0]
                    for tt in range(NT):
                        t0 = tt * 512
                        lo = max(t0 - 1, 0)
                        hi = min(t0 + 513, NTOK)
                        if lo > t0 - 1:
                            S.op("pool", lambda: nc.gpsimd.memset(xh[:, :, 0:1], 0.0), w=[xh])
                        if hi < t0 + 513:
                            S.op("pool", lambda: nc.gpsimd.memset(xh[:, :, 513:514], 0.0), w=[xh])
                        S.dma("sp", xh[:, :, lo - (t0 - 1):hi - (t0 - 1)], xin_ap[:, lo:hi].rearrange("(c p) t -> p c t", p=128), r=dbt(xin_name, lo, hi), w=[xh])
                        if t0 % L == 0 and t0 > 0:
                            S.op("dve", lambda: nc.vector.tensor_scalar(out=xh[:, :, 0:1], in0=xh[:, :, 0:1], scalar1=flag[:, 0:1], scalar2=None, op0=ALU.mult), r=[xh, flag], w=[xh])
                        if (t0 + 512) % L == 0 and t0 + 512 < NTOK:
                            S.op("dve", lambda: nc.vector.tensor_scalar(out=xh[:, :, 513:514], in0=xh[:, :, 513:514], scalar1=flag[:, 0:1], scalar2=None, op0=ALU.mult), r=[xh, flag], w=[xh])
                        S.op("pool", lambda: nc.gpsimd.tensor_tensor(out=xx[:], in0=xh[:, :, 0:512], in1=xh[:, :, 2:514], op=ALU.add), r=[xh], w=[xx])
                        S.op("dve", lambda: nc.vector.scalar_tensor_tensor(out=xx[:], in0=xx[:], scalar=0.5, in1=xh[:, :, 1:513], op0=ALU.mult, op1=ALU.subtract), r=[xx, xh], w=[xx])

                        def mixed(m):
                            b = xm[m % 2]
                            for kc in range(16):
                                S.op("dve", lambda kc=kc, b=b: nc.vector.scalar_tensor_tensor(out=b[:, kc, :], in0=xx[:, kc, :], scalar=mixc[m][:, kc:kc + 1], in1=xh[:, kc, 1:513],
                                                                                          op0=ALU.mult, op1=ALU.add), r=[xx, xh, mixc[m]], w=[b])
                            return b

                        def ev_store(dst, nm, func=None, dtype_f32=True, bias=None, scale=1.0, post=None):
                            def ev(n, ps):
                                i = cnt[0] % 3
                                cnt[0] += 1
                                e = evf[i] if dtype_f32 else evb[i]
                                if func is None:
                                    S.op("act", lambda: nc.scalar.copy(e[:], ps[:, :]), r=[ps], w=[e])
                                elif bias is None:
                                    S.op("act", lambda: nc.scalar.activation(out=e[:], in_=ps[:, :], func=func), r=[ps], w=[e])
                                else:
                                    S.op("act", lambda: nc.scalar.activation(out=e[:], in_=ps[:, :], func=func, bias=bias[:, n:n + 1]), r=[ps, bias], w=[e])
                                if post is not None:
                                    S.op("dve", lambda: nc.vector.tensor_scalar(out=e[:], in0=e[:], scalar1=post, scalar2=None, op0=ALU.mult), r=[e], w=[e])
                                S.dma("sp", dst[n * 128:(n + 1) * 128, t0:t0 + 512], e[:], r=[e], w=dbt(nm, t0, t0 + 512))
                            return ev

                        b = mixed(0)
                        linear(I["rwkv_w_rkv"][0], D, D, lambda kc, b=b: b[:, kc, :], [b], ev_store(rT_d, "r_r"))
                        b = mixed(1)
                        for d in range(2):
                            h = hl[d]

                            def evh(n, ps, h=h):
                                S.op("act", lambda: nc.scalar.activation(out=h[0:96, 0, :], in_=ps[0:96, :], func=AF.Tanh), r=[ps], w=[h])
                            linear(I["rwkv_w1"][d], D, 96, lambda kc, b=b: b[:, kc, :], [b], evh)
                            linear(I["rwkv_w2"][d], 96, D, lambda kc, h=h: h[0:96, 0, :], [h], ev_store(lw_d[d], f"r_lw{d}", func=AF.Sigmoid, bias=w0c[d], post=-0.6065306597126334))
                        b = mixed(2)
                        linear(I["rwkv_w_rkv"][1], D, D, lambda kc, b=b: b[:, kc, :], [b], ev_store(kT2_d, "r_k"))
                        b = mixed(3)
                        linear(I["rwkv_w_rkv"][2], D, D, lambda kc, b=b: b[:, kc, :], [b], ev_store(vT_d, "r_v"))
                        b = mixed(4)
                        for d in range(2):
                            h = hl[d]

                            def evh2(n, ps, h=h):
                                S.op("act", lambda: nc.scalar.copy(h[0:96, 0, :], ps[0:96, :]), r=[ps], w=[h])
                            linear(I["rwkv_a1"][d], D, 96, lambda kc, b=b: b[:, kc, :], [b], evh2)
                            linear(I["rwkv_a2"][d], 96, D, lambda kc, h=h: h[0:96, 0, :], [h], ev_store(a_d[d], f"r_a{d}", func=AF.Sigmoid, bias=a0c[d]))
                        b = mixed(5)
                        h = hl[0]

                        def evg(n, ps, h=h):
                            S.op("act", lambda: nc.scalar.activation(out=h[:, n, :], in_=ps[:, :], func=AF.Sigmoid), r=[ps], w=[h])
                        linear(I["rwkv_g1"], D, 256, lambda kc, b=b: b[:, kc, :], [b], evg)
                        linear(I["rwkv_g2"], 256, D, lambda kc, h=h: h[:, kc, :], [h], ev_store(g_d, "r_g", dtype_f32=False))
                S.barrier()
                stage(23)
                with ExitStack() as st:
                    def T(name, dt=F32, shape=(128, 512)):
                        return sbuf(st, name, list(shape), dt)
                    k_ = T("qk"); v_ = T("qv"); r_ = T("qr")
                    a_ = [T("qa0"), T("qa1")]
                    lw_ = [T("ql0"), T("ql1")]
                    t1 = T("qt1"); sq = T("qsq"); rn = T("qrn"); kk = T("qkk"); tmp = T("qtmp"); kd = T("qkd"); bp = T("qbp"); kds = T("qkds")
                    lwt = sbuf(st, "qlwt", [128, 4, 128], F32)
                    gc = T("qgc"); gcp = T("qgcp"); E1 = T("qE1"); E2 = T("qE2"); E3 = T("qE3"); E4 = T("qE4")
                    ob = [T(f"qob{i}", BF16) for i in range(6)]
                    vb = T("qvb", BF16)
                    tkb = [sbuf(st, f"qtk{i}", [128, 4, 128], BF16) for i in range(3)]
                    wcs = sbuf(st, "qwc", [128, 4], F32)
                    bon = T("qbon")
                    for tt in range(NT):
                        t0 = tt * 512
                        for fc in range(16):
                            rows = slice(fc * 128, (fc + 1) * 128)
                            S.dma("sp", k_[:], kT2_d[rows, t0:t0 + 512], r=dbt("r_k", t0, t0 + 512), w=[k_])
                            S.dma("sp", v_[:], vT_d[rows, t0:t0 + 512], r=dbt("r_v", t0, t0 + 512), w=[v_])
                            S.dma("sp", r_[:], rT_d[rows, t0:t0 + 512], r=dbt("r_r", t0, t0 + 512), w=[r_])
                            for d in range(2):
                                S.dma("sp", a_[d][:], a_d[d][rows, t0:t0 + 512], r=dbt(f"r_a{d}", t0, t0 + 512), w=[a_[d]])
                                S.dma("sp", lw_[d][:], lw_d[d][rows, t0:t0 + 512], r=dbt(f"r_lw{d}", t0, t0 + 512), w=[lw_[d]])
                            S.op("dve", lambda: nc.vector.tensor_scalar(out=t1[:], in0=k_[:], scalar1=kkc[:, fc:fc + 1], scalar2=None, op0=ALU.mult), r=[k_, kkc], w=[t1])
                            S.op("pool", lambda: nc.gpsimd.tensor_tensor(out=sq[:], in0=t1[:], in1=t1[:], op=ALU.mult), r=[t1], w=[sq])
                            ps = nps()
                            S.op("pe", lambda ps=ps: nc.tensor.matmul(ps[:, :], C("blk64"), sq[:], start=True, stop=True), r=[cst, sq], w=[ps])
                            S.op("act", lambda ps=ps: nc.scalar.activation(out=rn[:], in_=ps[:, :], func=AF.Ln, bias=1e-6), r=[ps], w=[rn])
                            S.op("act", lambda: nc.scalar.activation(out=rn[:], in_=rn[:], func=AF.Exp, scale=-0.5), r=[rn], w=[rn])
                            S.op("dve", lambda: nc.vector.tensor_tensor(out=kk[:], in0=t1[:], in1=rn[:], op=ALU.mult), r=[t1, rn], w=[kk])
                            S.op("act", lambda: nc.scalar.copy(vb[:], v_[:]), r=[v_], w=[vb])
                            ph = nph()
                            for b in range(4):
                                S.op("pe", lambda ph=ph, b=b: nc.tensor.transpose(ph[:, b * 128:(b + 1) * 128], vb[:, b * 128:(b + 1) * 128], Cb("ident")), r=[vb, cstb], w=[ph])
                            tk = tkb[0]
                            S.op("act", lambda ph=ph, tk=tk: nc.scalar.copy(tk[:], ph[:, :].rearrange("p (b n) -> p b n", b=4)), r=[ph], w=[tk])
                            S.dma("sp", vtk_d[t0:t0 + 512, rows].rearrange("(b p) n -> p b n", p=128), tk[:], r=[tk], w=dbt("r_vtk", t0, t0 + 512))
                            for d in range(2):
                                tri = "triF" if d == 0 else "triB"
                                S.op("dve", lambda d=d: nc.vector.tensor_scalar(out=tmp[:], in0=a_[d][:], scalar1=kac[:, fc:fc + 1], scalar2=omka[:, fc:fc + 1], op0=ALU.mult, op1=ALU.add),
                                     r=[a_[d], kac, omka], w=[tmp])
                                S.op("pool", lambda: nc.gpsimd.tensor_tensor(out=kd[:], in0=k_[:], in1=tmp[:], op=ALU.mult), r=[k_, tmp], w=[kd])
                                S.op("dve", lambda d=d: nc.vector.tensor_tensor(out=bp[:], in0=kk[:], in1=a_[d][:], op=ALU.mult), r=[kk, a_[d]], w=[bp])
                                if d == 0:
                                    S.op("pool", lambda: nc.gpsimd.tensor_copy(kds[:], kd[:]), r=[kd], w=[kds])
                                else:
                                    S.op("pool", lambda: nc.gpsimd.tensor_tensor(out=kds[:], in0=kds[:], in1=kd[:], op=ALU.add), r=[kds, kd], w=[kds])
                                pt = nps()
                                for b in range(4):
                                    S.op("pe", lambda pt=pt, b=b, d=d: nc.tensor.transpose(pt[:, b * 128:(b + 1) * 128], lw_[d][:, b * 128:(b + 1) * 128], C("ident")), r=[lw_[d], cst], w=[pt])
                                S.op("act", lambda pt=pt: nc.scalar.copy(lwt[:], pt[:, :].rearrange("p (b n) -> p b n", b=4)), r=[pt], w=[lwt])
                                pg = nps()
                                for b in range(4):
                                    S.op("pe", lambda pg=pg, b=b, tri=tri: nc.tensor.matmul(pg[:, b * 128:(b + 1) * 128], lwt[:, b, :], C(tri), start=True, stop=True), r=[lwt, cst], w=[pg])
                                S.op("dve", lambda pg=pg: nc.vector.tensor_copy(gc[:], pg[:, :]), r=[pg], w=[gc])
                                S.op("pool", lambda d=d: nc.gpsimd.tensor_tensor(out=gcp[:], in0=gc[:], in1=lw_[d][:], op=ALU.subtract), r=[gc, lw_[d]], w=[gcp])
                                S.op("act", lambda: nc.scalar.activation(out=E1[:], in_=gc[:], func=AF.Exp), r=[gc], w=[E1])
                                S.op("act", lambda: nc.scalar.activation(out=E2[:], in_=gc[:], func=AF.Exp, scale=-1.0), r=[gc], w=[E2])
                                S.op("act", lambda: nc.scalar.activation(out=E3[:], in_=gcp[:], func=AF.Exp), r=[gcp], w=[E3])
                                for b in range(4):
                                    gl = b * 128 + (127 if d == 0 else 0)
                                    S.op("act", lambda b=b, gl=gl: nc.scalar.activation(out=E4[:, b * 128:(b + 1) * 128], in_=gc[:, b * 128:(b + 1) * 128], func=AF.Exp, scale=-1.0, bias=gc[:, gl:gl + 1]),
                                         r=[gc], w=[E4])
                                    S.op("act", lambda b=b, gl=gl: nc.scalar.copy(wcs[:, b:b + 1], E1[:, gl:gl + 1]), r=[E1], w=[wcs])
                                with nc.allow_non_contiguous_dma("wc"):
                                    S.dma("sp", WC_d[d][rows, tt * 4:(tt + 1) * 4], wcs[:], r=[wcs], w=[db(f"r_wc{d}", tt)])
                                S.op("dve", lambda: nc.vector.tensor_tensor(out=ob[0][:], in0=r_[:], in1=E1[:], op=ALU.mult), r=[r_, E1], w=[ob[0]])
                                S.op("pool", lambda: nc.gpsimd.tensor_tensor(out=ob[1][:], in0=bp[:], in1=E2[:], op=ALU.mult), r=[bp, E2], w=[ob[1]])
                                S.op("dve", lambda: nc.vector.tensor_tensor(out=ob[2][:], in0=kd[:], in1=E2[:], op=ALU.mult), r=[kd, E2], w=[ob[2]])
                                S.op("dve", lambda: nc.vector.scalar_tensor_tensor(out=ob[3][:], in0=kk[:], scalar=-1.0, in1=E3[:], op0=ALU.mult, op1=ALU.mult), r=[kk, E3], w=[ob[3]])
                                S.op("pool", lambda: nc.gpsimd.tensor_tensor(out=ob[4][:], in0=bp[:], in1=E4[:], op=ALU.mult), r=[bp, E4], w=[ob[4]])
                                S.op("dve", lambda: nc.vector.tensor_tensor(out=ob[5][:], in0=kd[:], in1=E4[:], op=ALU.mult), r=[kd, E4], w=[ob[5]])
                                for i, (dst, nm) in enumerate(((rt_d[d], f"r_rt{d}"), (bt_d[d], f"r_bt{d}"), (kt_d[d], f"r_kt{d}"), (at_d[d], f"r_at{d}"))):
                                    S.dma("sp", dst[rows, t0:t0 + 512], ob[i][:], r=[ob[i]], w=dbt(nm, t0, t0 + 512))
                                for i, (dst, nm) in ((4, (bhk_d[d], f"r_bhk{d}")), (5, (khk_d[d], f"r_khk{d}"))):
                                    ph = nph()
                                    for b in range(4):
                                        S.op("pe", lambda ph=ph, b=b, i=i: nc.tensor.transpose(ph[:, b * 128:(b + 1) * 128], ob[i][:, b * 128:(b + 1) * 128], Cb("ident")), r=[ob[i], cstb], w=[ph])
                                    tk = tkb[i - 3]
                                    S.op("act", lambda ph=ph, tk=tk: nc.scalar.copy(tk[:], ph[:, :].rearrange("p (b n) -> p b n", b=4)), r=[ph], w=[tk])
                                    S.dma("sp", dst[t0:t0 + 512, rows].rearrange("(b p) n -> p b n", p=128), tk[:], r=[tk], w=dbt(nm, t0, t0 + 512))
                            S.op("dve", lambda: nc.vector.scalar_tensor_tensor(out=tmp[:], in0=r_[:], scalar=rkc[:, fc:fc + 1], in1=kds[:], op0=ALU.mult, op1=ALU.mult), r=[r_, rkc, kds], w=[tmp])
                            ps = nps()
                            S.op("pe", lambda ps=ps: nc.tensor.matmul(ps[:, :], C("blk64"), tmp[:], start=True, stop=True), r=[cst, tmp], w=[ps])
                            S.op("dve", lambda ps=ps: nc.vector.tensor_tensor(out=bon[:], in0=ps[:, :], in1=v_[:], op=ALU.mult), r=[ps, v_], w=[bon])
                            S.dma("sp", bonus_d[rows, t0:t0 + 512], bon[:], r=[bon], w=dbt("r_bonus", t0, t0 + 512))
                S.barrier()
                stage(26)
                LM = min(L, 1024)
                NBM = LM // 128
                NSUB = NTOK // LM
                for d in range(2):
                    with ExitStack() as st:
                        rt = sbuf(st, "srt", [128, 2, LM], BF16)
                        bt = sbuf(st, "sbt", [128, 2, LM], BF16)
                        kt = sbuf(st, "skt", [128, 2, LM], BF16)
                        at = sbuf(st, "sat", [128, 2, LM], BF16)
                        ath = [sbuf(st, f"sath{i}", [128, 2, LM], BF16) for i in range(2)]
                        rth = [sbuf(st, f"srth{i}", [128, 2, LM], BF16) for i in range(2)]
                        bhk = sbuf(st, "sbhk", [128, NBM, 256], BF16)
                        khk = sbuf(st, "skhk", [128, NBM, 256], BF16)
                        vtk = sbuf(st, "svtk", [128, NBM, 256], BF16)
                        wc = sbuf(st, "swc", [128, 2, NBM], F32)
                        oacc = sbuf(st, "soacc", [128, NBM, 256], F32)
                        Hst = sbuf(st, "sH", [128, 16, 128], F32)
                        Hb = sbuf(st, "sHb", [128, 2, 128], BF16)
                        S.op("pool", lambda: nc.gpsimd.memset(Hst[:], 0.0), w=[Hst])
                        I4 = sbuf(st, "sI4", [128, 4, 128], BF16)
                        mS = sbuf(st, "smS", [128, 4, 128], F32)
                        mSt = sbuf(st, "smSt", [128, 4, 128], F32)
                        mI = sbuf(st, "smI", [128, 4, 128], F32)
                        for u in range(4):
                            S.op("dve", lambda u=u: nc.vector.tensor_copy(I4[:, u, :], C("ident")), r=[cst], w=[I4])
                            S.op("dve", lambda u=u: nc.vector.tensor_copy(mS[:, u, :], C("triFs" if d == 0 else "triBs")), r=[cst], w=[mS])
                            S.op("dve", lambda u=u: nc.vector.tensor_copy(mSt[:, u, :], C("triBs" if d == 0 else "triFs")), r=[cst], w=[mSt])
                            S.op("dve", lambda u=u: nc.vector.tensor_copy(mI[:, u, :], C("triF" if d == 0 else "triB")), r=[cst], w=[mI])
                        NT4 = sbuf(st, "sNT4", [128, 4, 128], BF16)
                        N4 = sbuf(st, "sN4", [128, 4, 128], BF16)
                        XT = [sbuf(st, f"sXT{i}", [128, 4, 128], BF16) for i in range(2)]
                        P4 = [sbuf(st, f"sP{i}", [128, 4, 128], BF16) for i in range(2)]
                        PT4 = [sbuf(st, f"sPT{i}", [128, 4, 128], BF16) for i in range(2)]
                        evt = [sbuf(st, f"sevt{i}", [128, 4, 128], F32) for i in range(3)]
                        Aak = sbuf(st, "sAak", [128, 4, 128], BF16)
                        Arb = [sbuf(st, f"sArb{i}", [128, 4, 128], BF16) for i in range(2)]
                        Ark = [sbuf(st, f"sArk{i}", [128, 4, 128], BF16) for i in range(2)]
                        akv = [sbuf(st, f"sakv{i}", [128, 256], F32) for i in range(2)]
                        rhsu = sbuf(st, "srhsu", [128, 256], BF16)
                        ut = sbuf(st, "sut", [128, 256], BF16)

                        def v4(ps):
                            return ps[:, :].rearrange("p (u n) -> p u n", u=4)

                        subs = list(range(NSUB)) if d == 0 else list(range(NSUB - 1, -1, -1))
                        for si, sub in enumerate(subs):
                            s0 = sub * LM
                            if si > 0 and ((s0 % L == 0) if d == 0 else ((s0 + LM) % L == 0)):
                                S.op("dve", lambda: nc.vector.tensor_scalar(out=Hst[:], in0=Hst[:], scalar1=flag[:, 0:1], scalar2=None, op0=ALU.mult), r=[Hst, flag], w=[Hst])
                            for fp in range(8):
                                frows = slice(fp * 256, (fp + 1) * 256)
                                for tile_, src, nm in ((rt, rt_d[d], f"r_rt{d}"), (bt, bt_d[d], f"r_bt{d}"), (kt, kt_d[d], f"r_kt{d}"), (at, at_d[d], f"r_at{d}")):
                                    S.dma("sp", tile_[:], src[frows, s0:s0 + LM].rearrange("(f p) t -> p f t", p=128), r=dbt(nm, s0, s0 + LM), w=[tile_])
                                for tile_, src, nm in ((bhk, bhk_d[d], f"r_bhk{d}"), (khk, khk_d[d], f"r_khk{d}"), (vtk, vtk_d, "r_vtk")):
                                    S.dma("sp", tile_[:], src[s0:s0 + LM, frows].rearrange("(b p) n -> p b n", p=128), r=dbt(nm, s0, s0 + LM), w=[tile_])
                                with nc.allow_non_contiguous_dma("wc"):
                                    S.dma("sp", wc[:], WC_d[d][frows, s0 // 128:s0 // 128 + NBM].rearrange("(f p) b -> p f b", p=128),
                                          r=[db(f"r_wc{d}", i) for i in range(s0 // 512, (s0 + LM) // 512)], w=[wc])
                                if d == 1:
                                    S.dma("sp", oacc[:], ytok_d[s0:s0 + LM, frows].rearrange("(b p) n -> p b n", p=128), r=dbt("r_ytok", s0, s0 + LM), w=[oacc])
                                for h_ in range(2):
                                    oth = slice((1 - h_) * 64, (2 - h_) * 64)
                                    S.op("pool", lambda h_=h_: nc.gpsimd.tensor_copy(ath[h_][:], at[:]), r=[at], w=[ath[h_]])
                                    S.op("pool", lambda h_=h_, oth=oth: nc.gpsimd.memset(ath[h_][oth, :, :], 0.0), w=[ath[h_]])
                                    S.op("dve", lambda h_=h_: nc.vector.tensor_copy(rth[h_][:], rt[:]), r=[rt], w=[rth[h_]])
                                    S.op("dve", lambda h_=h_, oth=oth: nc.vector.memset(rth[h_][oth, :, :], 0.0), w=[rth[h_]])
                                S.op("act", lambda fp=fp: nc.scalar.copy(Hb[:], Hst[:, 2 * fp:2 * fp + 2, :]), r=[Hst], w=[Hb])
                                chunks = list(range(NBM)) if d == 0 else list(range(NBM - 1, -1, -1))
                                for ci, c in enumerate(chunks):
                                    cs = slice(c * 128, (c + 1) * 128)
                                    par = ci % 2
                                    UN = [(u // 2, u % 2, slice((u % 2) * 64, (u % 2) * 64 + 64)) for u in range(4)]
                                    stage(29)
                                    pNT = nps()
                                    pN = nps()
                                    for u, (f, h, P) in enumerate(UN):
                                        S.op("pe", lambda u=u, f=f, h=h, pNT=pNT: nc.tensor.matmul(pNT[:, u * 128:(u + 1) * 128], bt[:, f, cs], ath[h][:, f, cs], start=True, stop=True), r=[bt, ath[h]], w=[pNT])
                                        S.op("pe", lambda u=u, f=f, h=h, pN=pN: nc.tensor.matmul(pN[:, u * 128:(u + 1) * 128], ath[h][:, f, cs], bt[:, f, cs], start=True, stop=True), r=[bt, ath[h]], w=[pN])
                                    S.op("dve", lambda pNT=pNT: nc.vector.tensor_tensor(out=NT4[:], in0=v4(pNT), in1=mS[:], op=ALU.mult), r=[pNT, mS], w=[NT4])
                                    S.op("dve", lambda pN=pN: nc.vector.tensor_tensor(out=N4[:], in0=v4(pN), in1=mSt[:], op=ALU.mult), r=[pN, mSt], w=[N4])
                                    xt = XT[0]
                                    S.op("pool", lambda xt=xt: nc.gpsimd.tensor_tensor(out=xt[:], in0=I4[:], in1=NT4[:], op=ALU.add), r=[I4, NT4], w=[xt])
                                    stage(30)
                                    outs = []
                                    for j, (la, ra, msk, dst) in enumerate(((kt, ath, mS, Aak), (bt, rth, mI, Arb[par]), (kt, rth, mI, Ark[par]))):
                                        pp = nps()
                                        for u, (f, h, P) in enumerate(UN):
                                            S.op("pe", lambda u=u, f=f, h=h, pp=pp, la=la, ra=ra: nc.tensor.matmul(pp[:, u * 128:(u + 1) * 128], la[:, f, cs], ra[h][:, f, cs], start=True, stop=True),
                                                 r=[la, ra[h]], w=[pp])
                                        e = evt[j]
                                        S.op("act", lambda pp=pp, e=e: nc.scalar.copy(e[:], v4(pp)), r=[pp], w=[e])
                                        S.op("pool", lambda e=e, msk=msk, dst=dst: nc.gpsimd.tensor_tensor(out=dst[:], in0=e[:], in1=msk[:], op=ALU.mult), r=[e, msk], w=[dst])
                                    arb = Arb[par]
                                    ark = Ark[par]
                                    stage(31)
                                    Pc, PTc = N4, NT4
                                    xi = 0
                                    for lev in range(6):
                                        Pn = P4[lev % 2]
                                        PTn = PT4[lev % 2]
                                        pp = nps()
                                        for u in range(4):
                                            S.op("pe", lambda u=u, pp=pp, Pc=Pc, PTc=PTc: nc.tensor.matmul(pp[:, u * 128:(u + 1) * 128], PTc[:, u, :], Pc[:, u, :], start=True, stop=True), r=[Pc, PTc], w=[pp])
                                        S.op("act", lambda pp=pp, Pn=Pn: nc.scalar.copy(Pn[:], v4(pp)), r=[pp], w=[Pn])
                                        if lev < 5:
                                            pp2 = nps()
                                            for u in range(4):
                                                S.op("pe", lambda u=u, pp2=pp2, Pc=Pc, PTc=PTc: nc.tensor.matmul(pp2[:, u * 128:(u + 1) * 128], Pc[:, u, :], PTc[:, u, :], start=True, stop=True), r=[Pc, PTc], w=[pp2])
                                            S.op("act", lambda pp2=pp2, PTn=PTn: nc.scalar.copy(PTn[:], v4(pp2)), r=[pp2], w=[PTn])
                                        px = nps()
                                        xo_ = XT[xi % 2]
                                        xn_ = XT[(xi + 1) % 2]
                                        for u in range(4):
                                            S.op("pe", lambda u=u, px=px, Pn=Pn, xo_=xo_: nc.tensor.matmul(px[:, u * 128:(u + 1) * 128], Pn[:, u, :], xo_[:, u, :], start=True, stop=True), r=[Pn, xo_], w=[px])
                                        S.op("dve", lambda px=px, xo_=xo_, xn_=xn_: nc.vector.tensor_tensor(out=xn_[:], in0=v4(px), in1=xo_[:], op=ALU.add), r=[px, xo_], w=[xn_])
                                        xi += 1
                                        Pc, PTc = Pn, PTn
                                    xt = XT[xi % 2]
                                    stage(32)
                                    pk = nps()
                                    for u in range(4):
                                        S.op("pe", lambda u=u, pk=pk: nc.tensor.matmul(pk[:, u * 128:u * 128 + 64], Aak[:, u, :], vtk[:, c, u * 64:(u + 1) * 64], start=True, stop=True), r=[Aak, vtk], w=[pk])
                                    av = akv[par]
                                    S.op("act", lambda pk=pk, av=av: nc.scalar.copy(av[:, :].rearrange("p (u n) -> p u n", u=4), v4(pk)[:, :, 0:64]), r=[pk], w=[av])
                                    stage(33)
                                    p1 = nps()
                                    for u, (f, h, P) in enumerate(UN):
                                        S.op("pe", lambda u=u, f=f, h=h, P=P, p1=p1: nc.tensor.matmul(p1[:, u * 128:u * 128 + 64], ath[h][:, f, cs], Hb[:, f, h * 64:(h + 1) * 64], start=True, stop=True), r=[ath[h], Hb], w=[p1])
                                    S.op("dve", lambda p1=p1, av=av: nc.vector.tensor_tensor(out=rhsu[:, :].rearrange("p (u n) -> p u n", u=4), in0=v4(p1)[:, :, 0:64],
                                                                                         in1=av[:, :].rearrange("p (u n) -> p u n", u=4), op=ALU.add), r=[p1, av], w=[rhsu])
                                    stage(34)
                                    p2 = nps()
                                    for u in range(4):
                                        S.op("pe", lambda u=u, p2=p2, xt=xt: nc.tensor.matmul(p2[:, u * 128:u * 128 + 64], xt[:, u, :], rhsu[:, u * 64:(u + 1) * 64], start=True, stop=True), r=[xt, rhsu], w=[p2])
                                    S.op("act", lambda p2=p2: nc.scalar.copy(ut[:, :].rearrange("p (u n) -> p u n", u=4), v4(p2)[:, :, 0:64]), r=[p2], w=[ut])
                                    stage(35)
                                    po = nps()
                                    for u, (f, h, P) in enumerate(UN):
                                        S.op("pe", lambda u=u, f=f, h=h, P=P, po=po: nc.tensor.matmul(po[:, u * 128:u * 128 + 64], rth[h][:, f, cs], Hb[:, f, h * 64:(h + 1) * 64], start=True, stop=False), r=[rth[h], Hb], w=[po])
                                        S.op("pe", lambda u=u, po=po, arb=arb: nc.tensor.matmul(po[:, u * 128:u * 128 + 64], arb[:, u, :], ut[:, u * 64:(u + 1) * 64], start=False, stop=False), r=[arb, ut], w=[po])
                                        S.op("pe", lambda u=u, po=po, ark=ark: nc.tensor.matmul(po[:, u * 128:u * 128 + 64], ark[:, u, :], vtk[:, c, u * 64:(u + 1) * 64], start=False, stop=True), r=[ark, vtk], w=[po])
                                    ov = oacc[:, c, :].rearrange("p (u n) -> p u n", u=4)
                                    if d == 0:
                                        S.op("act", lambda po=po, ov=ov: nc.scalar.copy(ov, v4(po)[:, :, 0:64]), r=[po], w=[oacc])
                                    else:
                                        S.op("dve", lambda po=po, ov=ov: nc.vector.tensor_tensor(out=ov, in0=ov, in1=v4(po)[:, :, 0:64], op=ALU.add), r=[po, oacc], w=[oacc])
                                    stage(36)
                                    pH = nps()
                                    for f in range(2):
                                        S.op("pe", lambda f=f, pH=pH: nc.tensor.matmul(pH[:, f * 128:(f + 1) * 128], bhk[:, c, f * 128:(f + 1) * 128], ut[:, f * 128:(f + 1) * 128], start=True, stop=False), r=[bhk, ut], w=[pH])
                                        S.op("pe", lambda f=f, pH=pH: nc.tensor.matmul(pH[:, f * 128:(f + 1) * 128], khk[:, c, f * 128:(f + 1) * 128], vtk[:, c, f * 128:(f + 1) * 128], start=False, stop=True), r=[khk, vtk], w=[pH])
                                    for f in range(2):
                                        S.op("dve", lambda f=f, pH=pH, fp=fp: nc.vector.scalar_tensor_tensor(out=Hst[:, 2 * fp + f, :], in0=Hst[:, 2 * fp + f, :], scalar=wc[:, f, c:c + 1],
                                                                                                         in1=pH[:, f * 128:(f + 1) * 128], op0=ALU.mult, op1=ALU.add), r=[Hst, wc, pH], w=[Hst])
                                    S.op("act", lambda fp=fp: nc.scalar.copy(Hb[:], Hst[:, 2 * fp:2 * fp + 2, :]), r=[Hst], w=[Hb])
                                S.dma("sp", ytok_d[s0:s0 + LM, frows].rearrange("(b p) n -> p b n", p=128), oacc[:], r=[oacc], w=dbt("r_ytok", s0, s0 + LM))
                    S.barrier()
                stage(27)
                with ExitStack() as st:
                    yt = sbuf(st, "pyt", [128, 4, D], F32)
                    yf = sbuf(st, "pyf", [128, 512], F32)
                    sq2 = sbuf(st, "psq", [128, 512], F32)
                    mean = sbuf(st, "pmean", [128, 512], F32)
                    rstd = sbuf(st, "prstd", [128, 512], F32)
                    bon = sbuf(st, "pbon", [128, 512], F32)
                    gt = sbuf(st, "pgt", [128, 512], BF16)
                    yo = [sbuf(st, f"pyo{i}", [128, 512], BF16) for i in range(2)]
                    for tt in range(NT):
                        t0 = tt * 512
                        S.dma("sp", yt[:], ytok_d[t0:t0 + 512, :].rearrange("(b p) n -> p b n", p=128), r=dbt("r_ytok", t0, t0 + 512), w=[yt])
                        for fc in range(16):
                            rows = slice(fc * 128, (fc + 1) * 128)
                            S.dma("sp", bon[:], bonus_d[rows, t0:t0 + 512], r=dbt("r_bonus", t0, t0 + 512), w=[bon])
                            S.dma("sp", gt[:], g_d[rows, t0:t0 + 512], r=dbt("r_g", t0, t0 + 512), w=[gt])
                            pt = nps()
                            for b in range(4):
                                S.op("pe", lambda pt=pt, b=b: nc.tensor.transpose(pt[:, b * 128:(b + 1) * 128], yt[:, b, fc * 128:(fc + 1) * 128], C("ident")), r=[yt, cst], w=[pt])
                            S.op("act", lambda pt=pt: nc.scalar.copy(yf[:], pt[:, :]), r=[pt], w=[yf])
                            S.op("pool", lambda: nc.gpsimd.tensor_tensor(out=sq2[:], in0=yf[:], in1=yf[:], op=ALU.mult), r=[yf], w=[sq2])
                            p1 = nps()
                            p2 = nps()
                            S.op("pe", lambda p1=p1: nc.tensor.matmul(p1[:, :], C("blk64"), yf[:], start=True, stop=True), r=[cst, yf], w=[p1])
                            S.op("pe", lambda p2=p2: nc.tensor.matmul(p2[:, :], C("blk64"), sq2[:], start=True, stop=True), r=[cst, sq2], w=[p2])
                            S.op("act", lambda p1=p1: nc.scalar.mul(mean[:], p1[:, :], 1.0 / 64), r=[p1], w=[mean])
                            S.op("dve", lambda: nc.vector.tensor_tensor(out=rstd[:], in0=mean[:], in1=mean[:], op=ALU.mult), r=[mean], w=[rstd])
                            S.op("dve", lambda p2=p2: nc.vector.scalar_tensor_tensor(out=rstd[:], in0=p2[:, :], scalar=1.0 / 64, in1=rstd[:], op0=ALU.mult, op1=ALU.subtract), r=[p2, rstd], w=[rstd])
                            S.op("act", lambda: nc.scalar.activation(out=rstd[:], in_=rstd[:], func=AF.Ln, bias=64e-5), r=[rstd], w=[rstd])
                            S.op("act", lambda: nc.scalar.activation(out=rstd[:], in_=rstd[:], func=AF.Exp, scale=-0.5), r=[rstd], w=[rstd])
                            S.op("dve", lambda: nc.vector.tensor_tensor(out=yf[:], in0=yf[:], in1=mean[:], op=ALU.subtract), r=[yf, mean], w=[yf])
                            S.op("pool", lambda: nc.gpsimd.tensor_tensor(out=yf[:], in0=yf[:], in1=rstd[:], op=ALU.mult), r=[yf, rstd], w=[yf])
                            S.op("act", lambda: nc.scalar.activation(out=yf[:], in_=yf[:], func=AF.Identity, scale=lwc[:, fc:fc + 1], bias=lbc[:, fc:fc + 1]), r=[yf, lwc, lbc], w=[yf])
                            S.op("dve", lambda: nc.vector.tensor_tensor(out=yf[:], in0=yf[:], in1=bon[:], op=ALU.add), r=[yf, bon], w=[yf])
                            o = yo[fc % 2]
                            S.op("pool", lambda o=o: nc.gpsimd.tensor_tensor(out=o[:], in0=yf[:], in1=gt[:], op=ALU.mult), r=[yf, gt], w=[o])
                            S.dma("sp", ym_d[rows, t0:t0 + 512], o[:], r=[o], w=dbt("r_ym", t0, t0 + 512))
                S.barrier()
                stage(20)
                layer_post(li, ym_d, D, I["rwkv_w_o"], "r_ym", xin_name, xin_ap, last=(li == n_layers - 1))

        try:
            gdn_layer(0, "xT0", xT[0])
            if n_layers > 1:
                rwkv_layer(1, "xT1", xT[1])
        except _Stop:
            pass
        S.finish()
        n_inst = S.n_inst
    return nc, n_inst


_CACHE = {}


def run_stream(NSEG, L, xs, ps, flags, W, n_layers=2):
    key = (NSEG, L, n_layers)
    if key not in _CACHE:
        _CACHE[key] = build(NSEG, L, n_layers)
    nc, n_inst = _CACHE[key]
    in_maps = []
    for x, p, f in zip(xs, ps, flags):
        m = {"x": np.ascontiguousarray(x, dtype=np.float32), "p": np.ascontiguousarray(p, dtype=np.float32),
             "flag": np.full((128, 1), f, np.float32), "consts": CONST_ARR}
        m.update(W)
        in_maps.append(m)
    res = run_bass_kernel_spmd(nc, in_maps, core_ids=list(range(len(in_maps))))
    return [np.asarray(r["y"]) for r in res.results]


def prep_weights(kw):
    W = {}
    for n, sh in W_SPECS:
        a = np.asarray(kw[n], dtype=np.float32)
        if n in ("ln_g", "ln_b"):
            a = a.reshape(4, D)
        elif n in ("gdn_a_log", "gdn_dt_bias"):
            a = a.reshape(1, 64)
        elif n == "gdn_norm":
            a = a.reshape(128, 1)
        elif n == "rwkv_r_k":
            a = a.reshape(1, D)
        else:
            a = a.reshape(sh) if a.ndim != len(sh) or list(a.shape) != sh else a
            a = a.reshape(sh)
        W[n] = np.ascontiguousarray(a)
    return W


def kernel(x_prompt, x_sample, p_prompt, p_sample, **kw):
    W = prep_weights(kw)
    NSEG, L = 4, 2048
    xp = np.asarray(x_prompt, np.float32)
    xs_ = np.asarray(x_sample, np.float32)
    pp = np.asarray(p_prompt, np.float32)
    psm = np.asarray(p_sample, np.float32)
    xs = [xp[0:4].reshape(NSEG * L, D), xp[4:8].reshape(NSEG * L, D), xs_[0]]
    ps = [pp[:, 0:4].reshape(2, NSEG * L, 256), pp[:, 4:8].reshape(2, NSEG * L, 256), psm[:, 0]]
    flags = [0.0, 0.0, 1.0]
    for _ in range(5):
        xs.append(xs_[0]); ps.append(psm[:, 0]); flags.append(1.0)
    ys = run_stream(NSEG, L, xs, ps, flags, W)
    y_prompt = np.concatenate([ys[0].reshape(4, L, D), ys[1].reshape(4, L, D)], 0)
    y_sample = ys[2].reshape(1, NSEG * L, D)
    return (y_prompt.astype(np.float32), y_sample.astype(np.float32))
```
